# Optimizing a Trainium2 kernel written in Bass

```python
import math
import jax, jax.numpy as jnp
from jax import lax
import numpy as np

D_MODEL = 1024
BATCH = 8
SEQ = 8192
DEPTH = 4
DEC_BATCH = 8
DEC_SEQ = 32
PAST_LEN = 1024

CHUNK = 64
N_META = 16
Q_BLOCK = 128
SCAN_BLOCK = 64
EPS = 1e-6
L2_EPS = 1e-6

GROUP_W = D_MODEL // 4
MIX_W = 4 * GROUP_W
A_HEADS = 4
A_NOPE = 64
A_ROPE = 32
A_V = GROUP_W // A_HEADS
A_QRANK = 192
A_KVRANK = 128
A_SCALE = (A_NOPE + A_ROPE) ** -0.5
ROPE_BASE = 10000.0
B_HEADS = 4
B_HD = GROUP_W // B_HEADS
B_W_LORA = 64
B_A_LORA = 64
B_G_LORA = 128
B_GN_EPS = 64e-5
C_HEADS = 4
C_P = GROUP_W // C_HEADS
C_GROUPS = 2
C_N = 64
C_CONV = 4
C_CONV_CH = GROUP_W + 2 * C_GROUPS * C_N
D_HEADS = 4
D_DK = GROUP_W // D_HEADS
D_DV = GROUP_W // D_HEADS
D_CONV = 4
D_CONV_CH = 3 * GROUP_W
D_FF = 2816
FFN_CONV = 3
A_COLS = A_QRANK + A_KVRANK + A_ROPE
B_COLS = 3 * GROUP_W + B_W_LORA + B_A_LORA + B_G_LORA
C_COLS = GROUP_W + C_CONV_CH + C_HEADS
D_COLS = D_CONV_CH + GROUP_W + 2 * D_HEADS
IN_COLS = A_COLS + B_COLS + C_COLS + D_COLS

STATE_KEYS = ('ckv', 'krope', 'rwkv_S', 'rwkv_shift', 'ssd_S', 'ssd_conv', 'gdn_S', 'gdn_conv', 'ffn_conv')

kernel_name = 'hymba_style_streaming_hybrid_step'


def _split(u, sizes):
    return jnp.split(u, [int(i) for i in np.cumsum(sizes)[:-1]], axis=-1)


def _rmsnorm(x, g, eps=EPS):
    xf = x.astype(jnp.float32)
    y = xf * lax.rsqrt(jnp.mean(xf * xf, axis=-1, keepdims=True) + eps)
    return (y * g.astype(jnp.float32)).astype(x.dtype)


def _l2norm(x):
    xf = x.astype(jnp.float32)
    return xf * lax.rsqrt(jnp.sum(xf * xf, axis=-1, keepdims=True) + L2_EPS)


def _rope(x, pos):
    half = A_ROPE // 2
    inv = jnp.power(ROPE_BASE, -jnp.arange(half, dtype=jnp.float32) / half)
    ang = pos.astype(jnp.float32)[:, None] * inv
    shp = (pos.shape[0],) + (1,) * (x.ndim - 3) + (half,)
    cos, sin = jnp.cos(ang).reshape(shp), jnp.sin(ang).reshape(shp)
    xf = x.astype(jnp.float32)
    x1, x2 = xf[..., :half], xf[..., half:]
    return jnp.concatenate([x1 * cos - x2 * sin, x1 * sin + x2 * cos], axis=-1).astype(x.dtype)


def _dwconv(x, hist, w, b=None):
    k, t = w.shape[0], x.shape[1]
    xf = jnp.concatenate([hist.astype(x.dtype), x], axis=1)
    y = sum(xf[:, i:i + t] * w[i] for i in range(k))
    if b is not None:
        y = y + b
    return y, xf[:, t:]


def _to_blocks(u, nc):
    b, t = u.shape[:2]
    u = jnp.pad(u, [(0, 0), (0, nc * SCAN_BLOCK - t)] + [(0, 0)] * (u.ndim - 2))
    return jnp.moveaxis(u.reshape((b, nc, SCAN_BLOCK) + u.shape[2:]), 2, 3)


def _from_blocks(u, t):
    b, nc, h, q, d = u.shape
    return jnp.moveaxis(u, 3, 2).reshape(b, nc * q, h, d)[:, :t]


def _mla_attention(q_nope, q_rope, k_nope, k_rope, v, q_cid, k_cid):
    b, t = q_nope.shape[:2]
    nb = -(-t // Q_BLOCK)
    tp = nb * Q_BLOCK

    def blocks(u):
        u = jnp.pad(u, [(0, 0), (0, tp - t)] + [(0, 0)] * (u.ndim - 2))
        return jnp.moveaxis(u.reshape((b, nb, Q_BLOCK) + u.shape[2:]), 1, 0)

    qc = jnp.pad(q_cid, (0, tp - t), constant_values=2 ** 30).reshape(nb, Q_BLOCK)

    def one(args):
        qn, qr, cq = args
        s = (jnp.einsum('bqhd,bkhd->bhqk', qn, k_nope)
             + jnp.einsum('bqhr,bkr->bhqk', qr, k_rope)).astype(jnp.float32) * A_SCALE
        s = jnp.where(k_cid[None, :] <= cq[:, None], s, -jnp.inf)
        p = jax.nn.softmax(s, axis=-1).astype(v.dtype)
        return jnp.einsum('bhqk,bkhd->bqhd', p, v)

    o = lax.map(one, (blocks(q_nope), blocks(q_rope), qc))
    return jnp.moveaxis(o, 0, 1).reshape(b, tp, A_HEADS * A_V)[:, :t]


def _rwkv7(cols, prev, s0, mu, w0, w2, a0, a2, g2, k_k, k_a, r_k, gn_w, gn_b):
    b, t, _ = cols.shape
    cols = cols.astype(jnp.float32)
    shifted = jnp.concatenate([prev.astype(jnp.float32)[:, None], cols[:, :-1]], axis=1)
    xm = cols + (shifted - cols) * mu
    r, k, v, dw, da, dg = _split(xm, [GROUP_W, GROUP_W, GROUP_W, B_W_LORA, B_A_LORA, B_G_LORA])
    w_log = -jax.nn.softplus(-(w0 + jnp.tanh(dw) @ w2)) - 0.5
    a = jax.nn.sigmoid(a0 + da @ a2)
    g = jax.nn.sigmoid(dg) @ g2
    heads = lambda u: u.reshape(b, t, B_HEADS, B_HD)
    kk = _l2norm(heads(k * k_k))
    k = k * (1.0 + (a - 1.0) * k_a)
    rh, kh, vh, ah = heads(r), heads(k), heads(v), heads(a)
    decay = jnp.exp(-jnp.exp(heads(w_log)))

    def step(S, inp):
        r_t, d_t, k_t, v_t, kk_t, a_t = inp
        sa = jnp.einsum('bhvk,bhk->bhv', S, -kk_t)
        S = (S * d_t[:, :, None, :] + sa[..., None] * (kk_t * a_t)[:, :, None, :]
             + v_t[..., None] * k_t[:, :, None, :])
        return S, jnp.einsum('bhvk,bhk->bhv', S, r_t)

    tm = lambda u: jnp.moveaxis(u, 1, 0)
    s_fin, o = lax.scan(step, s0.astype(jnp.float32), tuple(map(tm, (rh, decay, kh, vh, kk, ah))))
    o = jnp.moveaxis(o, 0, 1)
    mean = jnp.mean(o, axis=-1, keepdims=True)
    var = jnp.mean(jnp.square(o - mean), axis=-1, keepdims=True)
    o = ((o - mean) * lax.rsqrt(var + B_GN_EPS)).reshape(b, t, GROUP_W) * gn_w + gn_b
    bonus = jnp.sum(rh * kh * r_k, axis=-1, keepdims=True) * vh
    return (o + bonus.reshape(b, t, GROUP_W)) * g, s_fin, cols[:, -1]


def _ssd_chunked(x, dt, a_head, bm, cm, s0):
    t = x.shape[1]
    nc = -(-t // SCAN_BLOCK)
    x, dt, bm, cm = (_to_blocks(u, nc) for u in (x, dt, bm, cm))
    a_cum = jnp.cumsum(dt * a_head[:, None], axis=-1)
    incl = jnp.tril(jnp.ones((SCAN_BLOCK, SCAN_BLOCK), bool))
    decay = jnp.exp(jnp.where(incl, a_cum[..., :, None] - a_cum[..., None, :], -jnp.inf))
    xdt = x * dt[..., None]
    y_diag = jnp.einsum('bchij,bchjp->bchip', jnp.einsum('bchin,bchjn->bchij', cm, bm) * decay, xdt)
    chunk_states = jnp.einsum('bchjn,bchjp->bchpn', bm * jnp.exp(a_cum[..., -1:] - a_cum)[..., None], xdt)
    chunk_decay = jnp.exp(a_cum[..., -1])

    def step(s, inp):
        cs, cd = inp
        return s * cd[..., None, None] + cs, s

    mv = lambda u: jnp.moveaxis(u, 1, 0)
    s_fin, s_prev = lax.scan(step, s0, (mv(chunk_states), mv(chunk_decay)))
    s_prev = jnp.moveaxis(s_prev, 0, 1)
    y_off = jnp.einsum('bchin,bchpn->bchip', cm, s_prev) * jnp.exp(a_cum)[..., None]
    return _from_blocks(y_diag + y_off, t), s_fin


def _mamba2(cols, hist, s0, conv_w, conv_b, dt_bias, a_log, d_skip, g_norm):
    b, t, _ = cols.shape
    z, xbc, dt = _split(cols, [GROUP_W, C_CONV_CH, C_HEADS])
    xbc, new_hist = _dwconv(xbc, hist, conv_w, conv_b)
    xbc = jax.nn.silu(xbc.astype(jnp.float32))
    xs, bm, cm = _split(xbc, [GROUP_W, C_GROUPS * C_N, C_GROUPS * C_N])
    grp = lambda u: jnp.repeat(u.reshape(b, t, C_GROUPS, C_N), C_HEADS // C_GROUPS, axis=2)
    xh = xs.reshape(b, t, C_HEADS, C_P)
    dt = jax.nn.softplus(dt.astype(jnp.float32) + dt_bias)
    y, s_fin = _ssd_chunked(xh, dt, -jnp.exp(a_log.astype(jnp.float32)), grp(bm), grp(cm), s0.astype(jnp.float32))
    y = (y + d_skip[:, None] * xh).reshape(b, t, GROUP_W)
    y = _rmsnorm(y * jax.nn.silu(z.astype(jnp.float32)), g_norm)
    return y, s_fin, new_hist


def _gdn_chunked(q, k, v, g, beta, s0):
    t = q.shape[1]
    nc = -(-t // SCAN_BLOCK)
    q, k, v, g, beta = (_to_blocks(u, nc) for u in (q, k, v, g, beta))
    g_cum = jnp.cumsum(g, axis=-1)
    kb, vb = k * beta[..., None], v * beta[..., None]
    incl = jnp.tril(jnp.ones((SCAN_BLOCK, SCAN_BLOCK), bool))
    strict = jnp.tril(jnp.ones((SCAN_BLOCK, SCAN_BLOCK), bool), -1)
    decay = jnp.exp(jnp.where(incl, g_cum[..., :, None] - g_cum[..., None, :], -jnp.inf))
    lower = jnp.where(strict, jnp.einsum('bchid,bchjd->bchij', kb, k) * decay, 0.0)
    eye = jnp.eye(SCAN_BLOCK, dtype=lower.dtype)
    tinv = lax.linalg.triangular_solve(eye + lower, jnp.broadcast_to(eye, lower.shape),
                                       left_side=True, lower=True, unit_diagonal=True)
    w = tinv @ (kb * jnp.exp(g_cum)[..., None])
    u = tinv @ vb
    a_qk = jnp.where(incl, jnp.einsum('bchid,bchjd->bchij', q, k) * decay, 0.0)
    q_dec = q * jnp.exp(g_cum)[..., None]
    k_dec = k * jnp.exp(g_cum[..., -1:] - g_cum)[..., None]
    g_last = jnp.exp(g_cum[..., -1])

    def step(S, inp):
        w_c, u_c, qd_c, a_c, kd_c, gl_c = inp
        v_new = u_c - jnp.einsum('bhik,bhkv->bhiv', w_c, S)
        o = jnp.einsum('bhik,bhkv->bhiv', qd_c, S) + jnp.einsum('bhij,bhjv->bhiv', a_c, v_new)
        S = S * gl_c[..., None, None] + jnp.einsum('bhik,bhiv->bhkv', kd_c, v_new)
        return S, o

    mv = lambda z: jnp.moveaxis(z, 1, 0)
    s_fin, o = lax.scan(step, s0, tuple(map(mv, (w, u, q_dec, a_qk, k_dec, g_last))))
    return _from_blocks(jnp.moveaxis(o, 0, 1), t), s_fin


def _gdn(cols, hist, s0, conv_w, a_log, dt_bias, g_norm):
    b, t, _ = cols.shape
    qkv, z, beta, alpha = _split(cols, [D_CONV_CH, GROUP_W, D_HEADS, D_HEADS])
    qkv, new_hist = _dwconv(qkv, hist, conv_w)
    q, k, v = _split(jax.nn.silu(qkv.astype(jnp.float32)), [GROUP_W] * 3)
    q = _l2norm(q.reshape(b, t, D_HEADS, D_DK)) * (D_DK ** -0.5)
    k = _l2norm(k.reshape(b, t, D_HEADS, D_DK))
    v = v.reshape(b, t, D_HEADS, D_DV)
    g = -jnp.exp(a_log.astype(jnp.float32)) * jax.nn.softplus(alpha.astype(jnp.float32) + dt_bias)
    o, s_fin = _gdn_chunked(q, k, v, g, jax.nn.sigmoid(beta.astype(jnp.float32)), s0.astype(jnp.float32))
    o = _rmsnorm(o, g_norm) * jax.nn.silu(z.astype(jnp.float32)).reshape(b, t, D_HEADS, D_DV)
    return o.reshape(b, t, GROUP_W), s_fin, new_hist


def _conv_ffn(h, hist, w_up, conv_w, w_down):
    u, new_hist = _dwconv(h @ w_up, hist, conv_w)
    gate, val = jnp.split(u, 2, axis=-1)
    return (jax.nn.silu(gate) * val) @ w_down, new_hist


def _zero_state(b, dtype):
    z = lambda *s: jnp.zeros((DEPTH, b) + s, dtype)
    return dict(ckv=z(0, A_KVRANK), krope=z(0, A_ROPE), rwkv_S=z(B_HEADS, B_HD, B_HD),
                rwkv_shift=z(B_COLS), ssd_S=z(C_HEADS, C_P, C_N), ssd_conv=z(C_CONV - 1, C_CONV_CH),
                gdn_S=z(D_HEADS, D_DK, D_DV), gdn_conv=z(D_CONV - 1, D_CONV_CH),
                ffn_conv=z(FFN_CONV - 1, 2 * D_FF))


def _trunk(x, pos, q_cid, st, P):
    b, t, _ = x.shape
    dtype = x.dtype
    k_cid = jnp.concatenate([jnp.full((st['ckv'].shape[2],), -1, jnp.int32), q_cid])
    new = {name: [] for name in STATE_KEYS}
    for l in range(DEPTH):
        h = _rmsnorm(x, P['norm1_g'][l])
        pa, pb, pc, pd = _split(h @ P['w_in'][l], [A_COLS, B_COLS, C_COLS, D_COLS])
        q_lat, c_raw, kr_raw = _split(pa, [A_QRANK, A_KVRANK, A_ROPE])
        q = (_rmsnorm(q_lat, P['a_gq'][l]) @ P['a_wuq'][l]).reshape(b, t, A_HEADS, A_NOPE + A_ROPE)
        c = _rmsnorm(c_raw, P['a_gkv'][l])
        kr = _rope(kr_raw, pos)
        c_all = jnp.concatenate([st['ckv'][l].astype(c.dtype), c], axis=1)
        kr_all = jnp.concatenate([st['krope'][l].astype(kr.dtype), kr], axis=1)
        n_k = c_all.shape[1]
        k_nope = (c_all @ P['a_wuk'][l]).reshape(b, n_k, A_HEADS, A_NOPE)
        v = (c_all @ P['a_wuv'][l]).reshape(b, n_k, A_HEADS, A_V)
        ya = _mla_attention(q[..., :A_NOPE], _rope(q[..., A_NOPE:], pos), k_nope, kr_all, v, q_cid, k_cid)
        ya = _rmsnorm(ya, P['a_gout'][l])
        yb, s_b, sh_b = _rwkv7(pb, st['rwkv_shift'][l], st['rwkv_S'][l], P['b_mu'][l], P['b_w0'][l],
                               P['b_w2'][l], P['b_a0'][l], P['b_a2'][l], P['b_g2'][l], P['b_kk'][l],
                               P['b_ka'][l], P['b_rk'][l], P['b_gnw'][l], P['b_gnb'][l])
        yc, s_c, cv_c = _mamba2(pc, st['ssd_conv'][l], st['ssd_S'][l], P['c_convw'][l], P['c_convb'][l],
                                P['c_dtb'][l], P['c_alog'][l], P['c_d'][l], P['c_gnorm'][l])
        yd, s_d, cv_d = _gdn(pd, st['gdn_conv'][l], st['gdn_S'][l], P['d_convw'][l], P['d_alog'][l],
                             P['d_dtb'][l], P['d_gnorm'][l])
        mix = jnp.concatenate([ya, yb.astype(dtype), yc.astype(dtype), yd.astype(dtype)], axis=-1)
        x = x + mix @ P['w_out'][l]
        f, cv_f = _conv_ffn(_rmsnorm(x, P['norm2_g'][l]), st['ffn_conv'][l], P['f_wup'][l],
                            P['f_convw'][l], P['f_wdown'][l])
        x = x + f
        for name, val in zip(STATE_KEYS, (c, kr, s_b, sh_b, s_c, cv_c, s_d, cv_d, cv_f)):
            new[name].append(val)
    return _rmsnorm(x, P['final_g']), {name: jnp.stack(vals) for name, vals in new.items()}


def setup_inputs(seed: int = 0) -> dict:
    key = jax.random.key(seed)
    ks = iter(jax.random.split(key, 64))
    nrm = lambda shape, scale: scale * jax.random.normal(next(ks), shape, jnp.float32)
    gain = lambda shape: 1.0 + 0.02 * jax.random.normal(next(ks), shape, jnp.float32)
    uni = lambda shape, lo, hi: jax.random.uniform(next(ks), shape, jnp.float32, lo, hi)

    def dt_bias(n):
        dt = jnp.exp(uni((DEPTH, n), math.log(1e-3), math.log(1e-1)))
        return dt + jnp.log(-jnp.expm1(-dt))

    return {
        'x_prompt': nrm((BATCH, SEQ, D_MODEL), 1.0),
        'x_sample': nrm((DEC_BATCH, DEC_SEQ, D_MODEL), 1.0),
        'cache_mla_ckv': nrm((DEPTH, DEC_BATCH, PAST_LEN, A_KVRANK), 1.0),
        'cache_mla_krope': nrm((DEPTH, DEC_BATCH, PAST_LEN, A_ROPE), 1.0),
        'state_rwkv': nrm((DEPTH, DEC_BATCH, B_HEADS, B_HD, B_HD), 0.3),
        'state_rwkv_shift': nrm((DEPTH, DEC_BATCH, B_COLS), 1.0),
        'state_ssd': nrm((DEPTH, DEC_BATCH, C_HEADS, C_P, C_N), 0.1),
        'state_ssd_conv': nrm((DEPTH, DEC_BATCH, C_CONV - 1, C_CONV_CH), 1.0),
        'state_gdn': nrm((DEPTH, DEC_BATCH, D_HEADS, D_DK, D_DV), 0.1),
        'state_gdn_conv': nrm((DEPTH, DEC_BATCH, D_CONV - 1, D_CONV_CH), 1.0),
        'state_ffn_conv': nrm((DEPTH, DEC_BATCH, FFN_CONV - 1, 2 * D_FF), 0.5),
        'meta_tokens': nrm((N_META, D_MODEL), 1.0),
        'norm1_g': gain((DEPTH, D_MODEL)),
        'w_in': nrm((DEPTH, D_MODEL, IN_COLS), D_MODEL ** -0.5),
        'a_gq': gain((DEPTH, A_QRANK)),
        'a_wuq': nrm((DEPTH, A_QRANK, A_HEADS * (A_NOPE + A_ROPE)), A_QRANK ** -0.5),
        'a_gkv': gain((DEPTH, A_KVRANK)),
        'a_wuk': nrm((DEPTH, A_KVRANK, A_HEADS * A_NOPE), A_KVRANK ** -0.5),
        'a_wuv': nrm((DEPTH, A_KVRANK, A_HEADS * A_V), A_KVRANK ** -0.5),
        'a_gout': gain((DEPTH, GROUP_W)),
        'b_mu': uni((DEPTH, B_COLS), 0.0, 1.0),
        'b_w0': uni((DEPTH, GROUP_W), -3.0, 1.0),
        'b_w2': nrm((DEPTH, B_W_LORA, GROUP_W), 0.1 * B_W_LORA ** -0.5),
        'b_a0': nrm((DEPTH, GROUP_W), 0.1),
        'b_a2': nrm((DEPTH, B_A_LORA, GROUP_W), 0.3 * B_A_LORA ** -0.5),
        'b_g2': nrm((DEPTH, B_G_LORA, GROUP_W), B_G_LORA ** -0.5),
        'b_kk': 1.0 + nrm((DEPTH, GROUP_W), 0.05),
        'b_ka': 1.0 + nrm((DEPTH, GROUP_W), 0.05),
        'b_rk': nrm((DEPTH, B_HEADS, B_HD), 0.1),
        'b_gnw': gain((DEPTH, GROUP_W)),
        'b_gnb': nrm((DEPTH, GROUP_W), 0.02),
        'c_convw': nrm((DEPTH, C_CONV, C_CONV_CH), C_CONV ** -0.5),
        'c_convb': nrm((DEPTH, C_CONV_CH), 0.02),
        'c_dtb': dt_bias(C_HEADS),
        'c_alog': jnp.log(uni((DEPTH, C_HEADS), 1.0, 16.0)),
        'c_d': 1.0 + nrm((DEPTH, C_HEADS), 0.1),
        'c_gnorm': gain((DEPTH, GROUP_W)),
        'd_convw': nrm((DEPTH, D_CONV, D_CONV_CH), D_CONV ** -0.5),
        'd_alog': jnp.log(uni((DEPTH, D_HEADS), 1.0, 16.0)),
        'd_dtb': dt_bias(D_HEADS),
        'd_gnorm': gain((DEPTH, D_DV)),
        'w_out': nrm((DEPTH, MIX_W, D_MODEL), MIX_W ** -0.5),
        'norm2_g': gain((DEPTH, D_MODEL)),
        'f_wup': nrm((DEPTH, D_MODEL, 2 * D_FF), D_MODEL ** -0.5),
        'f_convw': nrm((DEPTH, FFN_CONV, 2 * D_FF), FFN_CONV ** -0.5),
        'f_wdown': nrm((DEPTH, D_FF, D_MODEL), D_FF ** -0.5),
        'final_g': gain((D_MODEL,)),
    }


def reference(x_prompt, x_sample, cache_mla_ckv, cache_mla_krope, state_rwkv, state_rwkv_shift,
              state_ssd, state_ssd_conv, state_gdn, state_gdn_conv, state_ffn_conv,
              meta_tokens, norm1_g, w_in, a_gq, a_wuq, a_gkv, a_wuk, a_wuv, a_gout,
              b_mu, b_w0, b_w2, b_a0, b_a2, b_g2, b_kk, b_ka, b_rk, b_gnw, b_gnb,
              c_convw, c_convb, c_dtb, c_alog, c_d, c_gnorm,
              d_convw, d_alog, d_dtb, d_gnorm,
              w_out, norm2_g, f_wup, f_convw, f_wdown, final_g):
    P = dict(norm1_g=norm1_g, w_in=w_in, a_gq=a_gq, a_wuq=a_wuq, a_gkv=a_gkv, a_wuk=a_wuk,
             a_wuv=a_wuv, a_gout=a_gout, b_mu=b_mu, b_w0=b_w0, b_w2=b_w2, b_a0=b_a0, b_a2=b_a2,
             b_g2=b_g2, b_kk=b_kk, b_ka=b_ka, b_rk=b_rk, b_gnw=b_gnw, b_gnb=b_gnb,
             c_convw=c_convw, c_convb=c_convb, c_dtb=c_dtb, c_alog=c_alog, c_d=c_d, c_gnorm=c_gnorm,
             d_convw=d_convw, d_alog=d_alog, d_dtb=d_dtb, d_gnorm=d_gnorm,
             w_out=w_out, norm2_g=norm2_g, f_wup=f_wup, f_convw=f_convw, f_wdown=f_wdown,
             final_g=final_g)
    b_p = x_prompt.shape[0]
    meta = jnp.broadcast_to(meta_tokens[None].astype(x_prompt.dtype), (b_p, N_META, D_MODEL))
    x_ext = jnp.concatenate([meta, x_prompt], axis=1)
    idx = jnp.arange(x_ext.shape[1], dtype=jnp.int32)
    pos_p = idx - N_META
    cid_p = jnp.where(idx < N_META, 0, 1 + (idx - N_META) // CHUNK)
    y_p, ns_p = _trunk(x_ext, pos_p, cid_p, _zero_state(b_p, x_prompt.dtype), P)
    t_s = x_sample.shape[1]
    past = cache_mla_ckv.shape[2]
    ar = jnp.arange(t_s, dtype=jnp.int32)
    st_s = dict(ckv=cache_mla_ckv, krope=cache_mla_krope, rwkv_S=state_rwkv, rwkv_shift=state_rwkv_shift,
                ssd_S=state_ssd, ssd_conv=state_ssd_conv, gdn_S=state_gdn, gdn_conv=state_gdn_conv,
                ffn_conv=state_ffn_conv)
    y_s, ns_s = _trunk(x_sample, past + ar, ar // CHUNK, st_s, P)
    return (y_p[:, N_META:], y_s,
            ns_p['ckv'], ns_p['krope'], ns_p['rwkv_S'], ns_p['rwkv_shift'], ns_p['ssd_S'], ns_p['ssd_conv'],
            ns_p['gdn_S'], ns_p['gdn_conv'], ns_p['ffn_conv'],
            ns_s['ckv'], ns_s['krope'], ns_s['rwkv_S'], ns_s['rwkv_shift'], ns_s['ssd_S'], ns_s['ssd_conv'],
            ns_s['gdn_S'], ns_s['gdn_conv'], ns_s['ffn_conv'])
```

```python
import math
from contextlib import ExitStack
import numpy as np
import concourse.bass as bass
import concourse.mybir as mybir
from concourse.bass_utils import run_bass_kernel_spmd

F32 = mybir.dt.float32
BF16 = mybir.dt.bfloat16
AF = mybir.ActivationFunctionType
ALU = mybir.AluOpType
AX = mybir.AxisListType

D_MODEL = 1024
DEPTH = 4
N_META = 16
CHUNK = 64
PAST = 1024
DEC_SEQ = 32
EPS = 1e-6
GW = 256
A_QRANK, A_KVRANK, A_ROPE, A_NOPE = 192, 128, 32, 64
A_SCALE = (A_NOPE + A_ROPE) ** -0.5
A_COLS = A_QRANK + A_KVRANK + A_ROPE
B_COLS = 1024
C_COLS = 256 + 512 + 4
D_COLS = 768 + 256 + 8
D_FF = 2816
B_GN_EPS = 64e-5
NEG = -1.0e30

ENGS = ("pe", "act", "dve", "pool", "sp")


class Buf:
    __slots__ = ("name", "t", "last_w", "readers", "sem", "cnt", "space")

    def __init__(self, name, t, space):
        self.name = name
        self.t = t
        self.space = space
        self.last_w = None
        self.readers = []
        self.sem = {}
        self.cnt = {}

    def __getitem__(self, k):
        return self.t[k]

    def ap(self):
        return self.t if self.space == "dr" else self.t.ap()


class Prog:
    def __init__(self, nc):
        self.nc = nc
        self.es = ExitStack()
        self.q = {e: [] for e in ENGS}
        self.seen = {e: {} for e in ENGS}
        self.esem = {}
        self.n_sem = 0

    def sbuf(self, name, shape, dtype):
        t = self.es.enter_context(self.nc.sbuf_tensor("sb_" + name, list(shape), dtype))
        return Buf(name, t, "sb")

    def psum(self, name, shape, dtype):
        t = self.es.enter_context(self.nc.psum_tensor("pq_" + name, list(shape), dtype))
        return Buf(name, t, "ps")

    def dram(self, name, shape, dtype, kind="Internal"):
        t = self.nc.dram_tensor(name, list(shape), dtype, kind=kind).ap()
        return Buf(name, t, "dr")

    def _sem(self, name):
        self.n_sem += 1
        return self.es.enter_context(self.nc.semaphore(name))

    def _need(self, eng, h, waits):
        if h is None:
            return
        if h[0] == "e":
            _, e2, idx = h
            if e2 == eng and eng in ("pe", "sp"):
                return
            key, val = ("e", e2), idx
        else:
            _, b, c, kd = h
            key, val = ("d", id(b), kd), c
        if self.seen[eng].get(key, -1) >= val:
            return
        self.seen[eng][key] = val
        if h[0] == "e":
            self.q[h[1]][h[2]]["mark"] = True
        waits.append(h)

    def _deps(self, eng, reads, writes):
        waits = []
        for b in reads:
            self._need(eng, b.last_w, waits)
        for b in writes:
            self._need(eng, b.last_w, waits)
            for r in b.readers:
                self._need(eng, r, waits)
        return waits

    def op(self, eng, fn, reads=(), writes=()):
        pr = [b for b in reads if b.space == "ps" and b not in writes]
        if pr:
            reads = [b for b in reads if b.space != "ps"]
            writes = list(writes) + pr
        waits = self._deps(eng, reads, writes)
        idx = len(self.q[eng])
        self.q[eng].append(dict(fn=fn, waits=waits, mark=False, dma=None))
        h = ("e", eng, idx)
        for b in reads:
            b.readers.append(h)
        for b in writes:
            b.last_w = h
            b.readers = []
        return h

    def dma(self, eng, out_ap, in_ap, reads=(), writes=(), owner=None, **kw):
        waits = self._deps(eng, reads, writes)
        if owner is None:
            owner = writes[0] if writes else reads[0]
        kd = "sw" if eng == "pool" else "hw"
        if kd not in owner.sem:
            owner.sem[kd] = self._sem("d_" + kd + "_" + owner.name)
            owner.cnt[kd] = 0
        owner.cnt[kd] += 16
        h = ("d", owner, owner.cnt[kd], kd)
        self.q[eng].append(dict(
            fn=lambda e: e.dma_start(out=out_ap, in_=in_ap, **kw),
            waits=waits, mark=False, dma=owner.sem[kd]))
        for b in reads:
            b.readers.append(h)
        for b in writes:
            b.last_w = h
            b.readers = []
        return h

    def wait_all(self, eng, handles):
        waits = []
        for h in handles:
            self._need(eng, h, waits)
        self.q[eng].append(dict(fn=None, waits=waits, mark=False, dma=None))

    def build(self):
        nc = self.nc
        for e in ENGS:
            self.esem[e] = self._sem("e_" + e)
        rank = {}
        for e in ENGS:
            c = 0
            for i, ins in enumerate(self.q[e]):
                if ins["mark"]:
                    c += 1
                    rank[(e, i)] = c
        engobj = {"pe": "tensor", "act": "scalar", "dve": "vector", "pool": "gpsimd", "sp": "sync"}

        def run(ename):
            def body(eng):
                for i, ins in enumerate(self.q[ename]):
                    for h in ins["waits"]:
                        if h[0] == "e":
                            eng.wait_ge(self.esem[h[1]], rank[(h[1], h[2])])
                        else:
                            eng.wait_ge(h[1].sem[h[3]], h[2])
                    if ins["fn"] is None:
                        continue
                    r = ins["fn"](eng)
                    if ins["dma"] is not None:
                        r.then_inc(ins["dma"], 16)
                    elif ins["mark"]:
                        r.then_inc(self.esem[ename], 1)
            return body

        with nc.Block() as block:
            for e in ENGS:
                if self.q[e]:
                    getattr(block, engobj[e])(run(e))
        self.es.close()


def _win_chunks():
    ch = []
    Z = [-1]

    def cols(base, n):
        return list(range(base, base + n))

    ch.append(cols(0, 128))
    ch.append(cols(128, 64))
    ch.append(cols(192, 128))
    kr = 192 + 128
    ch.append(Z * 32 + cols(kr, 32))
    ch.append(Z * 32 + cols(kr + 16, 16) + cols(kr, 16))
    b0 = A_COLS
    for c in range(8):
        ch.append(cols(b0 + 128 * c, 128))
    c0 = A_COLS + B_COLS
    ch.append(cols(c0, 128)); ch.append(cols(c0 + 128, 128))
    xb = c0 + 256
    ch.append(cols(xb, 128)); ch.append(cols(xb + 128, 128))
    ch.append(cols(xb + 256, 64) * 2)
    ch.append(cols(xb + 320, 64) * 2)
    ch.append(cols(xb + 384, 64) * 2)
    ch.append(cols(xb + 448, 64) * 2)
    ch.append(cols(c0 + 768, 4) * 2)
    d0 = c0 + C_COLS
    for c in range(6):
        ch.append(cols(d0 + 128 * c, 128))
    ch.append(cols(d0 + 768, 128)); ch.append(cols(d0 + 896, 128))
    ch.append(cols(d0 + 1024, 4) * 2)
    ch.append(cols(d0 + 1028, 4) * 2)
    return ch


WIN_CH = _win_chunks()
NWIN = len(WIN_CH)
G_WIN = NWIN // 4
G_WOUT = 2
G_WUP = 11
G_WDN = 8
NGRP = G_WIN + G_WOUT + G_WUP + G_WDN
GSZ = 4096

VOFF = {}
_o = 0
for _n, _w in [("n1g", 8), ("n2g", 8), ("gq", 2), ("gkv", 1), ("gout", 2), ("mu", 8), ("w0", 2), ("a0", 2),
               ("kk", 2), ("ka", 2), ("rk", 2), ("gnw", 2), ("gnb", 2), ("ccw", 24), ("ccb", 6), ("cdtb", 1),
               ("calog", 1), ("cd", 2), ("cgn", 2), ("dcw", 24), ("dalog", 1), ("ddtb", 1), ("dgn", 2),
               ("fcw", 132)]:
    VOFF[_n] = _o
    _o += _w
NV = _o

SOFF = {}
_o = 0
for _n, _w in [("wuq", 2 * 4 * 192), ("wuk", 4 * 128), ("wuv", 256), ("w2a2", 256), ("g2", 256)]:
    SOFF[_n] = _o
    _o += _w
NSM = _o

COFF = {}
_o = 0
for _n, _w in [("ident", 128), ("ones", 128), ("blk1", 128), ("col0", 32), ("colk", 32),
               ("mn_inclT", 64), ("mn_str", 64), ("mn_strT", 64),
               ("m1_strT", 64), ("m1_inclT", 64), ("m1_str", 64),
               ("sel", 256), ("selh", 4), ("m1", 1), ("m2", 1), ("nm2", 1)]:
    COFF[_n] = _o
    _o += _w
NCST = _o


def _consts():
    c = np.zeros((128, NCST), np.float32)
    c[:, COFF["ident"]:COFF["ident"] + 128] = np.eye(128, dtype=np.float32)
    c[:, COFF["ones"]:COFF["ones"] + 128] = 1.0
    b = np.zeros((128, 128), np.float32)
    b[:64, :64] = 1.0
    b[64:, 64:] = 1.0
    c[:, COFF["blk1"]:COFF["blk1"] + 128] = b
    c[:, COFF["col0"]] = 1.0
    c[32:, COFF["colk"]] = 1.0
    i = np.arange(64)[:, None]
    j = np.arange(64)[None, :]
    c[:64, COFF["mn_inclT"]:COFF["mn_inclT"] + 64] = np.where(i <= j, 0.0, NEG)
    c[:64, COFF["mn_str"]:COFF["mn_str"] + 64] = np.where(j < i, 0.0, NEG)
    c[:64, COFF["mn_strT"]:COFF["mn_strT"] + 64] = np.where(i < j, 0.0, NEG)
    c[:64, COFF["m1_strT"]:COFF["m1_strT"] + 64] = (i < j)
    c[:64, COFF["m1_inclT"]:COFF["m1_inclT"] + 64] = (i <= j)
    c[:64, COFF["m1_str"]:COFF["m1_str"] + 64] = (j < i)
    sel = np.zeros((8, 2, 128), np.float32)
    for cc in range(2):
        for m in range(128):
            sel[2 * cc + m // 64, cc, m] = 1.0
    c[:8, COFF["sel"]:COFF["sel"] + 256] = sel.reshape(8, 256)
    for h in range(4):
        c[h, COFF["selh"] + h] = 1.0
        c[4 + h, COFF["selh"] + h] = 1.0
    c[0:4, COFF["m1"]] = 1.0
    c[4:8, COFF["m2"]] = 1.0
    c[4:8, COFF["nm2"]] = -1.0
    return c


def _rope_tables(pos):
    half = A_ROPE // 2
    inv = np.power(np.float32(10000.0), -np.arange(half, dtype=np.float32) / np.float32(half)).astype(np.float32)
    ang = pos.astype(np.float32)[None, :] * inv[:, None]
    cos, sin = np.cos(ang).astype(np.float32), np.sin(ang).astype(np.float32)
    C = np.concatenate([cos, cos], 0)
    S = np.concatenate([-sin, sin], 0)
    return np.ascontiguousarray(np.stack([C, S], 1))


def _fm(v, nchunk):
    return np.ascontiguousarray(np.asarray(v, np.float32).reshape(nchunk, 128).T)


C_XBC_SRC = [list(range(0, 128)), list(range(128, 256)),
             list(range(256, 320)) * 2, list(range(320, 384)) * 2,
             list(range(384, 448)) * 2, list(range(448, 512)) * 2]
FF_ORDER = []
for _i in range(22):
    FF_ORDER.append(list(range(128 * _i, 128 * _i + 128)))
    FF_ORDER.append(list(range(D_FF + 128 * _i, D_FF + 128 * _i + 128)))


def prep_weights(W):
    wbig = np.zeros((DEPTH, NGRP, 128, GSZ), np.float32)
    wsm = np.zeros((DEPTH, 128, NSM), np.float32)
    vec = np.zeros((DEPTH, 128, NV), np.float32)
    for l in range(DEPTH):
        win = np.asarray(W["w_in"][l], np.float32)
        winx = np.concatenate([win, np.zeros((1024, 1), np.float32)], 1)
        for ci, src in enumerate(WIN_CH):
            m = np.zeros((1024, 128), np.float32)
            m[:, :len(src)] = winx[:, src]
            g, k = divmod(ci, 4)
            wbig[l, g].reshape(128, 4, 8, 128)[:, k] = m.reshape(8, 128, 128).transpose(1, 0, 2)
        wo = np.asarray(W["w_out"][l], np.float32)
        for j in range(8):
            g, k = divmod(j, 4)
            wbig[l, G_WIN + g].reshape(128, 4, 8, 128)[:, k] = \
                wo[:, 128 * j:128 * j + 128].reshape(8, 128, 128).transpose(1, 0, 2)
        wu = np.asarray(W["f_wup"][l], np.float32)
        for ci, src in enumerate(FF_ORDER):
            g, k = divmod(ci, 4)
            wbig[l, G_WIN + G_WOUT + g].reshape(128, 4, 8, 128)[:, k] = \
                wu[:, src].reshape(8, 128, 128).transpose(1, 0, 2)
        wd = np.asarray(W["f_wdown"][l], np.float32)
        for j in range(8):
            wbig[l, G_WIN + G_WOUT + G_WUP + j][:, :22 * 128].reshape(128, 22, 128)[:] = \
                wd[:, 128 * j:128 * j + 128].reshape(22, 128, 128).transpose(1, 0, 2)
        wuq = np.asarray(W["a_wuq"][l], np.float32)
        wq = np.zeros((128, 2, 4, 192), np.float32)
        for h in range(4):
            main = np.zeros((256, 128), np.float32)
            main[:192, 64:128] = wuq[:, h * 96:h * 96 + 64]
            main[:192, 32:64] = wuq[:, h * 96 + 64:h * 96 + 96]
            sw = np.zeros((256, 64), np.float32)
            sw[:192, 32:48] = wuq[:, h * 96 + 80:h * 96 + 96]
            sw[:192, 48:64] = wuq[:, h * 96 + 64:h * 96 + 80]
            for kc in range(2):
                wq[:, kc, h, 0:128] = main[kc * 128:(kc + 1) * 128]
                wq[:, kc, h, 128:192] = sw[kc * 128:(kc + 1) * 128]
        wsm[l, :, SOFF["wuq"]:SOFF["wuq"] + 1536] = wq.reshape(128, 1536)
        wuk = np.asarray(W["a_wuk"][l], np.float32)
        wk = np.zeros((128, 4, 128), np.float32)
        for h in range(4):
            wk[:, h, 64:128] = wuk[:, h * 64:(h + 1) * 64]
        wsm[l, :, SOFF["wuk"]:SOFF["wuk"] + 512] = wk.reshape(128, 512)
        wsm[l, :, SOFF["wuv"]:SOFF["wuv"] + 256] = np.asarray(W["a_wuv"][l], np.float32)
        wsm[l, 0:64, SOFF["w2a2"]:SOFF["w2a2"] + 256] = np.asarray(W["b_w2"][l], np.float32)
        wsm[l, 64:128, SOFF["w2a2"]:SOFF["w2a2"] + 256] = np.asarray(W["b_a2"][l], np.float32)
        wsm[l, :, SOFF["g2"]:SOFF["g2"] + 256] = np.asarray(W["b_g2"][l], np.float32)
        v = vec[l]

        def put(name, arr):
            arr = np.asarray(arr, np.float32)
            v[:arr.shape[0], VOFF[name]:VOFF[name] + arr.shape[1]] = arr

        put("n1g", _fm(W["norm1_g"][l], 8)); put("n2g", _fm(W["norm2_g"][l], 8))
        gq = np.zeros(256, np.float32); gq[:192] = W["a_gq"][l]
        put("gq", _fm(gq, 2)); put("gkv", _fm(W["a_gkv"][l], 1)); put("gout", _fm(W["a_gout"][l], 2))
        put("mu", _fm(W["b_mu"][l], 8)); put("w0", _fm(W["b_w0"][l], 2)); put("a0", _fm(W["b_a0"][l], 2))
        put("kk", _fm(W["b_kk"][l], 2)); put("ka", _fm(W["b_ka"][l], 2))
        put("rk", _fm(np.asarray(W["b_rk"][l]).reshape(256), 2))
        put("gnw", _fm(W["b_gnw"][l], 2)); put("gnb", _fm(W["b_gnb"][l], 2))
        ccw = np.asarray(W["c_convw"][l], np.float32)
        ccb = np.asarray(W["c_convb"][l], np.float32)
        a = np.zeros((128, 6, 4), np.float32); bb = np.zeros((128, 6), np.float32)
        for c, src in enumerate(C_XBC_SRC):
            a[:, c, :] = ccw[:, src].T
            bb[:, c] = ccb[src]
        put("ccw", a.reshape(128, 24)); put("ccb", bb)
        put("cdtb", np.tile(np.asarray(W["c_dtb"][l], np.float32), 2)[:, None])
        put("calog", np.tile(np.asarray(W["c_alog"][l], np.float32), 2)[:, None])
        put("cd", _fm(np.repeat(np.asarray(W["c_d"][l], np.float32), 64), 2))
        put("cgn", _fm(W["c_gnorm"][l], 2))
        dcw = np.asarray(W["d_convw"][l], np.float32)
        put("dcw", dcw.T.reshape(6, 128, 4).transpose(1, 0, 2).reshape(128, 24))
        put("dalog", np.tile(np.asarray(W["d_alog"][l], np.float32), 2)[:, None])
        put("ddtb", np.tile(np.asarray(W["d_dtb"][l], np.float32), 2)[:, None])
        put("dgn", _fm(np.tile(np.asarray(W["d_gnorm"][l], np.float32), 4), 2))
        fcw = np.asarray(W["f_convw"][l], np.float32)
        a = np.zeros((128, 44, 3), np.float32)
        for c, src in enumerate(FF_ORDER):
            a[:, c, :] = fcw[:, src].T
        put("fcw", a.reshape(128, 132))
    fin = _fm(W["final_g"], 8)
    return wbig, wsm, vec, fin


class Cfg:
    def __init__(self, seq=8192, tt=256, en=("A", "B", "C", "D"), sample=True, nslot=3, dbg=()):
        self.seq, self.tt, self.en, self.sample, self.nslot, self.dbg = seq, tt, set(en), sample, nslot, dbg
        self.ntok = N_META + seq
        assert seq % tt == 0 and tt % 128 == 0


def blocks_of(T):
    if T <= 64:
        return [(0, T)]
    return [(64 * b, 64) for b in range(T // 64)]


class KB:
    def __init__(self, cfg):
        self.cfg = cfg
        nc = bass.Bass("TRN2", target_bir_lowering=False)
        self.nc = nc
        P = self.P = Prog(nc)
        TT = cfg.tt
        NT = cfg.ntok
        D = lambda n, s, dt=F32, kind="ExternalInput": P.dram(n, s, dt, kind=kind)
        self.xT_d = D("xT", [1024, NT])
        self.xsT_d = D("xsT", [1024, DEC_SEQ])
        self.ropeP_d = D("ropeP", [32, 2, NT])
        self.ropeS_d = D("ropeS", [32, 2, DEC_SEQ])
        self.cst_d = D("cst", [128, NCST])
        self.wbig_d = D("wbig", [DEPTH, NGRP, 128, GSZ])
        self.wsm_d = D("wsm", [DEPTH, 128, NSM])
        self.vec_d = D("vec", [DEPTH, 128, NV])
        self.fin_d = D("fin", [128, 8])
        self.s_in = dict(
            ckvT=D("s_ckvT", [DEPTH, 128, PAST]), kropeT=D("s_kropeT", [DEPTH, 32, PAST]),
            SB=D("s_SB", [DEPTH, 128, 2, 64]), shift=D("s_shift", [DEPTH, 128, 8]),
            SC=D("s_SC", [DEPTH, 128, 2, 64]), convC=D("s_convC", [DEPTH, 128, 6, 3]),
            SD=D("s_SD", [DEPTH, 128, 2, 64]), convD=D("s_convD", [DEPTH, 128, 6, 3]),
            convF=D("s_convF", [DEPTH, 128, 44, 2]))
        O = lambda n, s: P.dram(n, s, F32, kind="ExternalOutput")
        self.out = {}
        for sq, nt in (("p", NT), ("s", DEC_SEQ)):
            self.out[sq] = dict(
                yT=O(f"o_{sq}_yT", [1024, nt]), ckvT=O(f"o_{sq}_ckvT", [DEPTH, 128, nt]),
                kropeT=O(f"o_{sq}_kropeT", [DEPTH, 32, nt]),
                SB=O(f"o_{sq}_SB", [DEPTH, 128, 2, 64]), shift=O(f"o_{sq}_shift", [DEPTH, 128, 8]),
                SC=O(f"o_{sq}_SC", [DEPTH, 128, 2, 64]), convC=O(f"o_{sq}_convC", [DEPTH, 128, 6, 3]),
                SD=O(f"o_{sq}_SD", [DEPTH, 128, 2, 64]), convD=O(f"o_{sq}_convD", [DEPTH, 128, 6, 3]),
                convF=O(f"o_{sq}_convF", [DEPTH, 128, 44, 2]))
        self.out_handles = []
        self.dbg_out = {}
        self.wb_d = [[P.dram(f"wb_{l}_{g}", [128, GSZ], BF16) for g in range(NGRP)] for l in range(DEPTH)]
        self.kT_d = [P.dram(f"kTs_{l}", [128, 4, NT], BF16) for l in range(DEPTH)]
        self.v_d = [P.dram(f"vs_{l}", [NT, 260], BF16) for l in range(DEPTH)]
        S = P.sbuf
        self.cst = S("cst", [128, NCST - COFF["sel"]], F32)
        self.cstb = S("cstb", [128, NCST], BF16)
        self.vec = [S(f"vec{l}", [128, NV], F32) for l in range(DEPTH)]
        self.wsmring = [S(f"wsm{i}", [128, NSM], BF16) for i in range(1)]
        self.wsm = {}
        self.wsmb_d = [P.dram(f"wsmb_{l}", [128, NSM], BF16) for l in range(DEPTH)]
        self._wsmi = 0
        self.fin = S("fin", [128, 8], F32)
        self.dv = [S(f"dv{l}", [128, 8], F32) for l in range(DEPTH)]
        self.ring = [S(f"ring{i}", [128, GSZ], BF16) for i in range(cfg.nslot)]
        self.xT = S("xT_s", [128, 8, TT], F32)
        self.hT = S("hT", [128, 8, TT], BF16)
        self.rt = S("rt", [128, TT], F32)
        self.rstd = S("rstd", [128, TT], F32)
        self.mixT = S("mixT", [128, 8, TT], BF16)
        self.sq = self.mixT
        self.ystage = [S(f"ystage{i}", [128, TT], F32) for i in range(2)]
        self.pj = S("pj", [128, 8, TT + 3], F32)
        self.pj2 = S("pj2", [128, 4, TT], F32)
        self.ps = [P.psum(f"ps{i}", [128, 512], F32) for i in range(8)]
        self.epsc = S("epsc", [128, 4], F32)
        self._rr = 0
        self._rrb = 0
        self.ug = [S(f"ug{i}", [128, 4, TT + 2], F32) for i in range(1)]
        self.acc = [S(f"acc{i}", [128, 4, TT], F32) for i in range(1)]
        self.sil = S("sil", [128, 2, TT], F32)
        self._actbufs = None
        self.histF = [S(f"histF{l}", [128, 44, 2], F32) for l in range(DEPTH)]
        self.wseq = []
        self.wptr = 0
        self.wissued = 0

    def bank(self):
        b = self.ps[self._rr % 4]
        self._rr += 1
        return b

    def bankb(self):
        b = self.ps[4 + self._rrb % 4]
        self._rrb += 1
        return b

    def c32(self, name, rows=slice(0, 128), n=None, off=0):
        o = COFF[name] + off - COFF["sel"]
        return self.cst[rows, o:o + (n if n is not None else 1)]

    def cb(self, name, rows=slice(0, 128), n=None, off=0):
        o = COFF[name] + off
        return self.cstb[rows, o:o + (n if n is not None else 1)]

    def V(self, l, name, col=0, rows=slice(0, 128), n=1):
        o = VOFF[name] + col
        return self.vec[l][rows, o:o + n]

    def dbg(self, name, buf, ap, shape):
        if name not in self.cfg.dbg:
            return
        key = f"dbg_{name}_{len(self.dbg_out)}"
        d = self.P.dram(key, list(shape), F32, kind="ExternalOutput")
        nm = name
        while nm in self.dbg_out.values():
            nm = nm + "+"
        self.dbg_out[key] = nm
        h = self.P.dma("pool", d.ap(), ap, reads=[buf], owner=d)
        self.out_handles.append(h)

    def plan_weights(self, order):
        self.wseq = order

    def wget(self):
        P = self.P
        ns = self.cfg.nslot
        target = min(len(self.wseq), self.wptr + ns)
        while self.wissued < target:
            l, g = self.wseq[self.wissued]
            slot = self.ring[self.wissued % ns]
            P.dma("sp", slot[:, :], self.wb_d[l][g].ap(), reads=[self.wb_d[l][g]], writes=[slot])
            self.wissued += 1
        slot = self.ring[self.wptr % ns]
        self.wptr += 1
        return slot

    def prologue(self):
        P = self.P
        cfg = self.cfg
        P.dma("sp", self.cst[:, :], self.cst_d[:, COFF["sel"]:NCST], writes=[self.cst])
        P.dma("pool", self.cstb[:, :], self.cst_d.ap(), writes=[self.cstb])
        P.dma("sp", self.fin[:, :], self.fin_d.ap(), writes=[self.fin])
        for l in range(DEPTH):
            P.dma("sp", self.vec[l][:, :], self.vec_d[l], writes=[self.vec[l]])
            P.dma("pool", self.wsmb_d[l].ap(), self.wsm_d[l], writes=[self.wsmb_d[l]])
        for l in range(DEPTH):
            for (ga, gb) in ((0, G_WIN + G_WOUT), (G_WIN + G_WOUT, NGRP)):
                own = Buf(f"cast{l}_{ga}", None, "dr")
                for g in range(ga, gb):
                    P.dma("pool", self.wb_d[l][g].ap(), self.wbig_d[l, g], writes=[self.wb_d[l][g]],
                          owner=own, max_dma_last_dim=8192)
                for g in range(ga, gb):
                    self.wb_d[l][g].last_w = ("d", own, own.cnt["sw"], "sw")
        self.mla_init()
        self.scan_init()
        for i, v in enumerate((EPS, B_GN_EPS, 1.0, 0.0)):
            P.op("pool", lambda e, i=i, v=v: e.memset(self.epsc[:, i:i + 1], v), writes=[self.epsc])
        for l in range(DEPTH):
            P.op("pool", lambda e, l=l: e.memset(self.histF[l][:, :, :], 0.0), writes=[self.histF[l]])
            dv = self.dv[l]
            P.op("act", lambda e, l=l, dv=dv: e.activation(dv[0:8, 0:1], self.V(l, "calog", rows=slice(0, 8)), AF.Exp),
                 reads=[self.vec[l]], writes=[dv])
            P.op("act", lambda e, l=l, dv=dv: e.activation(dv[0:8, 1:2], self.V(l, "dalog", rows=slice(0, 8)), AF.Exp),
                 reads=[self.vec[l]], writes=[dv])
            P.op("dve", lambda e, dv=dv: e.tensor_scalar(dv[0:8, 0:2], dv[0:8, 0:2], -1.0, None, op0=ALU.mult),
                 reads=[dv], writes=[dv])
            P.op("dve", lambda e, l=l, dv=dv: e.tensor_scalar(dv[:, 2:4], self.V(l, "ka", n=2), -1.0, 1.0,
                                                             op0=ALU.mult, op1=ALU.add),
                 reads=[self.vec[l]], writes=[dv])

    def rmsnorm_x(self, T, gname, l):
        P = self.P
        xT, sq, hT, rt, rstd = self.xT, self.sq, self.hT, self.rt, self.rstd
        P.op("act", lambda e: e.activation(sq[:, :, 0:T], xT[:, :, 0:T], AF.Square), reads=[xT], writes=[sq])
        pb = self.bank()
        for kc in range(8):
            P.op("pe", lambda e, kc=kc: e.matmul(pb[:, 0:T], self.cb("ones", n=128), sq[:, kc, 0:T],
                                                 start=(kc == 0), stop=(kc == 7)),
                 reads=[self.cstb, sq], writes=[pb])
        P.op("act", lambda e: e.activation(rt[:, 0:T], pb[:, 0:T], AF.Sqrt, bias=self.epsc[:, 0:1],
                                           scale=1.0 / 1024.0),
             reads=[pb, self.epsc], writes=[rt])
        P.op("dve", lambda e: e.reciprocal(rstd[:, 0:T], rt[:, 0:T]), reads=[rt], writes=[rstd])
        for kc in range(8):
            g = self.fin[:, kc:kc + 1] if l is None else self.V(l, gname, kc)
            gb = self.fin if l is None else self.vec[l]
            P.op("dve", lambda e, kc=kc, g=g: e.scalar_tensor_tensor(
                hT[:, kc, 0:T], xT[:, kc, 0:T], g, rstd[:, 0:T], op0=ALU.mult, op1=ALU.mult),
                reads=[xT, gb, rstd], writes=[hT])

    def evac(self, out_ap, in_ap, rbufs, wbufs, eng=None):
        P = self.P
        if eng is None:
            self._ev = getattr(self, "_ev", 0) + 1
            eng = "act" if self._ev % 2 else "dve"
        if eng == "act":
            P.op("act", lambda e: e.activation(out_ap, in_ap, AF.Identity), reads=rbufs, writes=wbufs)
        elif eng == "dve":
            P.op("dve", lambda e: e.tensor_copy(out_ap, in_ap), reads=rbufs, writes=wbufs)
        else:
            P.op("pool", lambda e: e.tensor_copy(out_ap, in_ap), reads=rbufs, writes=wbufs)

    def proj_chunks(self, l, T, c0, c1, dest):
        P = self.P
        for ci in range(c0, c1):
            if ci % 4 == 0:
                self._wslot = self.wget()
            slot = self._wslot
            k = ci % 4
            pb = self.bank()
            for kc in range(8):
                o = k * 1024 + kc * 128
                P.op("pe", lambda e, o=o, kc=kc, pb=pb, slot=slot: e.matmul(
                    pb[:, 0:T], slot[:, o:o + 128], self.hT[:, kc, 0:T], start=(kc == 0), stop=(kc == 7)),
                    reads=[slot, self.hT], writes=[pb])
            buf, oap, r0, r1 = dest(ci)
            self.evac(oap, pb[r0:r1, 0:T], [pb], [buf])

    def outproj(self, l, T):
        P = self.P
        for g in range(2):
            slot = self.wget()
            for k in range(4):
                j = 4 * g + k
                pb = self.bank()
                for kc in range(8):
                    o = k * 1024 + kc * 128
                    P.op("pe", lambda e, o=o, kc=kc, pb=pb, slot=slot: e.matmul(
                        pb[:, 0:T], slot[:, o:o + 128], self.mixT[:, kc, 0:T], start=(kc == 0), stop=(kc == 7)),
                        reads=[slot, self.mixT], writes=[pb])
                P.op("dve", lambda e, j=j, pb=pb: e.tensor_tensor(
                    self.xT[:, j, 0:T], self.xT[:, j, 0:T], pb[:, 0:T], ALU.add),
                    reads=[self.xT, pb], writes=[self.xT])

    def act_view(self, i):
        TT = self.cfg.tt
        if self._actbufs is None:
            self._actbufs = [self.B("xbcs", [128, 8, TT], F32), self.B("khat32", [128, 2, TT], F32),
                             self.B("kb32", [128, 2, TT], F32)]
        if i < 16:
            b, k = self._actbufs[0], i
        elif i < 20:
            b, k = self._actbufs[1], i - 16
        else:
            b, k = self._actbufs[2], i - 20
        v = b.ap().bitcast(BF16).rearrange("p c (a t) -> p (c a) t", a=2)
        return b, v[:, k, :]

    def ffn(self, l, T):
        P = self.P
        hF = self.histF[l]
        for grp in range(G_WUP):
            slot = self.wget()
            ug = self.ug[0]
            acc = self.acc[0]
            P.op("pool", lambda e, ug=ug, grp=grp: e.tensor_copy(ug[:, :, 0:2], hF[:, 4 * grp:4 * grp + 4, :]),
                 reads=[hF], writes=[ug])
            for k in range(4):
                pb = self.bank()
                for kc in range(8):
                    o = k * 1024 + kc * 128
                    P.op("pe", lambda e, o=o, kc=kc, pb=pb, slot=slot: e.matmul(
                        pb[:, 0:T], slot[:, o:o + 128], self.hT[:, kc, 0:T], start=(kc == 0), stop=(kc == 7)),
                        reads=[slot, self.hT], writes=[pb])
                self.evac(ug[:, k, 2:2 + T], pb[:, 0:T], [pb], [ug], eng="act")
            P.op("pool", lambda e, ug=ug, grp=grp: e.tensor_copy(hF[:, 4 * grp:4 * grp + 4, :], ug[:, :, T:T + 2]),
                 reads=[ug], writes=[hF])
            for k in range(4):
                w = lambda i, k=k, grp=grp: self.V(l, "fcw", (4 * grp + k) * 3 + i)
                P.op("pool", lambda e, k=k, w=w, ug=ug, acc=acc: e.tensor_scalar(
                    acc[:, k, 0:T], ug[:, k, 0:T], w(0), None, op0=ALU.mult),
                    reads=[ug, self.vec[l]], writes=[acc])
                for i in (1, 2):
                    P.op("dve", lambda e, k=k, i=i, w=w, ug=ug, acc=acc: e.scalar_tensor_tensor(
                        acc[:, k, 0:T], ug[:, k, i:i + T], w(i), acc[:, k, 0:T], op0=ALU.mult, op1=ALU.add),
                        reads=[ug, acc, self.vec[l]], writes=[acc])
            for h2 in range(2):
                P.op("act", lambda e, h2=h2, acc=acc: e.activation(self.sil[:, h2, 0:T], acc[:, 2 * h2, 0:T], AF.Silu),
                     reads=[acc], writes=[self.sil])
            for h2 in range(2):
                ab, av = self.act_view(2 * grp + h2)
                self.TT("dve", av[:, 0:T], self.sil[:, h2, 0:T], acc[:, 2 * h2 + 1, 0:T], ALU.mult, [self.sil, acc], [ab])
        for j in range(8):
            slot = self.wget()
            pb = self.bank()
            for kc in range(22):
                ab, av = self.act_view(kc)
                self.MM(pb[:, 0:T], slot[:, kc * 128:(kc + 1) * 128], av[:, 0:T], kc == 0, kc == 21, [slot, ab], [pb])
            P.op("dve", lambda e, j=j, pb=pb: e.tensor_tensor(
                self.xT[:, j, 0:T], self.xT[:, j, 0:T], pb[:, 0:T], ALU.add),
                reads=[self.xT, pb], writes=[self.xT])

    def final_norm(self, sq_name, t0, T):
        P = self.P
        xT, sq, rt, rstd = self.xT, self.sq, self.rt, self.rstd
        P.op("act", lambda e: e.activation(sq[:, :, 0:T], xT[:, :, 0:T], AF.Square), reads=[xT], writes=[sq])
        pb = self.bank()
        for kc in range(8):
            P.op("pe", lambda e, kc=kc: e.matmul(pb[:, 0:T], self.cb("ones", n=128), sq[:, kc, 0:T],
                                                 start=(kc == 0), stop=(kc == 7)),
                 reads=[self.cstb, sq], writes=[pb])
        P.op("act", lambda e: e.activation(rt[:, 0:T], pb[:, 0:T], AF.Sqrt, bias=self.epsc[:, 0:1],
                                           scale=1.0 / 1024.0), reads=[pb, self.epsc], writes=[rt])
        P.op("dve", lambda e: e.reciprocal(rstd[:, 0:T], rt[:, 0:T]), reads=[rt], writes=[rstd])
        yd = self.out[sq_name]["yT"]
        for kc in range(8):
            ys = self.ystage[kc % 2]
            P.op("dve", lambda e, kc=kc, ys=ys: e.scalar_tensor_tensor(
                ys[:, 0:T], xT[:, kc, 0:T], self.fin[:, kc:kc + 1], rstd[:, 0:T], op0=ALU.mult, op1=ALU.mult),
                reads=[xT, self.fin, rstd], writes=[ys])
            h = P.dma("pool", yd[kc * 128:(kc + 1) * 128, t0:t0 + T], ys[:, 0:T], reads=[ys], owner=ys)
            self.out_handles.append(h)

    def process_tile(self, sqn, t0, T):
        P = self.P
        src = self.xT_d if sqn == "p" else self.xsT_d
        P.dma("sp", self.xT[:, :, 0:T],
              src.ap().rearrange("(c p) t -> p c t", p=128)[:, :, t0:t0 + T], writes=[self.xT])
        rsrc = self.ropeP_d if sqn == "p" else self.ropeS_d
        P.dma("sp", self.ropeT[32:64, :, 0:T], rsrc[:, :, t0:t0 + T], writes=[self.ropeT])
        for l in range(DEPTH):
            self.rmsnorm_x(T, "n1g", l)
            self.mixers(sqn, l, t0, T)
            self.outproj(l, T)
            self.rmsnorm_x(T, "n2g", l)
            self.ffn(l, T)
        self.final_norm(sqn, t0, T)

    def mixers(self, sqn, l, t0, T):
        P = self.P
        en = self.cfg.en
        wb_ = self.wsmring[0]
        self._wsmi += 1
        P.dma("sp", wb_[:, :], self.wsmb_d[l].ap(), reads=[self.wsmb_d[l]], writes=[wb_])
        self.wsm[l] = wb_
        def destA(ci):
            return self.pj, self.pj[:, ci, 0:T], 0, 128
        self.proj_chunks(l, T, 0, 5, destA)
        if "A" in en:
            self.mla(sqn, l, t0, T)
        else:
            P.op("pool", lambda e: e.memset(self.mixT[:, 0:2, 0:T], 0.0), writes=[self.mixT])
        def destB(ci):
            return self.pj, self.pj[:, ci - 5, 1:1 + T], 0, 128
        self.proj_chunks(l, T, 5, 13, destB)
        if "B" in en:
            self.rwkv(sqn, l, t0, T)
        else:
            P.op("pool", lambda e: e.memset(self.mixT[:, 2:4, 0:T], 0.0), writes=[self.mixT])
        def destC(ci):
            if ci < 15:
                return self.pj2, self.pj2[:, ci - 13, 0:T], 0, 128
            if ci < 21:
                return self.pj, self.pj[:, ci - 15, 3:3 + T], 0, 128
            return self.pj2, self.pj2[0:8, 2, 0:T], 0, 8
        self.proj_chunks(l, T, 13, 22, destC)
        if "C" in en:
            self.ssd(sqn, l, t0, T)
        else:
            P.op("pool", lambda e: e.memset(self.mixT[:, 4:6, 0:T], 0.0), writes=[self.mixT])
        def destD(ci):
            if ci < 28:
                return self.pj, self.pj[:, ci - 22, 3:3 + T], 0, 128
            if ci < 30:
                return self.pj2, self.pj2[:, ci - 28, 0:T], 0, 128
            return self.pj2, self.pj2[0:8, ci - 28, 0:T], 0, 8
        self.proj_chunks(l, T, 22, 32, destD)
        if "D" in en:
            self.gdn(sqn, l, t0, T)
        else:
            P.op("pool", lambda e: e.memset(self.mixT[:, 6:8, 0:T], 0.0), writes=[self.mixT])

    def tiles(self):
        cfg = self.cfg
        ts = [("p", 0, N_META)]
        for i in range(cfg.seq // cfg.tt):
            ts.append(("p", N_META + i * cfg.tt, cfg.tt))
        return ts

    def emit_states(self, sqn):
        P = self.P
        o = self.out[sqn]
        bl = []
        for l in range(DEPTH):
            for nm, b in (("convF", self.histF[l]), ("SB", self.S32["B"][l]), ("SC", self.S32["C"][l]),
                          ("SD", self.S32["D"][l]), ("convC", self.histC[l]), ("convD", self.histD[l]),
                          ("shift", self.shiftB[l])):
                own = self.__dict__.setdefault("_stout_" + sqn, Buf("stout_" + sqn, None, "dr"))
                P.dma("pool", o[nm][l], b.ap(), reads=[b], owner=own)
                bl.append(b)
        fin_h = ("d", own, own.cnt["sw"], "sw")
        for b in bl:
            b.readers = [fin_h if (r[0] == "d" and r[1] is own) else r for r in b.readers]
        self.out_handles.append(fin_h)

    def load_states(self):
        P = self.P
        bl = []
        for l in range(DEPTH):
            for nm, b in (("convF", self.histF[l]), ("SB", self.S32["B"][l]), ("SC", self.S32["C"][l]),
                          ("SD", self.S32["D"][l]), ("convC", self.histC[l]), ("convD", self.histD[l]),
                          ("shift", self.shiftB[l])):
                own = self.__dict__.setdefault("_stin", Buf("stin", None, "dr"))
                P.dma("sp", b.ap(), self.s_in[nm][l], writes=[b], owner=own)
                bl.append(b)
        for b in bl:
            b.last_w = ("d", own, own.cnt["hw"], "hw")

    def build(self):
        cfg = self.cfg
        P = self.P
        tl = self.tiles()
        n_tl = len(tl) + (1 if cfg.sample else 0)
        self.plan_weights([(l, g) for _ in range(n_tl) for l in range(DEPTH) for g in range(NGRP)])
        self.prologue()
        for (sqn, t0, T) in tl:
            self.process_tile(sqn, t0, T)
        self.emit_states("p")
        if cfg.sample:
            self.load_states()
            self.process_tile("s", 0, DEC_SEQ)
            self.emit_states("s")
        P.wait_all("pool", self.out_handles)
        P.build()
        return self.nc


def _prep_core_inputs(inp, b, cfg, shared):
    NT = cfg.ntok
    x = np.concatenate([np.asarray(inp["meta_tokens"], np.float32), np.asarray(inp["x_prompt"][b], np.float32)], 0)
    m = dict(shared)
    m["xT"] = np.ascontiguousarray(x.T)
    m["xsT"] = np.ascontiguousarray(np.asarray(inp["x_sample"][b], np.float32).T)
    ck = np.asarray(inp["cache_mla_ckv"][:, b], np.float32)
    m["s_ckvT"] = np.ascontiguousarray(ck.transpose(0, 2, 1))
    kr = np.asarray(inp["cache_mla_krope"][:, b], np.float32)
    m["s_kropeT"] = np.ascontiguousarray(kr.transpose(0, 2, 1))
    sb = np.asarray(inp["state_rwkv"][:, b], np.float32)
    m["s_SB"] = np.ascontiguousarray(sb.reshape(DEPTH, 2, 2, 64, 64).transpose(0, 2, 4, 1, 3).reshape(DEPTH, 128, 2, 64))
    m["s_shift"] = np.ascontiguousarray(np.asarray(inp["state_rwkv_shift"][:, b], np.float32).reshape(DEPTH, 8, 128).transpose(0, 2, 1))
    sc = np.asarray(inp["state_ssd"][:, b], np.float32)
    m["s_SC"] = np.ascontiguousarray(sc.reshape(DEPTH, 2, 2, 64, 64).transpose(0, 2, 4, 1, 3).reshape(DEPTH, 128, 2, 64))
    cc = np.asarray(inp["state_ssd_conv"][:, b], np.float32)
    a = np.zeros((DEPTH, 128, 6, 3), np.float32)
    for c, srcc in enumerate(C_XBC_SRC):
        a[:, :, c, :] = cc[:, :, srcc].transpose(0, 2, 1)
    m["s_convC"] = a
    sd = np.asarray(inp["state_gdn"][:, b], np.float32)
    m["s_SD"] = np.ascontiguousarray(sd.reshape(DEPTH, 2, 2, 64, 64).transpose(0, 2, 3, 1, 4).reshape(DEPTH, 128, 2, 64))
    cd = np.asarray(inp["state_gdn_conv"][:, b], np.float32)
    m["s_convD"] = np.ascontiguousarray(cd.reshape(DEPTH, 3, 6, 128).transpose(0, 3, 2, 1))
    cf = np.asarray(inp["state_ffn_conv"][:, b], np.float32)
    a = np.zeros((DEPTH, 128, 44, 2), np.float32)
    for c, srcc in enumerate(FF_ORDER):
        a[:, :, c, :] = cf[:, :, srcc].transpose(0, 2, 1)
    m["s_convF"] = a
    return m


def _post_core(res, cfg):
    o = {}
    for sq in ("p", "s"):
        g = lambda n: np.asarray(res[f"o_{sq}_{n}"])
        y = g("yT").T
        o[f"{sq}_y"] = y[N_META:] if sq == "p" else y
        o[f"{sq}_ckv"] = g("ckvT").transpose(0, 2, 1)
        o[f"{sq}_krope"] = g("kropeT").transpose(0, 2, 1)
        sb = g("SB").reshape(DEPTH, 2, 64, 2, 64)
        o[f"{sq}_SB"] = sb.transpose(0, 3, 1, 4, 2).reshape(DEPTH, 4, 64, 64)
        o[f"{sq}_shift"] = g("shift").transpose(0, 2, 1).reshape(DEPTH, 1024)
        sc = g("SC").reshape(DEPTH, 2, 64, 2, 64)
        o[f"{sq}_SC"] = sc.transpose(0, 3, 1, 4, 2).reshape(DEPTH, 4, 64, 64)
        cc = g("convC")
        full = np.zeros((DEPTH, 3, 512), np.float32)
        for c, srcc in enumerate(C_XBC_SRC):
            n = 128 if c < 2 else 64
            full[:, :, srcc[:n]] = cc[:, :n, c, :].transpose(0, 2, 1)
        o[f"{sq}_convC"] = full
        sd = g("SD").reshape(DEPTH, 2, 64, 2, 64)
        o[f"{sq}_SD"] = sd.transpose(0, 3, 1, 2, 4).reshape(DEPTH, 4, 64, 64)
        o[f"{sq}_convD"] = g("convD").transpose(0, 3, 2, 1).reshape(DEPTH, 3, 768)
        cf = g("convF")
        full = np.zeros((DEPTH, 2, 2 * D_FF), np.float32)
        for c, srcc in enumerate(FF_ORDER):
            full[:, :, srcc] = cf[:, :, c, :].transpose(0, 2, 1)
        o[f"{sq}_convF"] = full
    return o


OUT_ORDER = ["y", "ckv", "krope", "SB", "shift", "SC", "convC", "SD", "convD", "convF"]


def run_cfg(cfg, inp, n_cores, trace=False):
    kb = KB(cfg)
    nc = kb.build()
    wbig, wsm, vec, fin = prep_weights(inp)
    pos_p = np.arange(cfg.ntok, dtype=np.int64) - N_META
    pos_s = PAST + np.arange(DEC_SEQ, dtype=np.int64)
    shared = dict(ropeP=_rope_tables(pos_p), ropeS=_rope_tables(pos_s), cst=_consts(),
                  wbig=wbig, wsm=wsm, vec=vec, fin=fin)
    in_maps = [_prep_core_inputs(inp, b, cfg, shared) for b in range(n_cores)]
    res = run_bass_kernel_spmd(nc, in_maps, core_ids=list(range(n_cores)), trace=trace)
    per = [_post_core(r, cfg) for r in res.results]
    outs = []
    for sq in ("p", "s"):
        for n in OUT_ORDER:
            a = np.stack([p[f"{sq}_{n}"] for p in per], 0)
            if n != "y":
                a = np.moveaxis(a, 0, 1)
            outs.append(np.ascontiguousarray(a.astype(np.float32)))
    ordered = [outs[0], outs[10]] + outs[1:10] + outs[11:20]
    dbg = [{kb.dbg_out[k]: np.asarray(r[k]) for k in kb.dbg_out} for r in res.results]
    return tuple(ordered), dbg, res


def kernel(**inputs):
    cfg = Cfg(seq=int(np.asarray(inputs["x_prompt"]).shape[1]))
    outs, _, _ = run_cfg(cfg, inputs, 8)
    return outs


def _B(self, name, shape, dtype):
    d = self.__dict__.setdefault("_bufs", {})
    if name not in d:
        d[name] = self.P.sbuf(name, shape, dtype)
    return d[name]


KB.B = _B


def _mla_init(self):
    P = self.P
    TT = self.cfg.tt
    KT = TT // 128
    self.kaug = self.B("kaug", [128, 4, TT], BF16)
    self.qaug = self.B("qaug", [128, 4, TT], BF16)
    self.vaug = self.B("vaug", [128, KT, 4, 65], BF16)
    self.kseg = [self.B(f"kseg{i}", [128, 4, TT], BF16) for i in range(4)]
    self.vseg = [self.B(f"vseg{i}", [128, KT, 4, 65], BF16) for i in range(4)]
    self.pbuf = [self.B(f"pbuf{i}", [128, TT], BF16) for i in range(3)]
    self.zt = self.B("zt", [128, 260], BF16)
    self.kmax2 = [self.B(f"kmax2_{l}", [32, 4], F32) for l in range(DEPTH)]
    self.ropeT = self.B("ropeT", [64, 2, TT], F32)
    P.op("pool", lambda e: e.memset(self.zt[:, :], 0.0), writes=[self.zt])
    for kb_ in [self.kaug] + self.kseg:
        P.op("pool", lambda e, kb_=kb_: e.memset(kb_[0:32, :, :], 0.0), writes=[kb_])
        P.op("pool", lambda e, kb_=kb_: e.memset(kb_[0:1, :, :], 1.0), writes=[kb_])
    for vb_ in [self.vaug] + self.vseg:
        P.op("pool", lambda e, vb_=vb_: e.memset(vb_[:, :, :, 64:65], 1.0), writes=[vb_])
    for l in range(DEPTH):
        P.op("pool", lambda e, l=l: e.memset(self.kmax2[l][:, :], 0.0), writes=[self.kmax2[l]])


KB.mla_init = _mla_init


def _key_norm_update(self, l, kbuf, n):
    P = self.P
    TT = self.cfg.tt
    sqk = self.B("sqk", [128, TT], BF16)
    mx = self.B("mxk", [32, 1], F32)
    km = self.kmax2[l]
    for h in range(4):
        P.op("act", lambda e, h=h: e.activation(sqk[:, 0:n], kbuf[:, h, 0:n], AF.Square), reads=[kbuf], writes=[sqk])
        pb = self.bank()
        P.op("pe", lambda e, pb=pb: e.matmul(pb[0:32, 0:n], self.cb("colk", n=32), sqk[:, 0:n], start=True, stop=True),
             reads=[self.cstb, sqk], writes=[pb])
        P.op("dve", lambda e, pb=pb: e.reduce_max(mx[:, 0:1], pb[0:32, 0:n], AX.X), reads=[pb], writes=[mx])
        P.op("dve", lambda e, h=h: e.tensor_tensor(km[:, h:h + 1], km[:, h:h + 1], mx[:, 0:1], ALU.max),
             reads=[km, mx], writes=[km])


KB.key_norm_update = _key_norm_update


def _attend(self, kbuf, vbuf, n, T, own, masked, st):
    P = self.P
    QC = [(j * 128, min(128, T - j * 128)) for j in range((T + 127) // 128)]
    for kt in range((n + 127) // 128):
        nk = min(128, n - kt * 128)
        q0 = kt * 128 if own else 0
        for h in range(4):
            S = self.bank()
            P.op("pe", lambda e, S=S, kt=kt, nk=nk, h=h, q0=q0: e.matmul(
                S[0:nk, 0:T - q0], kbuf[:, h, kt * 128:kt * 128 + nk], self.qaug[:, h, q0:T], start=True, stop=True),
                reads=[kbuf, self.qaug], writes=[S])
            pt = self.pbuf[st["pi"] % 3]
            st["pi"] += 1
            P.op("act", lambda e, S=S, pt=pt, nk=nk, q0=q0: e.activation(
                pt[0:nk, 0:T - q0], S[0:nk, 0:T - q0], AF.Exp, scale=float(A_SCALE)), reads=[S], writes=[pt])
            if own and masked and nk == 128:
                P.op("pool", lambda e, pt=pt: e.memset(pt[64:128, 0:64], 0.0), writes=[pt])
            for j, (qs, qn) in enumerate(QC):
                if qs < q0:
                    continue
                ob = self.ps[4 + j]
                last = own and (kt == j)
                P.op("pe", lambda e, ob=ob, pt=pt, nk=nk, qs=qs, qn=qn, q0=q0, h=h, kt=kt, last=last: e.matmul(
                    ob[0:qn, h * 65:(h + 1) * 65], pt[0:nk, qs - q0:qs - q0 + qn], vbuf[0:nk, kt, h, :],
                    start=False, stop=last, skip_group_check=True),
                    reads=[pt, vbuf], writes=[ob])


KB.attend = _attend


def _mla(self, sqn, l, t0, T):
    P = self.P
    cfg = self.cfg
    TT = cfg.tt
    pj = self.pj
    vecl = self.vec[l]
    wsm = self.wsm[l]
    out = self.out[sqn]
    QC = [(j * 128, min(128, T - j * 128)) for j in range((T + 127) // 128)]
    sqA = self.B("sqA", [128, 2, TT], BF16)
    rtq = self.B("rtq", [128, TT], F32)
    rsq = self.B("rsq", [128, TT], F32)
    qn = self.B("qn", [128, 2, TT], BF16)
    sqc = self.B("sqc", [128, TT], BF16)
    cT32 = self.B("cT32", [128, TT], F32)
    cTb = self.B("cTb", [128, TT], BF16)
    krr = self.B("krr", [64, TT], F32)
    tmr = self.B("tmr", [64, TT], F32)
    tq1 = self.B("tq1", [64, TT], F32)
    tq2 = self.B("tq2", [64, TT], F32)
    sqq = self.B("sqq", [128, TT], BF16)
    nq = self.B("nq", [32, TT], F32)
    negk = self.B("negk", [32, 4], F32)
    kaug, qaug, vaug, ropeT = self.kaug, self.qaug, self.vaug, self.ropeT
    P.op("act", lambda e: e.activation(sqA[:, :, 0:T], pj[:, 0:2, 0:T], AF.Square), reads=[pj], writes=[sqA])
    pb = self.bank()
    for c in range(2):
        P.op("pe", lambda e, c=c, pb=pb: e.matmul(pb[:, 0:T], self.cb("ones", n=128), sqA[:, c, 0:T],
                                                  start=(c == 0), stop=(c == 1)), reads=[self.cstb, sqA], writes=[pb])
    P.op("act", lambda e, pb=pb: e.activation(rtq[:, 0:T], pb[:, 0:T], AF.Sqrt, bias=self.epsc[:, 0:1], scale=1.0 / 192.0),
         reads=[pb, self.epsc], writes=[rtq])
    P.op("dve", lambda e: e.reciprocal(rsq[:, 0:T], rtq[:, 0:T]), reads=[rtq], writes=[rsq])
    for c in range(2):
        P.op("dve", lambda e, c=c: e.scalar_tensor_tensor(qn[:, c, 0:T], pj[:, c, 0:T], self.V(l, "gq", c), rsq[:, 0:T],
                                                          op0=ALU.mult, op1=ALU.mult), reads=[pj, vecl, rsq], writes=[qn])
    P.op("act", lambda e: e.activation(sqc[:, 0:T], pj[:, 2, 0:T], AF.Square), reads=[pj], writes=[sqc])
    pb2 = self.bank()
    P.op("pe", lambda e: e.matmul(pb2[:, 0:T], self.cb("ones", n=128), sqc[:, 0:T], start=True, stop=True),
         reads=[self.cstb, sqc], writes=[pb2])
    P.op("act", lambda e: e.activation(rtq[:, 0:T], pb2[:, 0:T], AF.Sqrt, bias=self.epsc[:, 0:1], scale=1.0 / 128.0),
         reads=[pb2, self.epsc], writes=[rtq])
    P.op("dve", lambda e: e.reciprocal(rsq[:, 0:T], rtq[:, 0:T]), reads=[rtq], writes=[rsq])
    P.op("dve", lambda e: e.scalar_tensor_tensor(cT32[:, 0:T], pj[:, 2, 0:T], self.V(l, "gkv"), rsq[:, 0:T],
                                                 op0=ALU.mult, op1=ALU.mult), reads=[pj, vecl, rsq], writes=[cT32])
    P.op("act", lambda e: e.activation(cTb[:, 0:T], cT32[:, 0:T], AF.Identity), reads=[cT32], writes=[cTb])
    self.out_handles.append(P.dma("pool", out["ckvT"][l, :, t0:t0 + T], cT32[:, 0:T], reads=[cT32], owner=cT32))
    R = slice(32, 64)
    P.op("dve", lambda e: e.tensor_tensor(krr[R, 0:T], pj[R, 3, 0:T], ropeT[R, 0, 0:T], ALU.mult),
         reads=[pj, ropeT], writes=[krr])
    P.op("dve", lambda e: e.tensor_tensor(tmr[R, 0:T], pj[R, 4, 0:T], ropeT[R, 1, 0:T], ALU.mult),
         reads=[pj, ropeT], writes=[tmr])
    P.op("dve", lambda e: e.tensor_tensor(krr[R, 0:T], krr[R, 0:T], tmr[R, 0:T], ALU.add), reads=[krr, tmr], writes=[krr])
    self.out_handles.append(P.dma("pool", out["kropeT"][l, :, t0:t0 + T], krr[R, 0:T], reads=[krr], owner=krr))
    P.op("dve", lambda e: e.tensor_copy(kaug[R, :, 0:T], krr[R, 0:T].unsqueeze(1).to_broadcast([32, 4, T])),
         reads=[krr], writes=[kaug])
    if getattr(cfg, "cut", 99) == 1:
        P.op("pool", lambda e: e.memset(self.mixT[:, 0:2, 0:T], 0.0), writes=[self.mixT])
        return
    for h in range(4):
        pbk = self.bank()
        o = SOFF["wuk"] + h * 128
        P.op("pe", lambda e, pbk=pbk, o=o: e.matmul(pbk[:, 0:T], wsm[:, o:o + 128], cTb[:, 0:T], start=True, stop=True),
             reads=[wsm, cTb], writes=[pbk])
        self.evac(kaug[64:128, h, 0:T], pbk[64:128, 0:T], [pbk], [kaug])
    for kt in range((T + 127) // 128):
        nk = min(128, T - kt * 128)
        pbv = self.bank()
        o = SOFF["wuv"]
        P.op("pe", lambda e, pbv=pbv, kt=kt, nk=nk, o=o: e.matmul(
            pbv[0:nk, 0:256], cTb[:, kt * 128:kt * 128 + nk], wsm[:, o:o + 256], start=True, stop=True),
            reads=[cTb, wsm], writes=[pbv])
        self.evac(vaug[0:nk, kt, :, 0:64], pbv[0:nk, 0:256].rearrange("p (h d) -> p h d", h=4), [pbv], [vaug])
    if getattr(cfg, "cut", 99) == 2:
        P.op("pool", lambda e: e.memset(self.mixT[:, 0:2, 0:T], 0.0), writes=[self.mixT])
        return
    segs = []
    if sqn == "s":
        for h in range(4):
            pass
        P.op("pool", lambda e: e.memset(self.kmax2[l][:, :], 0.0), writes=[self.kmax2[l]])
        cpb = self.B("cpb", [128, TT], BF16)
        kps = self.B("kps", [64, TT], BF16)
        for si in range(PAST // TT):
            k0 = si * TT
            ks, vs = self.kseg[si], self.vseg[si]
            P.dma("pool", cpb[:, 0:TT], self.s_in["ckvT"][l, :, k0:k0 + TT], writes=[cpb])
            P.dma("pool", kps[R, 0:TT], self.s_in["kropeT"][l, :, k0:k0 + TT], writes=[kps])
            P.op("dve", lambda e, ks=ks: e.tensor_copy(ks[R, :, 0:TT], kps[R, 0:TT].unsqueeze(1).to_broadcast([32, 4, TT])),
                 reads=[kps], writes=[ks])
            for h in range(4):
                pbk = self.bank()
                o = SOFF["wuk"] + h * 128
                P.op("pe", lambda e, pbk=pbk, o=o: e.matmul(pbk[:, 0:TT], wsm[:, o:o + 128], cpb[:, 0:TT], start=True, stop=True),
                     reads=[wsm, cpb], writes=[pbk])
                self.evac(ks[64:128, h, 0:TT], pbk[64:128, 0:TT], [pbk], [ks])
            for kt in range(TT // 128):
                pbv = self.bank()
                o = SOFF["wuv"]
                P.op("pe", lambda e, pbv=pbv, kt=kt, o=o: e.matmul(
                    pbv[:, 0:256], cpb[:, kt * 128:(kt + 1) * 128], wsm[:, o:o + 256], start=True, stop=True),
                    reads=[cpb, wsm], writes=[pbv])
                self.evac(vs[:, kt, :, 0:64], pbv[:, 0:256].rearrange("p (h d) -> p h d", h=4), [pbv], [vs])
            self.key_norm_update(l, ks, TT)
            segs.append((ks, vs, TT))
    self.key_norm_update(l, kaug, T)
    P.op("act", lambda e: e.activation(negk[:, :], self.kmax2[l][:, :], AF.Sqrt), reads=[self.kmax2[l]], writes=[negk])
    P.op("dve", lambda e: e.tensor_scalar(negk[:, :], negk[:, :], -1.0, None, op0=ALU.mult), reads=[negk], writes=[negk])
    if getattr(cfg, "cut", 99) == 3:
        P.op("pool", lambda e: e.memset(self.mixT[:, 0:2, 0:T], 0.0), writes=[self.mixT])
        return
    for h in range(4):
        pm = self.bank()
        psw = self.bank()
        base = SOFF["wuq"]
        for kc in range(2):
            kp = 128 if kc == 0 else 64
            o = base + (kc * 4 + h) * 192
            P.op("pe", lambda e, pm=pm, kc=kc, kp=kp, o=o: e.matmul(
                pm[:, 0:T], wsm[0:kp, o:o + 128], qn[0:kp, kc, 0:T], start=(kc == 0), stop=(kc == 1)),
                reads=[wsm, qn], writes=[pm])
        for kc in range(2):
            kp = 128 if kc == 0 else 64
            o = base + (kc * 4 + h) * 192 + 128
            P.op("pe", lambda e, psw=psw, kc=kc, kp=kp, o=o: e.matmul(
                psw[0:64, 0:T], wsm[0:kp, o:o + 64], qn[0:kp, kc, 0:T], start=(kc == 0), stop=(kc == 1)),
                reads=[wsm, qn], writes=[psw])
        sub = getattr(cfg, "sub", 99)
        if sub < 1:
            continue
        P.op("act", lambda e, pm=pm, h=h: e.activation(qaug[64:128, h, 0:T], pm[64:128, 0:T], AF.Identity),
             reads=[pm], writes=[qaug])
        if sub < 2:
            continue
        P.op("dve", lambda e, pm=pm: e.tensor_tensor(tq1[R, 0:T], pm[R, 0:T], ropeT[R, 0, 0:T], ALU.mult),
             reads=[pm, ropeT], writes=[tq1])
        P.op("dve", lambda e, psw=psw: e.tensor_tensor(tq2[R, 0:T], psw[R, 0:T], ropeT[R, 1, 0:T], ALU.mult),
             reads=[psw, ropeT], writes=[tq2])
        P.op("dve", lambda e, h=h: e.tensor_tensor(qaug[R, h, 0:T], tq1[R, 0:T], tq2[R, 0:T], ALU.add),
             reads=[tq1, tq2], writes=[qaug])
        if sub < 3:
            continue
        P.op("act", lambda e, pm=pm: e.activation(sqq[:, 0:T], pm[:, 0:T], AF.Square), reads=[pm], writes=[sqq])
        pn = self.bank()
        P.op("pe", lambda e, pn=pn: e.matmul(pn[0:32, 0:T], self.cb("col0", n=32), sqq[:, 0:T], start=True, stop=True),
             reads=[self.cstb, sqq], writes=[pn])
        if sub < 4:
            continue
        P.op("act", lambda e, pn=pn: e.activation(nq[:, 0:T], pn[0:32, 0:T], AF.Sqrt), reads=[pn], writes=[nq])
        if sub < 5:
            continue
        P.op("dve", lambda e, h=h: e.tensor_scalar(qaug[0:32, h, 0:T], nq[:, 0:T], negk[:, h:h + 1], None, op0=ALU.mult),
             reads=[nq, negk], writes=[qaug])
    if getattr(cfg, "cut", 99) == 4:
        P.op("pool", lambda e: e.memset(self.mixT[:, 0:2, 0:T], 0.0), writes=[self.mixT])
        return
    if sqn == "p":
        P.dma("sp", self.kT_d[l][:, :, t0:t0 + T], kaug[:, :, 0:T], reads=[kaug], writes=[self.kT_d[l]])
        for kt in range((T + 127) // 128):
            nk = min(128, T - kt * 128)
            P.dma("sp", self.v_d[l][t0 + kt * 128:t0 + kt * 128 + nk, :],
                  vaug[0:nk, kt, :, :].rearrange("p h d -> p (h d)"), reads=[vaug], writes=[self.v_d[l]])
    if getattr(cfg, "cut", 99) == 5:
        P.op("pool", lambda e: e.memset(self.mixT[:, 0:2, 0:T], 0.0), writes=[self.mixT])
        return
    for j, (qs, qn_) in enumerate(QC):
        ob = self.ps[4 + j]
        P.op("pe", lambda e, ob=ob, qn_=qn_: e.matmul(ob[0:qn_, 0:260], self.zt[0:1, 0:qn_], self.zt[0:1, 0:260],
                                                      start=True, stop=False, skip_group_check=True),
             reads=[self.zt], writes=[ob])
    st = self.__dict__.setdefault("_attst", {"pi": 0, "si": 0})
    if sqn == "p" and t0 > 0:
        prev = [(0, N_META)] + [(N_META + i * TT, TT) for i in range((t0 - N_META) // TT)]
        for (k0, n) in prev:
            ks, vs = self.kseg[st["si"] % 4], self.vseg[st["si"] % 4]
            st["si"] += 1
            P.dma("sp", ks[:, :, 0:n], self.kT_d[l][:, :, k0:k0 + n], reads=[self.kT_d[l]], writes=[ks])
            nkt = (n + 127) // 128
            if n >= 128:
                P.dma("sp", vs[:, 0:nkt, :, :].rearrange("p k h d -> p k (h d)"),
                      self.v_d[l][k0:k0 + n, :].rearrange("(k p) c -> p k c", p=128), reads=[self.v_d[l]], writes=[vs])
            else:
                P.dma("sp", vs[0:n, 0, :, :].rearrange("p h d -> p (h d)"), self.v_d[l][k0:k0 + n, :],
                      reads=[self.v_d[l]], writes=[vs])
            self.attend(ks, vs, n, T, False, False, st)
    for (ks, vs, n) in segs:
        self.attend(ks, vs, n, T, False, False, st)
    self.attend(kaug, vaug, T, T, True, sqn == "p", st)
    if getattr(cfg, "cut", 99) == 6:
        P.op("pool", lambda e: e.memset(self.mixT[:, 0:2, 0:T], 0.0), writes=[self.mixT])
        return
    rden = self.B("rden", [128, 4], F32)
    on = self.B("on", [128, 4, 64], F32)
    junk = self.B("junkA", [128, 256], F32)
    ss = self.B("ssA", [128, 1], F32)
    rts = self.B("rtsA", [128, 1], F32)
    yat = self.B("yat", [128, 256], BF16)
    for j, (qs, qn_) in enumerate(QC):
        ob = self.ps[4 + j]
        obv = ob[0:qn_, 0:260].rearrange("p (h d) -> p h d", h=4)
        P.op("dve", lambda e, obv=obv, qn_=qn_: e.reciprocal(rden[0:qn_, :], obv[:, :, 64]), reads=[ob], writes=[rden])
        P.op("dve", lambda e, obv=obv, qn_=qn_: e.tensor_tensor(
            on[0:qn_, :, :], obv[:, :, 0:64], rden[0:qn_, :].unsqueeze(2).to_broadcast([qn_, 4, 64]), ALU.mult),
            reads=[ob, rden], writes=[on])
        P.op("act", lambda e, qn_=qn_: e.activation(junk[0:qn_, :], on[0:qn_, :, :].rearrange("p h d -> p (h d)"),
                                                    AF.Square, accum_out=ss[0:qn_, 0:1]), reads=[on], writes=[junk, ss])
        P.op("act", lambda e, qn_=qn_: e.activation(rts[0:qn_, :], ss[0:qn_, :], AF.Sqrt, bias=self.epsc[0:qn_, 0:1],
                                                    scale=1.0 / 256.0), reads=[ss, self.epsc], writes=[rts])
        P.op("dve", lambda e, qn_=qn_: e.reciprocal(rts[0:qn_, :], rts[0:qn_, :]), reads=[rts], writes=[rts])
        P.op("act", lambda e, qn_=qn_: e.activation(yat[0:qn_, :], on[0:qn_, :, :].rearrange("p h d -> p (h d)"),
                                                    AF.Identity, scale=rts[0:qn_, 0:1]), reads=[on, rts], writes=[yat])
        for c in range(2):
            pbT = self.bank()
            P.op("pe", lambda e, pbT=pbT, c=c, qn_=qn_: e.matmul(
                pbT[:, 0:qn_], yat[0:qn_, c * 128:(c + 1) * 128], self.cb("ident", rows=slice(0, qn_), n=qn_),
                start=True, stop=True), reads=[yat, self.cstb], writes=[pbT])
            P.op("dve", lambda e, pbT=pbT, c=c, qs=qs, qn_=qn_: e.tensor_scalar(
                self.mixT[:, c, qs:qs + qn_], pbT[:, 0:qn_], self.V(l, "gout", c), None, op0=ALU.mult),
                reads=[pbT, vecl], writes=[self.mixT])


KB.mla = _mla


def _scan_init(self):
    P = self.P
    TT = self.cfg.tt
    self.rstm = self.B("rstm", [128, TT], F32)
    P.op("pool", lambda e: e.memset(self.rstm[:, :], 1.0), writes=[self.rstm])
    P.op("pool", lambda e: e.memset(self.rstm.ap().rearrange("p (b l) -> p b l", l=64)[:, :, 0:1], 0.0),
         writes=[self.rstm])
    self.S32 = {m: [self.B(f"S32{m}{l}", [128, 2, 64], F32) for l in range(DEPTH)] for m in "BCD"}
    self.histC = [self.B(f"histC{l}", [128, 6, 3], F32) for l in range(DEPTH)]
    self.histD = [self.B(f"histD{l}", [128, 6, 3], F32) for l in range(DEPTH)]
    self.shiftB = [self.B(f"shiftB{l}", [128, 8], F32) for l in range(DEPTH)]
    for nm in ("SbB", "SbC", "SbD"):
        b = self.B(nm, [128, 2, 2, 64], BF16)
        P.op("pool", lambda e, b=b: e.memset(b.ap(), 0.0), writes=[b])
    for nm in ("opM1", "opM2"):
        b = self.B(nm, [128, 2, 2, TT], BF16)
        P.op("pool", lambda e, b=b: e.memset(b.ap(), 0.0), writes=[b])
    for l in range(DEPTH):
        for b in [self.S32[m][l] for m in "BCD"] + [self.histC[l], self.histD[l], self.shiftB[l]]:
            P.op("pool", lambda e, b=b: e.memset(b.ap(), 0.0), writes=[b])


KB.scan_init = _scan_init


def _rows_prep(self, g8, T, pfx):
    P = self.P
    TT = self.cfg.tt
    blks = blocks_of(T)
    L = blks[0][1]
    NB = len(blks)
    gc8 = self.B(pfx + "gc8", [8, TT], F32)
    egc8 = self.B(pfx + "egc8", [8, TT], F32)
    edl8 = self.B(pfx + "edl8", [8, TT], F32)
    A8 = self.B(pfx + "A8", [8, TT], F32)
    NG8 = self.B(pfx + "NG8", [8, TT], F32)
    B8 = self.B(pfx + "B8", [8, 4, TT], F32)
    P.op("dve", lambda e: e.tensor_tensor_scan(gc8[:, 0:T], self.rstm[0:8, 0:T], g8[0:8, 0:T], 0.0, ALU.mult, ALU.add),
         reads=[self.rstm, g8], writes=[gc8])
    P.op("act", lambda e: e.activation(egc8[:, 0:T], gc8[:, 0:T], AF.Exp), reads=[gc8], writes=[egc8])
    gv = gc8[:, 0:T].rearrange("p (b l) -> p b l", l=L)
    P.op("dve", lambda e: e.tensor_tensor(edl8[:, 0:T].rearrange("p (b l) -> p b l", l=L),
                                          gv[:, :, L - 1:L].to_broadcast([8, NB, L]), gv, ALU.subtract),
         reads=[gc8], writes=[edl8])
    P.op("act", lambda e: e.activation(edl8[:, 0:T], edl8[:, 0:T], AF.Exp), reads=[edl8], writes=[edl8])
    c8 = slice(0, 8)
    P.op("dve", lambda e: e.tensor_scalar(A8[:, 0:T], gc8[:, 0:T], self.c32("m1", c8), self.c32("m2", c8),
                                          op0=ALU.mult, op1=ALU.add), reads=[gc8, self.cst], writes=[A8])
    P.op("dve", lambda e: e.tensor_scalar(NG8[:, 0:T], gc8[:, 0:T], self.c32("nm2", c8), self.c32("m1", c8),
                                          op0=ALU.mult, op1=ALU.add), reads=[gc8, self.cst], writes=[NG8])
    for h in range(4):
        P.op("dve", lambda e, h=h: e.tensor_scalar(B8[:, h, 0:T], NG8[:, 0:T], self.c32("selh", c8, 1, h), None,
                                                   op0=ALU.mult), reads=[NG8, self.cst], writes=[B8])
    return dict(gc8=gc8, egc8=egc8, edl8=edl8, A8=A8, B8=B8)


KB.rows_prep = _rows_prep


def _bcast_rows(self, rows8, T, c):
    P = self.P
    pb = self.bank()
    o = c * 128
    P.op("pe", lambda e, pb=pb, o=o: e.matmul(pb[:, 0:T], self.cst[0:8, o:o + 128], rows8[0:8, 0:T], start=True, stop=True),
         reads=[self.cst, rows8], writes=[pb])
    return pb


KB.bcast_rows = _bcast_rows


def _decay_exp(self, pb, col0, rp, b0, L, mask, transposed, out_ap, out_buf):
    P = self.P
    A8, B8 = rp["A8"], rp["B8"]
    mo = COFF[mask]
    for h in range(4):
        oc = col0 + h * L
        if transposed:
            P.op("pe", lambda e, h=h, oc=oc: e.matmul(pb[0:L, oc:oc + L], B8[0:8, h, b0:b0 + L], A8[0:8, b0:b0 + L],
                                                      start=True, stop=False, skip_group_check=True),
                 reads=[A8, B8], writes=[pb])
        else:
            P.op("pe", lambda e, h=h, oc=oc: e.matmul(pb[0:L, oc:oc + L], A8[0:8, b0:b0 + L], B8[0:8, h, b0:b0 + L],
                                                      start=True, stop=False, skip_group_check=True),
                 reads=[A8, B8], writes=[pb])
        P.op("pe", lambda e, oc=oc: e.matmul(pb[0:L, oc:oc + L], self.cb("ident", slice(0, L), L),
                                             self.cstb[0:L, mo:mo + L], start=False, stop=True, skip_group_check=True),
             reads=[self.cstb], writes=[pb])
    P.op("act", lambda e: e.activation(out_ap, pb[0:L, col0:col0 + 4 * L].rearrange("p (h l) -> p h l", h=4), AF.Exp),
         reads=[pb], writes=[out_buf])


KB.decay_exp = _decay_exp


def _to_tm(self, src, T, b0, L, dst_ap, dst_buf):
    P = self.P
    pb = self.bank()
    for c in range(2):
        P.op("pe", lambda e, c=c, pb=pb: e.matmul(pb[0:L, c * 128:(c + 1) * 128], src[:, c, b0:b0 + L],
                                                  self.cb("ident", n=128), start=True, stop=True, skip_group_check=True),
             reads=[src, self.cstb], writes=[pb])
    self.evac(dst_ap, pb[0:L, 0:256].rearrange("p (h d) -> p h d", h=4), [pb], [dst_buf])


KB.to_tm = _to_tm


def _conv4(self, l, T, nch, wname, bname, out_buf):
    P = self.P
    TT = self.cfg.tt
    pj = self.pj
    vecl = self.vec[l]
    for c in range(nch):
        acc = self.B(f"cacc{c % 2}", [128, TT], F32)
        w = lambda i, c=c: self.V(l, wname, c * 4 + i)
        if bname is not None:
            P.op("act", lambda e, c=c, acc=acc, w=w: e.activation(acc[:, 0:T], pj[:, c, 0:T], AF.Identity,
                                                                  bias=self.V(l, bname, c), scale=w(0)),
                 reads=[pj, vecl], writes=[acc])
        else:
            P.op("act", lambda e, c=c, acc=acc, w=w: e.activation(acc[:, 0:T], pj[:, c, 0:T], AF.Identity, scale=w(0)),
                 reads=[pj, vecl], writes=[acc])
        for i in (1, 2, 3):
            P.op("dve", lambda e, c=c, i=i, acc=acc, w=w: e.scalar_tensor_tensor(
                acc[:, 0:T], pj[:, c, i:i + T], w(i), acc[:, 0:T], op0=ALU.mult, op1=ALU.add),
                reads=[pj, acc, vecl], writes=[acc])
        P.op("act", lambda e, c=c, acc=acc: e.activation(out_buf[:, c, 0:T], acc[:, 0:T], AF.Silu),
             reads=[acc], writes=[out_buf])


KB.conv4 = _conv4


def _softplus_rows(self, dst, src_ap, src_buf, bias_ap, bias_buf, T):
    P = self.P
    P.op("act", lambda e: e.activation(dst[0:8, 0:T], src_ap, AF.Exp, bias=bias_ap), reads=[src_buf, bias_buf], writes=[dst])
    P.op("act", lambda e: e.activation(dst[0:8, 0:T], dst[0:8, 0:T], AF.Ln, bias=self.epsc[0:8, 2:3]),
         reads=[dst, self.epsc], writes=[dst])


KB.softplus_rows = _softplus_rows


def _ssd(self, sqn, l, t0, T):
    P = self.P
    TT = self.cfg.tt
    pj, pj2, vecl = self.pj, self.pj2, self.vec[l]
    blks = blocks_of(T)
    L = blks[0][1]
    NB = len(blks)
    hist = self.histC[l]
    S32 = self.S32["C"][l]
    xbcs = self.B("xbcs", [128, 8, TT], F32)
    siluz = self.B("siluz", [128, 2, TT], F32)
    dt8 = self.B("dt8", [8, TT], F32)
    adt8 = self.B("adt8", [8, TT], F32)
    xdt = self.B("opA", [128, 2, TT], BF16)
    Cdec = self.B("opB", [128, 2, TT], BF16)
    Bdec = self.B("opC", [128, 2, TT], BF16)
    bm_b = self.B("opD", [128, 2, TT], BF16)
    cm_b = self.B("opM1", [128, 2, 2, TT], BF16)
    eal = self.B("ealC", [128, 2, 8], F32)
    Sb = self.B("SbC", [128, 2, 2, 64], BF16)
    xdt_tm = self.B("tmA", [64, 4, 64], BF16)
    Bdec_tm = self.B("tmB", [64, 4, 64], BF16)
    E1 = self.B("E1", [64, 4, 64], F32)
    GT = self.B("GT", [64, 4, 64], BF16)
    ytm = self.B("otm", [64, TT // 64, 4, 64], BF16)
    y2 = self.B("y2C", [128, 2, TT], F32)
    sqy = self.B("sqyC", [128, 2, TT], BF16)
    P.op("pool", lambda e: e.tensor_copy(pj[:, 0:6, 0:3], hist[:, :, :]), reads=[hist], writes=[pj])
    P.op("pool", lambda e: e.tensor_copy(hist[:, :, :], pj[:, 0:6, T:T + 3]), reads=[pj], writes=[hist])
    self.conv4(l, T, 6, "ccw", "ccb", xbcs)
    P.op("act", lambda e: e.activation(siluz[:, :, 0:T], pj2[:, 0:2, 0:T], AF.Silu), reads=[pj2], writes=[siluz])
    self.softplus_rows(dt8, pj2[0:8, 2, 0:T], pj2, self.V(l, "cdtb", rows=slice(0, 8)), vecl, T)
    P.op("dve", lambda e: e.tensor_scalar(adt8[:, 0:T], dt8[:, 0:T], self.dv[l][0:8, 0:1], None, op0=ALU.mult),
         reads=[dt8, self.dv[l]], writes=[adt8])
    rp = self.rows_prep(adt8, T, "R")
    for c in range(2):
        pbd = self.bcast_rows(dt8, T, c)
        P.op("dve", lambda e, c=c, pbd=pbd: e.tensor_tensor(xdt[:, c, 0:T], xbcs[:, c, 0:T], pbd[:, 0:T], ALU.mult),
             reads=[xbcs, pbd], writes=[xdt])
        pbe = self.bcast_rows(rp["egc8"], T, c)
        P.op("dve", lambda e, c=c, pbe=pbe: e.tensor_tensor(Cdec[:, c, 0:T], xbcs[:, 4 + c, 0:T], pbe[:, 0:T], ALU.mult),
             reads=[xbcs, pbe], writes=[Cdec])
        P.op("dve", lambda e, c=c, pbe=pbe: e.tensor_copy(
            eal[:, c, 0:NB], pbe[:, 0:T].rearrange("p (b l) -> p b l", l=L)[:, :, L - 1]), reads=[pbe], writes=[eal])
        pbl = self.bcast_rows(rp["edl8"], T, c)
        P.op("dve", lambda e, c=c, pbl=pbl: e.tensor_tensor(Bdec[:, c, 0:T], xbcs[:, 2 + c, 0:T], pbl[:, 0:T], ALU.mult),
             reads=[xbcs, pbl], writes=[Bdec])
    P.op("act", lambda e: e.activation(bm_b[:, :, 0:T], xbcs[:, 2:4, 0:T], AF.Identity), reads=[xbcs], writes=[bm_b])
    self.mask_copy("act", cm_b, xbcs[0:64, 4:6, 0:T], xbcs[64:128, 4:6, 0:T], T, [xbcs])
    self.sb_copy(Sb, S32)
    for bi, (b0, _) in enumerate(blks):
        self.to_tm(xdt, T, b0, L, xdt_tm[0:L, :, :], xdt_tm)
        self.to_tm(Bdec, T, b0, L, Bdec_tm[0:L, :, :], Bdec_tm)
        pr = self.bankb()
        for h in range(4):
            c, e_ = divmod(h, 2)
            hp = slice(64 * e_, 64 * e_ + 64)
            self.MM(pr[0:L, h * L:(h + 1) * L], bm_b[:, c, b0:b0 + L], cm_b[:, c, e_, b0:b0 + L], True, True, [bm_b, cm_b], [pr])
        if l == 0 and t0 > 0 and "Rpre" in self.cfg.dbg:
            Rp = self.B(f"Rpre{bi}", [64, 256], F32)
            P.op("dve", lambda e, pr=pr, Rp=Rp: e.tensor_copy(Rp[:, :], pr[0:64, 0:256]), reads=[pr], writes=[Rp])
            self.dbg("Rpre", Rp, Rp[:, :], [64, 256])
        self.decay_exp(pr, 256, rp, b0, L, "mn_inclT", True, E1[0:L, :, 0:L], E1)
        P.op("dve", lambda e, pr=pr: e.tensor_tensor(GT[0:L, :, 0:L], pr[0:L, 0:4 * L].rearrange("p (h l) -> p h l", h=4),
                                                     E1[0:L, :, 0:L], ALU.mult), reads=[pr, E1], writes=[GT])
        if l == 0 and t0 > 0 and bi == 0 and "Rraw" in self.cfg.dbg:
            Rr = self.B("Rraw", [64, 256], F32)
            P.op("dve", lambda e, pr=pr: e.tensor_copy(Rr[:, :], pr[0:64, 0:256]), reads=[pr], writes=[Rr])
            self.dbg("Rraw", Rr, Rr[:, :], [64, 256])
            self.dbg("bm_b", bm_b, bm_b[:, :, 0:64], [128, 2, 64])
            self.dbg("cm_b", cm_b, cm_b[:, :, 0:64], [128, 2, 64])
        if l == 0 and t0 > 0 and bi == 0:
            self.dbg("E1", E1, E1[:, :, :], [64, 4, 64])
            self.dbg("GT", GT, GT[:, :, :], [64, 4, 64])
            self.dbg("xdt_tm", xdt_tm, xdt_tm[:, :, :], [64, 4, 64])
            self.dbg("gc8", rp["gc8"], rp["gc8"][:, :], [8, TT])
            self.dbg("A8", rp["A8"], rp["A8"][:, :], [8, TT])
            self.dbg("B8", rp["B8"], rp["B8"][:, :, :], [8, 4, TT])
        py = self.bankb()
        for h in range(4):
            c, e_ = divmod(h, 2)
            hp = slice(64 * e_, 64 * e_ + 64)
            P.op("pe", lambda e, h=h, py=py: e.matmul(py[0:L, h * 64:(h + 1) * 64], GT[0:L, h, 0:L], xdt_tm[0:L, h, :],
                                                      start=True, stop=False, skip_group_check=True),
                 reads=[GT, xdt_tm], writes=[py])
            self.MM(py[0:L, h * 64:(h + 1) * 64], Cdec[:, c, b0:b0 + L], Sb[:, c, e_, :], False, True, [Cdec, Sb], [py])
        self.evac(ytm[0:L, bi, :, :], py[0:L, 0:256].rearrange("p (h d) -> p h d", h=4), [py], [ytm])
        pS = self.bankb()
        for h in range(4):
            c, e_ = divmod(h, 2)
            hp = slice(64 * e_, 64 * e_ + 64)
            P.op("pe", lambda e, h=h, c=c, hp=hp, pS=pS: e.matmul(
                pS[hp, c * 64:(c + 1) * 64], Bdec_tm[0:L, h, :], xdt_tm[0:L, h, :],
                start=True, stop=True, skip_group_check=True), reads=[Bdec_tm, xdt_tm], writes=[pS])
        for c in range(2):
            P.op("dve", lambda e, c=c, bi=bi, pS=pS: e.scalar_tensor_tensor(
                S32[:, c, :], S32[:, c, :], eal[:, c, bi:bi + 1], pS[:, c * 64:(c + 1) * 64], op0=ALU.mult, op1=ALU.add),
                reads=[S32, eal, pS], writes=[S32])
        self.sb_copy(Sb, S32)
    pys = [self.bankb(), self.bankb()]
    for bi, (b0, _) in enumerate(blks):
        for c in range(2):
            P.op("pe", lambda e, c=c, bi=bi, b0=b0: e.matmul(
                pys[c][:, b0:b0 + L], ytm[0:L, bi, 2 * c:2 * c + 2, :].rearrange("p h d -> p (h d)"),
                self.cb("ident", slice(0, L), L), start=(bi == 0), stop=True, skip_group_check=True),
                reads=[ytm, self.cstb], writes=[pys[c]])
    for c in range(2):
        P.op("dve", lambda e, c=c: e.scalar_tensor_tensor(y2[:, c, 0:T], xbcs[:, c, 0:T], self.V(l, "cd", c), pys[c][:, 0:T],
                                                          op0=ALU.mult, op1=ALU.add), reads=[xbcs, vecl, pys[c]], writes=[y2])
    P.op("dve", lambda e: e.tensor_tensor(y2[:, :, 0:T], y2[:, :, 0:T], siluz[:, :, 0:T], ALU.mult),
         reads=[y2, siluz], writes=[y2])
    P.op("act", lambda e: e.activation(sqy[:, :, 0:T], y2[:, :, 0:T], AF.Square), reads=[y2], writes=[sqy])
    pn = self.bank()
    for c in range(2):
        P.op("pe", lambda e, c=c: e.matmul(pn[:, 0:T], self.cb("ones", n=128), sqy[:, c, 0:T], start=(c == 0), stop=(c == 1)),
             reads=[self.cstb, sqy], writes=[pn])
    rtc = self.B("rtC", [128, TT], F32)
    P.op("act", lambda e: e.activation(rtc[:, 0:T], pn[:, 0:T], AF.Sqrt, bias=self.epsc[:, 0:1], scale=1.0 / 256.0),
         reads=[pn, self.epsc], writes=[rtc])
    P.op("dve", lambda e: e.reciprocal(rtc[:, 0:T], rtc[:, 0:T]), reads=[rtc], writes=[rtc])
    for c in range(2):
        P.op("dve", lambda e, c=c: e.scalar_tensor_tensor(self.mixT[:, 4 + c, 0:T], y2[:, c, 0:T], self.V(l, "cgn", c),
                                                          rtc[:, 0:T], op0=ALU.mult, op1=ALU.mult),
             reads=[y2, vecl, rtc], writes=[self.mixT])


    if l == 0 and t0 > 0:
        self.dbg("mixC", self.mixT, self.mixT[:, 4:6, 0:T], [128, 2, T])
        self.dbg("ytm", ytm, ytm[:, 0:4, :, :], [64, 4, 4, 64])
    return


KB.ssd = _ssd


def _MM(self, out, lhsT, rhs, start, stop, reads, writes):
    self.P.op("pe", lambda e: e.matmul(out, lhsT, rhs, start=start, stop=stop, skip_group_check=True),
              reads=reads, writes=writes)


def _TT(self, eng, out, in0, in1, op, reads, writes):
    self.P.op(eng, lambda e: e.tensor_tensor(out, in0, in1, op), reads=reads, writes=writes)


def _STT(self, out, in0, scalar, in1, op0, op1, reads, writes):
    self.P.op("dve", lambda e: e.scalar_tensor_tensor(out, in0, scalar, in1, op0=op0, op1=op1), reads=reads, writes=writes)


def _TS(self, eng, out, in0, s1, s2, op0, op1, reads, writes):
    if s2 is None:
        self.P.op(eng, lambda e: e.tensor_scalar(out, in0, s1, None, op0=op0), reads=reads, writes=writes)
    else:
        self.P.op(eng, lambda e: e.tensor_scalar(out, in0, s1, s2, op0=op0, op1=op1), reads=reads, writes=writes)


def _ACT(self, out, in_, func, reads, writes, bias=None, scale=None):
    kw = {}
    if bias is not None:
        kw["bias"] = bias
    if scale is not None:
        kw["scale"] = scale
    self.P.op("act", lambda e: e.activation(out, in_, func, **kw), reads=reads, writes=writes)


KB.MM, KB.TT, KB.STT, KB.TS, KB.ACT = _MM, _TT, _STT, _TS, _ACT


def _tri_inverse(self, NM0, L):
    P = self.P
    nlev = {16: 3, 32: 4, 64: 5}[L]
    NMb = [self.B("NMb0", [64, 2, 4, 64], F32), self.B("NMb1", [64, 2, 4, 64], F32)]
    Pb = [self.B("Pb0", [64, 4, 64], F32), self.B("Pb1", [64, 4, 64], F32)]
    idb = self.cb("ident", slice(0, L), L).unsqueeze(1).to_broadcast([L, 4, L])
    self.TT("dve", Pb[0][0:L, :, 0:L], NM0[0:L, 1, :, 0:L], idb, ALU.add, [NM0, self.cstb], [Pb[0]])
    cur = NM0
    pcur = Pb[0]
    for k in range(1, nlev + 1):
        nxt = NMb[k % 2]
        pb = self.bankb()
        for h in range(4):
            self.MM(pb[0:L, h * L:(h + 1) * L], cur[0:L, 1, h, 0:L], cur[0:L, 0, h, 0:L], True, True, [cur], [pb])
            self.MM(pb[0:L, 256 + h * L:256 + (h + 1) * L], cur[0:L, 0, h, 0:L], cur[0:L, 1, h, 0:L], True, True, [cur], [pb])
        src = pb[0:L, :].rearrange("p (t x) -> p t x", t=2)[:, :, 0:4 * L].rearrange("p t (h l) -> p t h l", h=4)
        self.evac(nxt[0:L, :, :, 0:L], src, [pb], [nxt])
        pp = self.bankb()
        for h in range(4):
            self.MM(pp[0:L, h * L:(h + 1) * L], nxt[0:L, 0, h, 0:L], pcur[0:L, h, 0:L], True, True, [nxt, pcur], [pp])
        pnx = Pb[k % 2]
        self.TT("dve", pnx[0:L, :, 0:L], pp[0:L, 0:4 * L].rearrange("p (h l) -> p h l", h=4), pcur[0:L, :, 0:L],
                ALU.add, [pp, pcur], [pnx])
        cur, pcur = nxt, pnx
    PTb = self.B("PTb", [64, 4, 64], BF16)
    self.ACT(PTb[0:L, :, 0:L], pcur[0:L, :, 0:L], AF.Identity, [pcur], [PTb])
    return PTb


KB.tri_inverse = _tri_inverse


def _sb_copy(self, Sb, S32):
    self.ACT(Sb[0:64, :, 0, :], S32[0:64, :, :], AF.Identity, [S32], [Sb])
    self.ACT(Sb[64:128, :, 1, :], S32[64:128, :, :], AF.Identity, [S32], [Sb])


def _mask_copy(self, eng, dst, src_ap_lo, src_ap_hi, T, rbufs):
    if eng == "act":
        self.ACT(dst[0:64, :, 0, 0:T], src_ap_lo, AF.Identity, rbufs, [dst])
        self.ACT(dst[64:128, :, 1, 0:T], src_ap_hi, AF.Identity, rbufs, [dst])


KB.sb_copy, KB.mask_copy = _sb_copy, _mask_copy


def _gdn(self, sqn, l, t0, T):
    P = self.P
    TT_ = self.cfg.tt
    pj, pj2, vecl = self.pj, self.pj2, self.vec[l]
    blks = blocks_of(T)
    L = blks[0][1]
    NB = len(blks)
    hist = self.histD[l]
    S32 = self.S32["D"][l]
    qkvs = self.B("xbcs", [128, 8, TT_], F32)
    siluz = self.B("siluz", [128, 2, TT_], F32)
    sq4 = self.B("sq4D", [128, 4, TT_], BF16)
    rn = self.B("rnD", [128, TT_], F32)
    khat32 = self.B("khat32", [128, 2, TT_], F32)
    kb32 = self.B("kb32", [128, 2, TT_], F32)
    qhat_b = self.B("opA", [128, 2, TT_], BF16)
    khat_b = self.B("opM1", [128, 2, 2, TT_], BF16)
    kb_b = self.B("opC", [128, 2, TT_], BF16)
    vb = self.B("opD", [128, 2, TT_], BF16)
    nkbg = self.B("opE", [128, 2, TT_], BF16)
    qdec = self.B("opF", [128, 2, TT_], BF16)
    kdec = self.B("opG", [128, 2, TT_], BF16)
    beta8 = self.B("beta8", [8, TT_], F32)
    sp8 = self.B("dt8", [8, TT_], F32)
    g8 = self.B("adt8", [8, TT_], F32)
    eal = self.B("ealD", [128, 2, 8], F32)
    Sb = self.B("SbD", [128, 2, 2, 64], BF16)
    kdec_tm = self.B("tmA", [64, 4, 64], BF16)
    Ebuf = self.B("E1", [64, 4, 64], F32)
    NM0 = self.B("NM0", [64, 2, 4, 64], F32)
    AqkT = self.B("GT", [64, 4, 64], BF16)
    rhs_s = self.B("rhs_s", [64, 4, 64], BF16)
    vnew_s = self.B("vnew_s", [64, 4, 64], BF16)
    otm = self.B("otm32", [64, TT_ // 64, 4, 64], F32)
    on_b = self.B("otm", [64, TT_ // 64, 4, 64], BF16)
    sqo_buf = self.acc[0]
    sqo = sqo_buf.ap()[0:64, :, :].rearrange("p a t -> p (a t)").rearrange("p (b h d) -> p b h d", h=4, d=64)
    ss = self.B("ssD", [64, 32], F32)
    c8 = slice(0, 8)
    P.op("pool", lambda e: e.tensor_copy(pj[:, 0:6, 0:3], hist[:, :, :]), reads=[hist], writes=[pj])
    P.op("pool", lambda e: e.tensor_copy(hist[:, :, :], pj[:, 0:6, T:T + 3]), reads=[pj], writes=[hist])
    self.conv4(l, T, 6, "dcw", None, qkvs)
    self.ACT(siluz[:, :, 0:T], pj2[:, 0:2, 0:T], AF.Silu, [pj2], [siluz])
    self.ACT(sq4[:, :, 0:T], qkvs[:, 0:4, 0:T], AF.Square, [qkvs], [sq4])
    for c in range(4):
        pb = self.bank()
        self.MM(pb[:, 0:T], self.cb("blk1", n=128), sq4[:, c, 0:T], True, True, [self.cstb, sq4], [pb])
        self.ACT(rn[:, 0:T], pb[:, 0:T], AF.Sqrt, [pb, self.epsc], [rn], bias=self.epsc[:, 0:1])
        P.op("dve", lambda e: e.reciprocal(rn[:, 0:T], rn[:, 0:T]), reads=[rn], writes=[rn])
        if c < 2:
            self.STT(qhat_b[:, c, 0:T], qkvs[:, c, 0:T], 0.125, rn[:, 0:T], ALU.mult, ALU.mult, [qkvs, rn], [qhat_b])
        else:
            self.TT("dve", khat32[:, c - 2, 0:T], qkvs[:, c, 0:T], rn[:, 0:T], ALU.mult, [qkvs, rn], [khat32])
    self.mask_copy("act", khat_b, khat32[0:64, :, 0:T], khat32[64:128, :, 0:T], T, [khat32])
    self.ACT(beta8[0:8, 0:T], pj2[0:8, 2, 0:T], AF.Sigmoid, [pj2], [beta8])
    self.softplus_rows(sp8, pj2[0:8, 3, 0:T], pj2, self.V(l, "ddtb", rows=c8), vecl, T)
    self.TS("dve", g8[0:8, 0:T], sp8[0:8, 0:T], self.dv[l][0:8, 1:2], None, ALU.mult, None, [sp8, self.dv[l]], [g8])
    rp = self.rows_prep(g8, T, "R")
    for c in range(2):
        pbb = self.bcast_rows(beta8, T, c)
        self.TT("dve", kb32[:, c, 0:T], khat32[:, c, 0:T], pbb[:, 0:T], ALU.mult, [khat32, pbb], [kb32])
        self.TT("dve", vb[:, c, 0:T], qkvs[:, 4 + c, 0:T], pbb[:, 0:T], ALU.mult, [qkvs, pbb], [vb])
        pbe = self.bcast_rows(rp["egc8"], T, c)
        self.STT(nkbg[:, c, 0:T], kb32[:, c, 0:T], -1.0, pbe[:, 0:T], ALU.mult, ALU.mult, [kb32, pbe], [nkbg])
        self.TT("dve", qdec[:, c, 0:T], qhat_b[:, c, 0:T], pbe[:, 0:T], ALU.mult, [qhat_b, pbe], [qdec])
        P.op("dve", lambda e, c=c, pbe=pbe: e.tensor_copy(
            eal[:, c, 0:NB], pbe[:, 0:T].rearrange("p (b l) -> p b l", l=L)[:, :, L - 1]), reads=[pbe], writes=[eal])
        pbl = self.bcast_rows(rp["edl8"], T, c)
        self.TT("dve", kdec[:, c, 0:T], khat32[:, c, 0:T], pbl[:, 0:T], ALU.mult, [khat32, pbl], [kdec])
    self.ACT(kb_b[:, :, 0:T], kb32[:, :, 0:T], AF.Identity, [kb32], [kb_b])
    self.sb_copy(Sb, S32)
    for bi, (b0, _) in enumerate(blks):
        bs = slice(b0, b0 + L)
        self.to_tm(kdec, T, b0, L, kdec_tm[0:L, :, :], kdec_tm)
        HP = [(h // 2, h % 2) for h in range(4)]
        pr = self.bankb()
        for h, (c, hp) in enumerate(HP):
            self.MM(pr[0:L, h * L:(h + 1) * L], kb_b[:, c, bs], khat_b[:, c, hp, bs], True, True, [kb_b, khat_b], [pr])
        self.decay_exp(pr, 256, rp, b0, L, "mn_str", False, Ebuf[0:L, :, 0:L], Ebuf)
        self.STT(NM0[0:L, 0, :, 0:L], pr[0:L, 0:4 * L].rearrange("p (h l) -> p h l", h=4), -1.0, Ebuf[0:L, :, 0:L],
                 ALU.mult, ALU.mult, [pr, Ebuf], [NM0])
        pr = self.bankb()
        for h, (c, hp) in enumerate(HP):
            self.MM(pr[0:L, h * L:(h + 1) * L], khat_b[:, c, hp, bs], kb_b[:, c, bs], True, True, [kb_b, khat_b], [pr])
        self.decay_exp(pr, 256, rp, b0, L, "mn_strT", True, Ebuf[0:L, :, 0:L], Ebuf)
        self.STT(NM0[0:L, 1, :, 0:L], pr[0:L, 0:4 * L].rearrange("p (h l) -> p h l", h=4), -1.0, Ebuf[0:L, :, 0:L],
                 ALU.mult, ALU.mult, [pr, Ebuf], [NM0])
        pr = self.bankb()
        for h, (c, hp) in enumerate(HP):
            self.MM(pr[0:L, h * L:(h + 1) * L], khat_b[:, c, hp, bs], qhat_b[:, c, bs], True, True, [qhat_b, khat_b], [pr])
        self.decay_exp(pr, 256, rp, b0, L, "mn_inclT", True, Ebuf[0:L, :, 0:L], Ebuf)
        self.TT("dve", AqkT[0:L, :, 0:L], pr[0:L, 0:4 * L].rearrange("p (h l) -> p h l", h=4), Ebuf[0:L, :, 0:L],
                ALU.mult, [pr, Ebuf], [AqkT])
        PT = self.tri_inverse(NM0, L)
        pq = self.bankb()
        for h, (c, hp) in enumerate(HP):
            e_ = hp
            self.MM(pq[0:L, h * 64:(h + 1) * 64], vb[:, c, bs], self.cstb[:, COFF["ident"] + 64 * e_:COFF["ident"] + 64 * e_ + 64],
                    True, False, [vb, self.cstb], [pq])
            self.MM(pq[0:L, h * 64:(h + 1) * 64], nkbg[:, c, bs], Sb[:, c, e_, :], False, True, [nkbg, Sb], [pq])
        self.evac(rhs_s[0:L, :, :], pq[0:L, 0:256].rearrange("p (h d) -> p h d", h=4), [pq], [rhs_s])
        pv = self.bankb()
        for h in range(4):
            self.MM(pv[0:L, h * 64:(h + 1) * 64], PT[0:L, h, 0:L], rhs_s[0:L, h, :], True, True, [PT, rhs_s], [pv])
        self.evac(vnew_s[0:L, :, :], pv[0:L, 0:256].rearrange("p (h d) -> p h d", h=4), [pv], [vnew_s])
        po = self.bankb()
        for h, (c, hp) in enumerate(HP):
            self.MM(po[0:L, h * 64:(h + 1) * 64], qdec[:, c, bs], Sb[:, c, hp, :], True, False, [qdec, Sb], [po])
            self.MM(po[0:L, h * 64:(h + 1) * 64], AqkT[0:L, h, 0:L], vnew_s[0:L, h, :], False, True, [AqkT, vnew_s], [po])
        self.evac(otm[0:L, bi, :, :], po[0:L, 0:256].rearrange("p (h d) -> p h d", h=4), [po], [otm])
        pS = self.bankb()
        for h, (c, hp) in enumerate(HP):
            self.MM(pS[64 * hp:64 * hp + 64, c * 64:(c + 1) * 64], kdec_tm[0:L, h, :], vnew_s[0:L, h, :], True, True, [kdec_tm, vnew_s], [pS])
        for c in range(2):
            self.STT(S32[:, c, :], S32[:, c, :], eal[:, c, bi:bi + 1], pS[:, c * 64:(c + 1) * 64], ALU.mult, ALU.add,
                     [S32, eal, pS], [S32])
        self.sb_copy(Sb, S32)
    ov = otm[0:L, 0:NB, :, :]
    self.TT("dve", sqo[0:L, 0:NB, :, :], ov, ov, ALU.mult, [otm], [sqo_buf])
    P.op("dve", lambda e: e.reduce_sum(ss[0:L, 0:NB * 4], sqo[0:L, 0:NB, :, :].rearrange("p b h d -> p (b h) d"), AX.X),
         reads=[sqo_buf], writes=[ss])
    self.ACT(ss[0:L, 0:NB * 4], ss[0:L, 0:NB * 4], AF.Sqrt, [ss, self.epsc], [ss], bias=self.epsc[0:L, 0:1], scale=1.0 / 64.0)
    P.op("dve", lambda e: e.reciprocal(ss[0:L, 0:NB * 4], ss[0:L, 0:NB * 4]), reads=[ss], writes=[ss])
    self.TT("dve", on_b[0:L, 0:NB, :, :].rearrange("p b h d -> p (b h) d"),
            otm[0:L, 0:NB, :, :].rearrange("p b h d -> p (b h) d"),
            ss[0:L, 0:NB * 4].unsqueeze(2).to_broadcast([L, NB * 4, 64]), ALU.mult, [otm, ss], [on_b])
    pys = [self.bankb(), self.bankb()]
    for bi, (b0, _) in enumerate(blks):
        for c in range(2):
            self.MM(pys[c][:, b0:b0 + L], on_b[0:L, bi, 2 * c:2 * c + 2, :].rearrange("p h d -> p (h d)"),
                    self.cb("ident", slice(0, L), L), True, True, [on_b, self.cstb], [pys[c]])
    for c in range(2):
        self.STT(self.mixT[:, 6 + c, 0:T], pys[c][:, 0:T], self.V(l, "dgn", c), siluz[:, c, 0:T], ALU.mult, ALU.mult,
                 [pys[c], vecl, siluz], [self.mixT])


KB.gdn = _gdn


def _rwkv(self, sqn, l, t0, T):
    P = self.P
    TT_ = self.cfg.tt
    pj, vecl, wsm = self.pj, self.vec[l], self.wsm[l]
    blks = blocks_of(T)
    L = blks[0][1]
    NB = len(blks)
    S32 = self.S32["B"][l]
    shift = self.shiftB[l]
    F = lambda n: self.B(n, [128, 2, TT_], F32)
    H = lambda n: self.B(n, [128, 2, TT_], BF16)
    xm = self.B("xbcs", [128, 8, TT_], F32)
    lw, G, ex, ag, gT, kkr, kmod, bvec, bon = F("y2C"), F("khat32"), F("kb32"), F("agB"), F("gTB"), F("kkrB"), \
        F("kmodB"), F("bvecB"), F("bonB")
    misc = self.B("sq4D", [128, 4, TT_], BF16)
    sqk = self.B("sqyC", [128, 2, TT_], BF16)
    rn = self.B("rnD", [128, TT_], F32)
    At, Rt, Pe, Ke, v_b = H("opA"), H("opD"), H("opE"), H("opF"), H("opG")
    Pt = self.B("opM1", [128, 2, 2, TT_], BF16)
    Kt = self.B("opM2", [128, 2, 2, TT_], BF16)
    V_tm = self.B("tmA", [64, 4, 64], BF16)
    Pe_tm = self.B("tmB", [64, 4, 64], BF16)
    Ke_tm = self.B("tmC", [64, 4, 64], BF16)
    NM0 = self.B("NM0", [64, 2, 4, 64], F32)
    AakT = self.B("GT", [64, 4, 64], BF16)
    ArpT = self.B("ArpT", [64, 4, 64], BF16)
    ArkT = self.B("ArkT", [64, 4, 64], BF16)
    rhs_s = self.B("rhs_s", [64, 4, 64], BF16)
    U_s = self.B("vnew_s", [64, 4, 64], BF16)
    otm = self.B("otm32", [64, TT_ // 64, 4, 64], F32)
    on_b = self.B("otm", [64, TT_ // 64, 4, 64], BF16)
    sqo_buf = self.acc[0]
    sqo = sqo_buf.ap()[0:64, :, :].rearrange("p a t -> p (a t)").rearrange("p (b h d) -> p b h d", h=4, d=64)
    ss = self.B("ssD", [64, 32], F32)
    s1 = self.B("ssB", [64, 32], F32)
    WL = self.B("ealD", [128, 2, 8], F32)
    Sb = self.B("SbB", [128, 2, 2, 64], BF16)
    HP = [(h // 2, h % 2) for h in range(4)]
    P.op("pool", lambda e: e.tensor_copy(pj[:, 0:8, 0:1], shift[:, :].unsqueeze(2)), reads=[shift], writes=[pj])
    P.op("pool", lambda e: e.tensor_copy(shift[:, :].unsqueeze(2), pj[:, 0:8, T:T + 1]), reads=[pj], writes=[shift])
    self.TT("pool", xm[:, :, 0:T], pj[:, 0:8, 0:T], pj[:, 0:8, 1:T + 1], ALU.subtract, [pj], [xm])
    for c in range(8):
        self.STT(xm[:, c, 0:T], xm[:, c, 0:T], self.V(l, "mu", c), pj[:, c, 1:T + 1], ALU.mult, ALU.add, [xm, vecl, pj], [xm])
    self.ACT(misc[0:64, 0, 0:T], xm[0:64, 6, 0:T], AF.Tanh, [xm], [misc])
    self.ACT(misc[64:128, 1, 0:T], xm[64:128, 6, 0:T], AF.Identity, [xm], [misc])
    self.ACT(misc[:, 2, 0:T], xm[:, 7, 0:T], AF.Sigmoid, [xm], [misc])
    ow, og = SOFF["w2a2"], SOFF["g2"]
    for c in range(2):
        pb = self.bank()
        self.MM(pb[:, 0:T], wsm[0:64, ow + 128 * c:ow + 128 * c + 128], misc[0:64, 0, 0:T], True, True, [wsm, misc], [pb])
        self.ACT(lw[:, c, 0:T], pb[:, 0:T], AF.Sigmoid, [pb, vecl], [lw], bias=self.V(l, "w0", c))
        pb = self.bank()
        self.MM(pb[:, 0:T], wsm[64:128, ow + 128 * c:ow + 128 * c + 128], misc[64:128, 1, 0:T], True, True, [wsm, misc], [pb])
        self.ACT(ag[:, c, 0:T], pb[:, 0:T], AF.Sigmoid, [pb, vecl], [ag], bias=self.V(l, "a0", c))
        pb = self.bank()
        self.MM(pb[:, 0:T], wsm[:, og + 128 * c:og + 128 * c + 128], misc[:, 2, 0:T], True, True, [wsm, misc], [pb])
        self.evac(gT[:, c, 0:T], pb[:, 0:T], [pb], [gT])
    self.TS("pool", lw[:, :, 0:T], lw[:, :, 0:T], -math.exp(-0.5), None, ALU.mult, None, [lw], [lw])
    for c in range(2):
        P.op("dve", lambda e, c=c: e.tensor_tensor_scan(G[:, c, 0:T], self.rstm[:, 0:T], lw[:, c, 0:T], 0.0, ALU.mult, ALU.add),
             reads=[self.rstm, lw], writes=[G])
    self.TT("pool", lw[:, :, 0:T], G[:, :, 0:T], lw[:, :, 0:T], ALU.subtract, [G, lw], [lw])
    for c in range(2):
        self.TS("dve", kkr[:, c, 0:T], xm[:, 2 + c, 0:T], self.V(l, "kk", c), None, ALU.mult, None, [xm, vecl], [kkr])
    self.ACT(sqk[:, :, 0:T], kkr[:, :, 0:T], AF.Square, [kkr], [sqk])
    for c in range(2):
        pb = self.bank()
        self.MM(pb[:, 0:T], self.cb("blk1", n=128), sqk[:, c, 0:T], True, True, [self.cstb, sqk], [pb])
        self.ACT(rn[:, 0:T], pb[:, 0:T], AF.Sqrt, [pb, self.epsc], [rn], bias=self.epsc[:, 0:1])
        P.op("dve", lambda e: e.reciprocal(rn[:, 0:T], rn[:, 0:T]), reads=[rn], writes=[rn])
        self.TT("dve", kkr[:, c, 0:T], kkr[:, c, 0:T], rn[:, 0:T], ALU.mult, [kkr, rn], [kkr])
        self.TS("dve", kmod[:, c, 0:T], ag[:, c, 0:T], self.V(l, "ka", c), self.dv[l][:, 2 + c:3 + c], ALU.mult, ALU.add,
                [ag, vecl, self.dv[l]], [kmod])
        self.TT("dve", kmod[:, c, 0:T], kmod[:, c, 0:T], xm[:, 2 + c, 0:T], ALU.mult, [kmod, xm], [kmod])
    self.TT("pool", bvec[:, :, 0:T], kkr[:, :, 0:T], ag[:, :, 0:T], ALU.mult, [kkr, ag], [bvec])
    for c in range(2):
        self.STT(sqk[:, c, 0:T], xm[:, c, 0:T], self.V(l, "rk", c), kmod[:, c, 0:T], ALU.mult, ALU.mult, [xm, vecl, kmod], [sqk])
        pb = self.bank()
        self.MM(pb[:, 0:T], self.cb("blk1", n=128), sqk[:, c, 0:T], True, True, [self.cstb, sqk], [pb])
        self.TT("dve", bon[:, c, 0:T], pb[:, 0:T], xm[:, 4 + c, 0:T], ALU.mult, [pb, xm], [bon])
    self.ACT(v_b[:, :, 0:T], xm[:, 4:6, 0:T], AF.Identity, [xm], [v_b])
    self.ACT(ex[:, :, 0:T], lw[:, :, 0:T], AF.Exp, [lw], [ex])
    self.STT(At[:, :, 0:T], kkr[:, :, 0:T], -1.0, ex[:, :, 0:T], ALU.mult, ALU.mult, [kkr, ex], [At])
    self.ACT(ex[:, :, 0:T], G[:, :, 0:T], AF.Exp, [G], [ex], scale=-1.0)
    for e_ in range(2):
        hs = slice(64 * e_, 64 * e_ + 64)
        self.TT("dve", Pt[hs, :, e_, 0:T], bvec[hs, :, 0:T], ex[hs, :, 0:T], ALU.mult, [bvec, ex], [Pt])
        self.TT("pool", Kt[hs, :, e_, 0:T], kmod[hs, :, 0:T], ex[hs, :, 0:T], ALU.mult, [kmod, ex], [Kt])
    self.ACT(ex[:, :, 0:T], G[:, :, 0:T], AF.Exp, [G], [ex])
    self.TT("dve", Rt[:, :, 0:T], xm[:, 0:2, 0:T], ex[:, :, 0:T], ALU.mult, [xm, ex], [Rt])
    exv = ex.ap()[:, :, 0:T].rearrange("p c (b l) -> p c b l", l=L)
    P.op("dve", lambda e: e.tensor_copy(WL[:, :, 0:NB], exv[:, :, :, L - 1]), reads=[ex], writes=[WL])
    Gv = G.ap()[:, :, 0:T].rearrange("p c (b l) -> p c b l", l=L)
    for c in range(2):
        self.TT("dve", ex[:, c, 0:T].rearrange("p (b l) -> p b l", l=L),
                Gv[:, c, :, L - 1:L].to_broadcast([128, NB, L]), Gv[:, c, :, :], ALU.subtract, [G], [ex])
    self.ACT(ex[:, :, 0:T], ex[:, :, 0:T], AF.Exp, [ex], [ex])
    self.TT("dve", Pe[:, :, 0:T], bvec[:, :, 0:T], ex[:, :, 0:T], ALU.mult, [bvec, ex], [Pe])
    self.TT("pool", Ke[:, :, 0:T], kmod[:, :, 0:T], ex[:, :, 0:T], ALU.mult, [kmod, ex], [Ke])
    self.sb_copy(Sb, S32)
    mk = lambda name: self.cb(name, slice(0, L), L).unsqueeze(1).to_broadcast([L, 4, L])
    for bi, (b0, _) in enumerate(blks):
        bs = slice(b0, b0 + L)
        self.to_tm(v_b, T, b0, L, V_tm[0:L, :, :], V_tm)
        self.to_tm(Pe, T, b0, L, Pe_tm[0:L, :, :], Pe_tm)
        self.to_tm(Ke, T, b0, L, Ke_tm[0:L, :, :], Ke_tm)
        prods = [(At, Pt, "m1_str", NM0[0:L, 0, :, 0:L], NM0), (Pt, At, "m1_strT", NM0[0:L, 1, :, 0:L], NM0),
                 (Kt, At, "m1_strT", AakT[0:L, :, 0:L], AakT), (Pt, Rt, "m1_inclT", ArpT[0:L, :, 0:L], ArpT),
                 (Kt, Rt, "m1_inclT", ArkT[0:L, :, 0:L], ArkT)]
        for (lh, rh, mname, oap, obuf) in prods:
            pr = self.bankb()
            for h, (c, hp) in enumerate(HP):
                la = lh[:, c, hp, bs] if (lh is Pt or lh is Kt) else lh[:, c, bs]
                ra = rh[:, c, hp, bs] if (rh is Pt or rh is Kt) else rh[:, c, bs]
                self.MM(pr[0:L, h * L:(h + 1) * L], la, ra, True, True, [lh, rh], [pr])
            self.TT("dve", oap, pr[0:L, 0:4 * L].rearrange("p (h l) -> p h l", h=4), mk(mname), ALU.mult,
                    [pr, self.cstb], [obuf])
        PT = self.tri_inverse(NM0, L)
        pq = self.bankb()
        for h, (c, hp) in enumerate(HP):
            self.MM(pq[0:L, h * 64:(h + 1) * 64], At[:, c, bs], Sb[:, c, hp, :], True, False, [At, Sb], [pq])
            self.MM(pq[0:L, h * 64:(h + 1) * 64], AakT[0:L, h, 0:L], V_tm[0:L, h, :], False, True, [AakT, V_tm], [pq])
        self.evac(rhs_s[0:L, :, :], pq[0:L, 0:256].rearrange("p (h d) -> p h d", h=4), [pq], [rhs_s])
        pv = self.bankb()
        for h in range(4):
            self.MM(pv[0:L, h * 64:(h + 1) * 64], PT[0:L, h, 0:L], rhs_s[0:L, h, :], True, True, [PT, rhs_s], [pv])
        self.evac(U_s[0:L, :, :], pv[0:L, 0:256].rearrange("p (h d) -> p h d", h=4), [pv], [U_s])
        po = self.bankb()
        for h, (c, hp) in enumerate(HP):
            self.MM(po[0:L, h * 64:(h + 1) * 64], Rt[:, c, bs], Sb[:, c, hp, :], True, False, [Rt, Sb], [po])
            self.MM(po[0:L, h * 64:(h + 1) * 64], ArpT[0:L, h, 0:L], U_s[0:L, h, :], False, False, [ArpT, U_s], [po])
            self.MM(po[0:L, h * 64:(h + 1) * 64], ArkT[0:L, h, 0:L], V_tm[0:L, h, :], False, True, [ArkT, V_tm], [po])
        self.evac(otm[0:L, bi, :, :], po[0:L, 0:256].rearrange("p (h d) -> p h d", h=4), [po], [otm])
        pS = self.bankb()
        for h, (c, hp) in enumerate(HP):
            hs = slice(64 * hp, 64 * hp + 64)
            self.MM(pS[hs, c * 64:(c + 1) * 64], Pe_tm[0:L, h, :], U_s[0:L, h, :], True, False, [Pe_tm, U_s], [pS])
            self.MM(pS[hs, c * 64:(c + 1) * 64], Ke_tm[0:L, h, :], V_tm[0:L, h, :], False, True, [Ke_tm, V_tm], [pS])
        for c in range(2):
            self.STT(S32[:, c, :], S32[:, c, :], WL[:, c, bi:bi + 1], pS[:, c * 64:(c + 1) * 64], ALU.mult, ALU.add,
                     [S32, WL, pS], [S32])
        self.sb_copy(Sb, S32)
    NH = NB * 4
    o3 = otm[0:L, 0:NB, :, :].rearrange("p b h d -> p (b h) d")
    P.op("dve", lambda e: e.reduce_sum(s1[0:L, 0:NH], o3, AX.X), reads=[otm], writes=[s1])
    self.TS("dve", s1[0:L, 0:NH], s1[0:L, 0:NH], 1.0 / 64.0, None, ALU.mult, None, [s1], [s1])
    self.TT("dve", o3, o3, s1[0:L, 0:NH].unsqueeze(2).to_broadcast([L, NH, 64]), ALU.subtract, [otm, s1], [otm])
    q3 = sqo[0:L, 0:NB, :, :].rearrange("p b h d -> p (b h) d")
    self.TT("dve", q3, o3, o3, ALU.mult, [otm], [sqo_buf])
    P.op("dve", lambda e: e.reduce_sum(ss[0:L, 0:NH], q3, AX.X), reads=[sqo_buf], writes=[ss])
    self.ACT(ss[0:L, 0:NH], ss[0:L, 0:NH], AF.Sqrt, [ss, self.epsc], [ss], bias=self.epsc[0:L, 1:2], scale=1.0 / 64.0)
    P.op("dve", lambda e: e.reciprocal(ss[0:L, 0:NH], ss[0:L, 0:NH]), reads=[ss], writes=[ss])
    self.TT("dve", on_b[0:L, 0:NB, :, :].rearrange("p b h d -> p (b h) d"), o3,
            ss[0:L, 0:NH].unsqueeze(2).to_broadcast([L, NH, 64]), ALU.mult, [otm, ss], [on_b])
    pys = [self.bankb(), self.bankb()]
    for bi, (b0, _) in enumerate(blks):
        for c in range(2):
            self.MM(pys[c][:, b0:b0 + L], on_b[0:L, bi, 2 * c:2 * c + 2, :].rearrange("p h d -> p (h d)"),
                    self.cb("ident", slice(0, L), L), True, True, [on_b, self.cstb], [pys[c]])
    for c in range(2):
        self.TS("dve", lw[:, c, 0:T], pys[c][:, 0:T], self.V(l, "gnw", c), self.V(l, "gnb", c), ALU.mult, ALU.add,
                [pys[c], vecl], [lw])
        self.TT("dve", lw[:, c, 0:T], lw[:, c, 0:T], bon[:, c, 0:T], ALU.add, [lw, bon], [lw])
        self.TT("dve", self.mixT[:, 2 + c, 0:T], lw[:, c, 0:T], gT[:, c, 0:T], ALU.mult, [lw, gT], [self.mixT])


KB.rwkv = _rwkv
```

```python
import math
from contextlib import ExitStack
import numpy as np
import concourse.bass as bass
import concourse.mybir as mybir
from concourse.bass_utils import run_bass_kernel_spmd

F32 = mybir.dt.float32
BF16 = mybir.dt.bfloat16
AF = mybir.ActivationFunctionType
ALU = mybir.AluOpType
AX = mybir.AxisListType

D_MODEL = 1024
DEPTH = 4
N_META = 16
CHUNK = 64
PAST = 1024
DEC_SEQ = 32
EPS = 1e-6
GW = 256
A_QRANK, A_KVRANK, A_ROPE, A_NOPE = 192, 128, 32, 64
A_SCALE = (A_NOPE + A_ROPE) ** -0.5
A_COLS = A_QRANK + A_KVRANK + A_ROPE
B_COLS = 1024
C_COLS = 256 + 512 + 4
D_COLS = 768 + 256 + 8
D_FF = 2816
B_GN_EPS = 64e-5
NEG = -1.0e30

ENGS = ("pe", "act", "dve", "pool", "sp")


class Buf:
    __slots__ = ("name", "t", "last_w", "readers", "sem", "cnt", "space")

    def __init__(self, name, t, space):
        self.name = name
        self.t = t
        self.space = space
        self.last_w = None
        self.readers = []
        self.sem = {}
        self.cnt = {}

    def __getitem__(self, k):
        return self.t[k]

    def ap(self):
        return self.t if self.space == "dr" else self.t.ap()


class View:
    def __init__(self, buf, ap):
        self.buf, self._ap = buf, ap

    def __getitem__(self, k):
        return self._ap[k]

    def ap(self):
        return self._ap


def _norm(bl):
    return [getattr(b, "buf", b) for b in bl]


class Prog:
    def __init__(self, nc):
        self.nc = nc
        self.es = ExitStack()
        self.q = {e: [] for e in ENGS}
        self.seen = {e: {} for e in ENGS}
        self.esem = {}
        self.n_sem = 0

    def sbuf(self, name, shape, dtype):
        t = self.es.enter_context(self.nc.sbuf_tensor("sb_" + name, list(shape), dtype))
        return Buf(name, t, "sb")

    def psum(self, name, shape, dtype):
        t = self.es.enter_context(self.nc.psum_tensor("pq_" + name, list(shape), dtype))
        return Buf(name, t, "ps")

    def dram(self, name, shape, dtype, kind="Internal"):
        t = self.nc.dram_tensor(name, list(shape), dtype, kind=kind).ap()
        return Buf(name, t, "dr")

    def _sem(self, name):
        self.n_sem += 1
        return self.es.enter_context(self.nc.semaphore(name))

    def _need(self, eng, h, waits):
        if h is None:
            return
        if h[0] == "e":
            _, e2, idx = h
            if e2 == eng and eng in ("pe", "sp"):
                return
            key, val = ("e", e2), idx
        else:
            _, b, c, kd = h
            key, val = ("d", id(b), kd), c
        if self.seen[eng].get(key, -1) >= val:
            return
        self.seen[eng][key] = val
        if h[0] == "e":
            self.q[h[1]][h[2]]["mark"] = True
        waits.append(h)

    def _deps(self, eng, reads, writes):
        waits = []
        for b in reads:
            self._need(eng, b.last_w, waits)
        for b in writes:
            self._need(eng, b.last_w, waits)
            for r in b.readers:
                self._need(eng, r, waits)
        return waits

    def op(self, eng, fn, reads=(), writes=()):
        reads, writes = _norm(reads), _norm(writes)
        pr = [b for b in reads if b.space == "ps" and b not in writes]
        if pr:
            reads = [b for b in reads if b.space != "ps"]
            writes = list(writes) + pr
        waits = self._deps(eng, reads, writes)
        idx = len(self.q[eng])
        self.q[eng].append(dict(fn=fn, waits=waits, mark=False, dma=None))
        h = ("e", eng, idx)
        for b in reads:
            b.readers.append(h)
        for b in writes:
            b.last_w = h
            b.readers = []
        return h

    def dma(self, eng, out_ap, in_ap, reads=(), writes=(), owner=None, **kw):
        reads, writes = _norm(reads), _norm(writes)
        if owner is not None:
            owner = getattr(owner, "buf", owner)
        waits = self._deps(eng, reads, writes)
        if owner is None:
            owner = writes[0] if writes else reads[0]
        kd = "sw" if eng == "pool" else "hw"
        if kd not in owner.sem:
            owner.sem[kd] = self._sem("d_" + kd + "_" + owner.name)
            owner.cnt[kd] = 0
        owner.cnt[kd] += 16
        h = ("d", owner, owner.cnt[kd], kd)
        self.q[eng].append(dict(
            fn=lambda e: e.dma_start(out=out_ap, in_=in_ap, **kw),
            waits=waits, mark=False, dma=owner.sem[kd]))
        for b in reads:
            b.readers.append(h)
        for b in writes:
            b.last_w = h
            b.readers = []
        return h

    def wait_all(self, eng, handles):
        waits = []
        for h in handles:
            self._need(eng, h, waits)
        self.q[eng].append(dict(fn=None, waits=waits, mark=False, dma=None))

    def build(self):
        nc = self.nc
        for e in ENGS:
            self.esem[e] = self._sem("e_" + e)
        rank = {}
        for e in ENGS:
            c = 0
            for i, ins in enumerate(self.q[e]):
                if ins["mark"]:
                    c += 1
                    rank[(e, i)] = c
        engobj = {"pe": "tensor", "act": "scalar", "dve": "vector", "pool": "gpsimd", "sp": "sync"}

        def run(ename):
            def body(eng):
                for i, ins in enumerate(self.q[ename]):
                    for h in ins["waits"]:
                        if h[0] == "e":
                            eng.wait_ge(self.esem[h[1]], rank[(h[1], h[2])])
                        else:
                            eng.wait_ge(h[1].sem[h[3]], h[2])
                    if ins["fn"] is None:
                        continue
                    r = ins["fn"](eng)
                    if ins["dma"] is not None:
                        r.then_inc(ins["dma"], 16)
                    elif ins["mark"]:
                        r.then_inc(self.esem[ename], 1)
            return body

        with nc.Block() as block:
            for e in ENGS:
                if self.q[e]:
                    getattr(block, engobj[e])(run(e))
        self.es.close()


def _win_chunks():
    ch = []
    Z = [-1]

    def cols(base, n):
        return list(range(base, base + n))

    ch.append(cols(0, 128))
    ch.append(cols(128, 64))
    ch.append(cols(192, 128))
    kr = 192 + 128
    ch.append(Z * 32 + cols(kr, 32))
    ch.append(Z * 32 + cols(kr + 16, 16) + cols(kr, 16))
    b0 = A_COLS
    for c in range(8):
        ch.append(cols(b0 + 128 * c, 128))
    c0 = A_COLS + B_COLS
    ch.append(cols(c0, 128)); ch.append(cols(c0 + 128, 128))
    xb = c0 + 256
    ch.append(cols(xb, 128)); ch.append(cols(xb + 128, 128))
    ch.append(cols(xb + 256, 64) * 2)
    ch.append(cols(xb + 320, 64) * 2)
    ch.append(cols(xb + 384, 64) * 2)
    ch.append(cols(xb + 448, 64) * 2)
    ch.append(cols(c0 + 768, 4) * 2)
    d0 = c0 + C_COLS
    for c in range(6):
        ch.append(cols(d0 + 128 * c, 128))
    ch.append(cols(d0 + 768, 128)); ch.append(cols(d0 + 896, 128))
    ch.append(cols(d0 + 1024, 4) * 2)
    ch.append(cols(d0 + 1028, 4) * 2)
    return ch


WIN_CH = _win_chunks()
NWIN = len(WIN_CH)
G_WIN = NWIN // 4
G_WOUT = 2
G_WUP = 11
G_WDN = 8
NGRP = G_WIN + G_WOUT + G_WUP + G_WDN
GSZ = 4096

VOFF = {}
_o = 0
for _n, _w in [("n1g", 8), ("n2g", 8), ("gq", 2), ("gkv", 1), ("gout", 2), ("mu", 8), ("w0", 2), ("a0", 2),
               ("kk", 2), ("ka", 2), ("rk", 2), ("gnw", 2), ("gnb", 2), ("ccw", 24), ("ccb", 6), ("cdtb", 1),
               ("calog", 1), ("cd", 2), ("cgn", 2), ("dcw", 24), ("dalog", 1), ("ddtb", 1), ("dgn", 2),
               ("fcw", 132)]:
    VOFF[_n] = _o
    _o += _w
NV = _o

SOFF = {}
_o = 0
for _n, _w in [("wuq", 2 * 4 * 192), ("wuk", 4 * 128), ("wuv", 256), ("w2a2", 256), ("g2", 256)]:
    SOFF[_n] = _o
    _o += _w
NSM = _o

COFF = {}
_o = 0
for _n, _w in [("ident", 128), ("ones", 128), ("blk1", 128), ("col0", 32), ("colk", 32),
               ("mn_inclT", 64), ("mn_str", 64), ("mn_strT", 64),
               ("m1_strT", 64), ("m1_inclT", 64), ("m1_str", 64),
               ("sel", 256), ("selh", 4), ("m1", 1), ("m2", 1), ("nm2", 1)]:
    COFF[_n] = _o
    _o += _w
NCST = _o


def _consts():
    c = np.zeros((128, NCST), np.float32)
    c[:, COFF["ident"]:COFF["ident"] + 128] = np.eye(128, dtype=np.float32)
    c[:, COFF["ones"]:COFF["ones"] + 128] = 1.0
    b = np.zeros((128, 128), np.float32)
    b[:64, :64] = 1.0
    b[64:, 64:] = 1.0
    c[:, COFF["blk1"]:COFF["blk1"] + 128] = b
    c[:, COFF["col0"]] = 1.0
    c[32:, COFF["colk"]] = 1.0
    i = np.arange(64)[:, None]
    j = np.arange(64)[None, :]
    c[:64, COFF["mn_inclT"]:COFF["mn_inclT"] + 64] = np.where(i <= j, 0.0, NEG)
    c[:64, COFF["mn_str"]:COFF["mn_str"] + 64] = np.where(j < i, 0.0, NEG)
    c[:64, COFF["mn_strT"]:COFF["mn_strT"] + 64] = np.where(i < j, 0.0, NEG)
    c[:64, COFF["m1_strT"]:COFF["m1_strT"] + 64] = (i < j)
    c[:64, COFF["m1_inclT"]:COFF["m1_inclT"] + 64] = (i <= j)
    c[:64, COFF["m1_str"]:COFF["m1_str"] + 64] = (j < i)
    sel = np.zeros((8, 2, 128), np.float32)
    for cc in range(2):
        for m in range(128):
            sel[2 * cc + m // 64, cc, m] = 1.0
    c[:8, COFF["sel"]:COFF["sel"] + 256] = sel.reshape(8, 256)
    for h in range(4):
        c[h, COFF["selh"] + h] = 1.0
        c[4 + h, COFF["selh"] + h] = 1.0
    c[0:4, COFF["m1"]] = 1.0
    c[4:8, COFF["m2"]] = 1.0
    c[4:8, COFF["nm2"]] = -1.0
    return c


def _rope_tables(pos):
    half = A_ROPE // 2
    inv = np.power(np.float32(10000.0), -np.arange(half, dtype=np.float32) / np.float32(half)).astype(np.float32)
    ang = pos.astype(np.float32)[None, :] * inv[:, None]
    cos, sin = np.cos(ang).astype(np.float32), np.sin(ang).astype(np.float32)
    C = np.concatenate([cos, cos], 0)
    S = np.concatenate([-sin, sin], 0)
    return np.ascontiguousarray(np.stack([C, S], 1))


def _fm(v, nchunk):
    return np.ascontiguousarray(np.asarray(v, np.float32).reshape(nchunk, 128).T)


C_XBC_SRC = [list(range(0, 128)), list(range(128, 256)),
             list(range(256, 320)) * 2, list(range(320, 384)) * 2,
             list(range(384, 448)) * 2, list(range(448, 512)) * 2]
FF_ORDER = []
for _i in range(22):
    FF_ORDER.append(list(range(128 * _i, 128 * _i + 128)))
    FF_ORDER.append(list(range(D_FF + 128 * _i, D_FF + 128 * _i + 128)))


def prep_weights(W):
    wbig = np.zeros((DEPTH, NGRP, 128, GSZ), np.float32)
    wsm = np.zeros((DEPTH, 128, NSM), np.float32)
    vec = np.zeros((DEPTH, 128, NV), np.float32)
    for l in range(DEPTH):
        win = np.asarray(W["w_in"][l], np.float32)
        winx = np.concatenate([win, np.zeros((1024, 1), np.float32)], 1)
        for ci, src in enumerate(WIN_CH):
            m = np.zeros((1024, 128), np.float32)
            m[:, :len(src)] = winx[:, src]
            g, k = divmod(ci, 4)
            wbig[l, g].reshape(128, 4, 8, 128)[:, k] = m.reshape(8, 128, 128).transpose(1, 0, 2)
        wo = np.asarray(W["w_out"][l], np.float32)
        for j in range(8):
            g, k = divmod(j, 4)
            wbig[l, G_WIN + g].reshape(128, 4, 8, 128)[:, k] = \
                wo[:, 128 * j:128 * j + 128].reshape(8, 128, 128).transpose(1, 0, 2)
        wu = np.asarray(W["f_wup"][l], np.float32)
        for ci, src in enumerate(FF_ORDER):
            g, k = divmod(ci, 4)
            wbig[l, G_WIN + G_WOUT + g].reshape(128, 4, 8, 128)[:, k] = \
                wu[:, src].reshape(8, 128, 128).transpose(1, 0, 2)
        wd = np.asarray(W["f_wdown"][l], np.float32)
        for j in range(8):
            wbig[l, G_WIN + G_WOUT + G_WUP + j][:, :22 * 128].reshape(128, 22, 128)[:] = \
                wd[:, 128 * j:128 * j + 128].reshape(22, 128, 128).transpose(1, 0, 2)
        wuq = np.asarray(W["a_wuq"][l], np.float32)
        wq = np.zeros((128, 2, 4, 192), np.float32)
        for h in range(4):
            main = np.zeros((256, 128), np.float32)
            main[:192, 64:128] = wuq[:, h * 96:h * 96 + 64]
            main[:192, 32:64] = wuq[:, h * 96 + 64:h * 96 + 96]
            sw = np.zeros((256, 64), np.float32)
            sw[:192, 32:48] = wuq[:, h * 96 + 80:h * 96 + 96]
            sw[:192, 48:64] = wuq[:, h * 96 + 64:h * 96 + 80]
            for kc in range(2):
                wq[:, kc, h, 0:128] = main[kc * 128:(kc + 1) * 128]
                wq[:, kc, h, 128:192] = sw[kc * 128:(kc + 1) * 128]
        wsm[l, :, SOFF["wuq"]:SOFF["wuq"] + 1536] = wq.reshape(128, 1536)
        wuk = np.asarray(W["a_wuk"][l], np.float32)
        wk = np.zeros((128, 4, 128), np.float32)
        for h in range(4):
            wk[:, h, 64:128] = wuk[:, h * 64:(h + 1) * 64]
        wsm[l, :, SOFF["wuk"]:SOFF["wuk"] + 512] = wk.reshape(128, 512)
        wsm[l, :, SOFF["wuv"]:SOFF["wuv"] + 256] = np.asarray(W["a_wuv"][l], np.float32)
        wsm[l, 0:64, SOFF["w2a2"]:SOFF["w2a2"] + 256] = np.asarray(W["b_w2"][l], np.float32)
        wsm[l, 64:128, SOFF["w2a2"]:SOFF["w2a2"] + 256] = np.asarray(W["b_a2"][l], np.float32)
        wsm[l, :, SOFF["g2"]:SOFF["g2"] + 256] = np.asarray(W["b_g2"][l], np.float32)
        v = vec[l]

        def put(name, arr):
            arr = np.asarray(arr, np.float32)
            v[:arr.shape[0], VOFF[name]:VOFF[name] + arr.shape[1]] = arr

        put("n1g", _fm(W["norm1_g"][l], 8)); put("n2g", _fm(W["norm2_g"][l], 8))
        gq = np.zeros(256, np.float32); gq[:192] = W["a_gq"][l]
        put("gq", _fm(gq, 2)); put("gkv", _fm(W["a_gkv"][l], 1)); put("gout", _fm(W["a_gout"][l], 2))
        put("mu", _fm(W["b_mu"][l], 8)); put("w0", _fm(W["b_w0"][l], 2)); put("a0", _fm(W["b_a0"][l], 2))
        put("kk", _fm(W["b_kk"][l], 2)); put("ka", _fm(W["b_ka"][l], 2))
        put("rk", _fm(np.asarray(W["b_rk"][l]).reshape(256), 2))
        put("gnw", _fm(W["b_gnw"][l], 2)); put("gnb", _fm(W["b_gnb"][l], 2))
        ccw = np.asarray(W["c_convw"][l], np.float32)
        ccb = np.asarray(W["c_convb"][l], np.float32)
        a = np.zeros((128, 6, 4), np.float32); bb = np.zeros((128, 6), np.float32)
        for c, src in enumerate(C_XBC_SRC):
            a[:, c, :] = ccw[:, src].T
            bb[:, c] = ccb[src]
        put("ccw", a.reshape(128, 24)); put("ccb", bb)
        put("cdtb", np.tile(np.asarray(W["c_dtb"][l], np.float32), 2)[:, None])
        put("calog", np.tile(np.asarray(W["c_alog"][l], np.float32), 2)[:, None])
        put("cd", _fm(np.repeat(np.asarray(W["c_d"][l], np.float32), 64), 2))
        put("cgn", _fm(W["c_gnorm"][l], 2))
        dcw = np.asarray(W["d_convw"][l], np.float32)
        put("dcw", dcw.T.reshape(6, 128, 4).transpose(1, 0, 2).reshape(128, 24))
        put("dalog", np.tile(np.asarray(W["d_alog"][l], np.float32), 2)[:, None])
        put("ddtb", np.tile(np.asarray(W["d_dtb"][l], np.float32), 2)[:, None])
        put("dgn", _fm(np.tile(np.asarray(W["d_gnorm"][l], np.float32), 4), 2))
        fcw = np.asarray(W["f_convw"][l], np.float32)
        a = np.zeros((128, 44, 3), np.float32)
        for c, src in enumerate(FF_ORDER):
            a[:, c, :] = fcw[:, src].T
        put("fcw", a.reshape(128, 132))
    fin = _fm(W["final_g"], 8)
    return wbig, wsm, vec, fin


class Cfg:
    def __init__(self, seq=8192, tt=256, en=("A", "B", "C", "D"), sample=True, nslot=3, dbg=()):
        self.seq, self.tt, self.en, self.sample, self.nslot, self.dbg = seq, tt, set(en), sample, nslot, dbg
        self.ntok = N_META + seq
        assert seq % tt == 0 and tt % 128 == 0


def blocks_of(T):
    if T <= 64:
        return [(0, T)]
    return [(64 * b, 64) for b in range(T // 64)]


class KB:
    def __init__(self, cfg):
        self.cfg = cfg
        nc = bass.Bass("TRN2", target_bir_lowering=False)
        self.nc = nc
        P = self.P = Prog(nc)
        TT = cfg.tt
        NT = cfg.ntok
        D = lambda n, s, dt=F32, kind="ExternalInput": P.dram(n, s, dt, kind=kind)
        self.xT_d = D("xT", [1024, NT])
        self.xsT_d = D("xsT", [1024, DEC_SEQ])
        self.ropeP_d = D("ropeP", [32, 2, NT])
        self.ropeS_d = D("ropeS", [32, 2, DEC_SEQ])
        self.cst_d = D("cst", [128, NCST])
        self.wbig_d = D("wbig", [DEPTH, NGRP, 128, GSZ])
        self.wsm_d = D("wsm", [DEPTH, 128, NSM])
        self.vec_d = D("vec", [DEPTH, 128, NV])
        self.fin_d = D("fin", [128, 8])
        self.s_in = dict(
            ckvT=D("s_ckvT", [DEPTH, 128, PAST]), kropeT=D("s_kropeT", [DEPTH, 32, PAST]),
            SB=D("s_SB", [DEPTH, 128, 2, 64]), shift=D("s_shift", [DEPTH, 128, 8]),
            SC=D("s_SC", [DEPTH, 128, 2, 64]), convC=D("s_convC", [DEPTH, 128, 6, 3]),
            SD=D("s_SD", [DEPTH, 128, 2, 64]), convD=D("s_convD", [DEPTH, 128, 6, 3]),
            convF=D("s_convF", [DEPTH, 128, 44, 2]))
        O = lambda n, s: P.dram(n, s, F32, kind="ExternalOutput")
        self.out = {}
        for sq, nt in (("p", NT), ("s", DEC_SEQ)):
            self.out[sq] = dict(
                yT=O(f"o_{sq}_yT", [1024, nt]), ckvT=O(f"o_{sq}_ckvT", [DEPTH, 128, nt]),
                kropeT=O(f"o_{sq}_kropeT", [DEPTH, 32, nt]),
                SB=O(f"o_{sq}_SB", [DEPTH, 128, 2, 64]), shift=O(f"o_{sq}_shift", [DEPTH, 128, 8]),
                SC=O(f"o_{sq}_SC", [DEPTH, 128, 2, 64]), convC=O(f"o_{sq}_convC", [DEPTH, 128, 6, 3]),
                SD=O(f"o_{sq}_SD", [DEPTH, 128, 2, 64]), convD=O(f"o_{sq}_convD", [DEPTH, 128, 6, 3]),
                convF=O(f"o_{sq}_convF", [DEPTH, 128, 44, 2]))
        self.out_handles = []
        self.dbg_out = {}
        self.wb_d = [[P.dram(f"wb_{l}_{g}", [128, GSZ], BF16) for g in range(NGRP)] for l in range(DEPTH)]
        self.kT_d = [P.dram(f"kTs_{l}", [128, 4, NT], BF16) for l in range(DEPTH)]
        self.v_d = [P.dram(f"vs_{l}", [NT, 260], BF16) for l in range(DEPTH)]
        S = P.sbuf
        self.cst = S("cst", [128, NCST - COFF["sel"]], F32)
        self.cstb = S("cstb", [128, NCST], BF16)
        self.vec = [S(f"vec{l}", [128, NV], F32) for l in range(DEPTH)]
        self.wsmring = [S(f"wsm{i}", [128, NSM], BF16) for i in range(1)]
        self.wsm = {}
        self.wsmb_d = [P.dram(f"wsmb_{l}", [128, NSM], BF16) for l in range(DEPTH)]
        self._wsmi = 0
        self.fin = S("fin", [128, 8], F32)
        self.dv = [S(f"dv{l}", [128, 8], F32) for l in range(DEPTH)]
        self.ring = [S(f"ring{i}", [128, GSZ], BF16) for i in range(cfg.nslot)]
        self.xT = S("xT_s", [128, 8, TT], F32)
        self.hT = S("hT", [128, 8, TT], BF16)
        self.rt = S("rt", [128, TT], F32)
        self.rstd = S("rstd", [128, TT], F32)
        self.mixT = S("mixT", [128, 8, TT], BF16)
        self.sq = self.mixT
        self.ystage = [S(f"ystage{i}", [128, TT], F32) for i in range(2)]
        self.pj = S("pj", [128, 8, TT + 3], F32)
        self.pj2 = S("pj2", [128, 4, TT], F32)
        self.ps = [P.psum(f"ps{i}", [128, 512], F32) for i in range(8)]
        self.epsc = S("epsc", [128, 4], F32)
        self._rr = 0
        self._rrb = 0
        self.ug = [S(f"ug{i}", [128, 4, TT + 2], F32) for i in range(2)]
        self.acc = [S(f"acc{i}", [128, 4, TT], F32) for i in range(2)]
        self.sil = S("sil", [128, 2, TT], F32)
        self._actbufs = None
        self.histF = [S(f"histF{l}", [128, 44, 2], F32) for l in range(DEPTH)]
        self.wseq = []
        self.wptr = 0
        self.wissued = 0

    def bank(self):
        b = self.ps[self._rr % 3]
        self._rr += 1
        return b

    def bankb(self):
        b = self.ps[3 + self._rrb % 3]
        self._rrb += 1
        return b

    def c32(self, name, rows=slice(0, 128), n=None, off=0):
        o = COFF[name] + off - COFF["sel"]
        return self.cst[rows, o:o + (n if n is not None else 1)]

    def cb(self, name, rows=slice(0, 128), n=None, off=0):
        o = COFF[name] + off
        return self.cstb[rows, o:o + (n if n is not None else 1)]

    def V(self, l, name, col=0, rows=slice(0, 128), n=1):
        o = VOFF[name] + col
        return self.vec[l][rows, o:o + n]

    def dbg(self, name, buf, ap, shape):
        if name not in self.cfg.dbg:
            return
        key = f"dbg_{name}_{len(self.dbg_out)}"
        d = self.P.dram(key, list(shape), F32, kind="ExternalOutput")
        nm = name
        while nm in self.dbg_out.values():
            nm = nm + "+"
        self.dbg_out[key] = nm
        h = self.P.dma("pool", d.ap(), ap, reads=[buf], owner=d)
        self.out_handles.append(h)

    def plan_weights(self, order):
        self.wseq = order

    def wget(self):
        P = self.P
        ns = self.cfg.nslot
        target = min(len(self.wseq), self.wptr + ns)
        while self.wissued < target:
            l, g = self.wseq[self.wissued]
            slot = self.ring[self.wissued % ns]
            P.dma("sp", slot[:, :], self.wb_d[l][g].ap(), reads=[self.wb_d[l][g]], writes=[slot])
            self.wissued += 1
        slot = self.ring[self.wptr % ns]
        self.wptr += 1
        return slot

    def prologue(self):
        P = self.P
        cfg = self.cfg
        P.dma("sp", self.cst[:, :], self.cst_d[:, COFF["sel"]:NCST], writes=[self.cst])
        P.dma("pool", self.cstb[:, :], self.cst_d.ap(), writes=[self.cstb])
        P.dma("sp", self.fin[:, :], self.fin_d.ap(), writes=[self.fin])
        for l in range(DEPTH):
            P.dma("sp", self.vec[l][:, :], self.vec_d[l], writes=[self.vec[l]])
            P.dma("pool", self.wsmb_d[l].ap(), self.wsm_d[l], writes=[self.wsmb_d[l]])
        for l in range(DEPTH):
            for (ga, gb) in ((0, G_WIN + G_WOUT), (G_WIN + G_WOUT, NGRP)):
                own = Buf(f"cast{l}_{ga}", None, "dr")
                for g in range(ga, gb):
                    P.dma("pool", self.wb_d[l][g].ap(), self.wbig_d[l, g], writes=[self.wb_d[l][g]],
                          owner=own, max_dma_last_dim=8192)
                for g in range(ga, gb):
                    self.wb_d[l][g].last_w = ("d", own, own.cnt["sw"], "sw")
        self.mla_init()
        self.scan_init()
        for i, v in enumerate((EPS, B_GN_EPS, 1.0, 0.0)):
            P.op("pool", lambda e, i=i, v=v: e.memset(self.epsc[:, i:i + 1], v), writes=[self.epsc])
        for l in range(DEPTH):
            P.op("pool", lambda e, l=l: e.memset(self.histF[l][:, :, :], 0.0), writes=[self.histF[l]])
            dv = self.dv[l]
            P.op("act", lambda e, l=l, dv=dv: e.activation(dv[0:8, 0:1], self.V(l, "calog", rows=slice(0, 8)), AF.Exp),
                 reads=[self.vec[l]], writes=[dv])
            P.op("act", lambda e, l=l, dv=dv: e.activation(dv[0:8, 1:2], self.V(l, "dalog", rows=slice(0, 8)), AF.Exp),
                 reads=[self.vec[l]], writes=[dv])
            P.op("dve", lambda e, dv=dv: e.tensor_scalar(dv[0:8, 0:2], dv[0:8, 0:2], -1.0, None, op0=ALU.mult),
                 reads=[dv], writes=[dv])
            P.op("dve", lambda e, l=l, dv=dv: e.tensor_scalar(dv[:, 2:4], self.V(l, "ka", n=2), -1.0, 1.0,
                                                             op0=ALU.mult, op1=ALU.add),
                 reads=[self.vec[l]], writes=[dv])

    def rmsnorm_x(self, T, gname, l):
        P = self.P
        xT, sq, hT, rt, rstd = self.xT, self.sq, self.hT, self.rt, self.rstd
        P.op("act", lambda e: e.activation(sq[:, :, 0:T], xT[:, :, 0:T], AF.Square), reads=[xT], writes=[sq])
        pb = self.bank()
        for kc in range(8):
            P.op("pe", lambda e, kc=kc: e.matmul(pb[:, 0:T], self.cb("ones", n=128), sq[:, kc, 0:T],
                                                 start=(kc == 0), stop=(kc == 7)),
                 reads=[self.cstb, sq], writes=[pb])
        P.op("act", lambda e: e.activation(rt[:, 0:T], pb[:, 0:T], AF.Sqrt, bias=self.epsc[:, 0:1],
                                           scale=1.0 / 1024.0),
             reads=[pb, self.epsc], writes=[rt])
        P.op("dve", lambda e: e.reciprocal(rstd[:, 0:T], rt[:, 0:T]), reads=[rt], writes=[rstd])
        for kc in range(8):
            g = self.fin[:, kc:kc + 1] if l is None else self.V(l, gname, kc)
            gb = self.fin if l is None else self.vec[l]
            P.op("dve", lambda e, kc=kc, g=g: e.scalar_tensor_tensor(
                hT[:, kc, 0:T], xT[:, kc, 0:T], g, rstd[:, 0:T], op0=ALU.mult, op1=ALU.mult),
                reads=[xT, gb, rstd], writes=[hT])

    def evac(self, out_ap, in_ap, rbufs, wbufs, eng=None):
        P = self.P
        if eng is None:
            self._ev = getattr(self, "_ev", 0) + 1
            eng = "act" if self._ev % 2 else "dve"
        if eng == "act":
            P.op("act", lambda e: e.activation(out_ap, in_ap, AF.Identity), reads=rbufs, writes=wbufs)
        elif eng == "dve":
            P.op("dve", lambda e: e.tensor_copy(out_ap, in_ap), reads=rbufs, writes=wbufs)
        else:
            P.op("pool", lambda e: e.tensor_copy(out_ap, in_ap), reads=rbufs, writes=wbufs)

    def proj_chunks(self, l, T, c0, c1, dest):
        P = self.P
        for ci in range(c0, c1):
            if ci % 4 == 0:
                self._wslot = self.wget()
            slot = self._wslot
            k = ci % 4
            pb = self.bank()
            for kc in range(8):
                o = k * 1024 + kc * 128
                P.op("pe", lambda e, o=o, kc=kc, pb=pb, slot=slot: e.matmul(
                    pb[:, 0:T], slot[:, o:o + 128], self.hT[:, kc, 0:T], start=(kc == 0), stop=(kc == 7)),
                    reads=[slot, self.hT], writes=[pb])
            buf, oap, r0, r1 = dest(ci)
            self.evac(oap, pb[r0:r1, 0:T], [pb], [buf])

    def outproj(self, l, T):
        P = self.P
        for g in range(2):
            slot = self.wget()
            for k in range(4):
                j = 4 * g + k
                pb = self.bank()
                for kc in range(8):
                    o = k * 1024 + kc * 128
                    P.op("pe", lambda e, o=o, kc=kc, pb=pb, slot=slot: e.matmul(
                        pb[:, 0:T], slot[:, o:o + 128], self.mixT[:, kc, 0:T], start=(kc == 0), stop=(kc == 7)),
                        reads=[slot, self.mixT], writes=[pb])
                P.op("dve", lambda e, j=j, pb=pb: e.tensor_tensor(
                    self.xT[:, j, 0:T], self.xT[:, j, 0:T], pb[:, 0:T], ALU.add),
                    reads=[self.xT, pb], writes=[self.xT])

    def act_view(self, i):
        TT = self.cfg.tt
        if self._actbufs is None:
            self._actbufs = [self.B("xbcs", [128, 8, TT], F32), self.B("khat32", [128, 2, TT], F32),
                             self.B("kb32", [128, 2, TT], F32)]
        if i < 16:
            b, k = self._actbufs[0], i
        elif i < 20:
            b, k = self._actbufs[1], i - 16
        else:
            b, k = self._actbufs[2], i - 20
        v = b.ap().bitcast(BF16).rearrange("p c (a t) -> p (c a) t", a=2)
        return b, v[:, k, :]

    def ffn(self, l, T):
        P = self.P
        hF = self.histF[l]
        vecl = self.vec[l]

        def emit_mm(grp):
            slot = self.wget()
            ug = self.ug[grp % 2]
            P.op("pool", lambda e: e.tensor_copy(ug[:, :, 0:2], hF[:, 4 * grp:4 * grp + 4, :]), reads=[hF], writes=[ug])
            for k in range(4):
                pb = self.bank()
                for kc in range(8):
                    o = k * 1024 + kc * 128
                    self.MM(pb[:, 0:T], slot[:, o:o + 128], self.hT[:, kc, 0:T], kc == 0, kc == 7, [slot, self.hT], [pb])
                self.evac(ug[:, k, 2:2 + T], pb[:, 0:T], [pb], [ug], eng="act")
            P.op("pool", lambda e: e.tensor_copy(hF[:, 4 * grp:4 * grp + 4, :], ug[:, :, T:T + 2]), reads=[ug], writes=[hF])

        def emit_elem(grp):
            ug = self.ug[grp % 2]
            acc = self.acc[grp % 2]
            wv = lambda i, k: self.V(l, "fcw", (4 * grp + k) * 3 + i)
            for k in range(4):
                self.ACT(acc[:, k, 0:T], ug[:, k, 0:T], AF.Identity, [ug, vecl], [acc], scale=wv(0, k))
            for i in (1, 2):
                for k in range(4):
                    self.STT(acc[:, k, 0:T], ug[:, k, i:i + T], wv(i, k), acc[:, k, 0:T], ALU.mult, ALU.add,
                             [ug, acc, vecl], [acc])
            for h2 in range(2):
                self.ACT(self.sil[:, h2, 0:T], acc[:, 2 * h2, 0:T], AF.Silu, [acc], [self.sil])
            for h2 in range(2):
                ab, av = self.act_view(2 * grp + h2)
                self.TT("dve", av[:, 0:T], self.sil[:, h2, 0:T], acc[:, 2 * h2 + 1, 0:T], ALU.mult, [self.sil, acc], [ab])

        emit_mm(0)
        for grp in range(G_WUP):
            if grp + 1 < G_WUP:
                emit_mm(grp + 1)
            emit_elem(grp)
        for j in range(8):
            slot = self.wget()
            pb = self.bank()
            for kc in range(22):
                ab, av = self.act_view(kc)
                self.MM(pb[:, 0:T], slot[:, kc * 128:(kc + 1) * 128], av[:, 0:T], kc == 0, kc == 21, [slot, ab], [pb])
            self.TT("dve", self.xT[:, j, 0:T], self.xT[:, j, 0:T], pb[:, 0:T], ALU.add, [self.xT, pb], [self.xT])

    def final_norm(self, sq_name, t0, T):
        P = self.P
        xT, sq, rt, rstd = self.xT, self.sq, self.rt, self.rstd
        P.op("act", lambda e: e.activation(sq[:, :, 0:T], xT[:, :, 0:T], AF.Square), reads=[xT], writes=[sq])
        pb = self.bank()
        for kc in range(8):
            P.op("pe", lambda e, kc=kc: e.matmul(pb[:, 0:T], self.cb("ones", n=128), sq[:, kc, 0:T],
                                                 start=(kc == 0), stop=(kc == 7)),
                 reads=[self.cstb, sq], writes=[pb])
        P.op("act", lambda e: e.activation(rt[:, 0:T], pb[:, 0:T], AF.Sqrt, bias=self.epsc[:, 0:1],
                                           scale=1.0 / 1024.0), reads=[pb, self.epsc], writes=[rt])
        P.op("dve", lambda e: e.reciprocal(rstd[:, 0:T], rt[:, 0:T]), reads=[rt], writes=[rstd])
        yd = self.out[sq_name]["yT"]
        for kc in range(8):
            ys = self.ystage[kc % 2]
            P.op("dve", lambda e, kc=kc, ys=ys: e.scalar_tensor_tensor(
                ys[:, 0:T], xT[:, kc, 0:T], self.fin[:, kc:kc + 1], rstd[:, 0:T], op0=ALU.mult, op1=ALU.mult),
                reads=[xT, self.fin, rstd], writes=[ys])
            h = P.dma("pool", yd[kc * 128:(kc + 1) * 128, t0:t0 + T], ys[:, 0:T], reads=[ys], owner=ys)
            self.out_handles.append(h)

    def process_tile(self, sqn, t0, T):
        P = self.P
        src = self.xT_d if sqn == "p" else self.xsT_d
        P.dma("sp", self.xT[:, :, 0:T],
              src.ap().rearrange("(c p) t -> p c t", p=128)[:, :, t0:t0 + T], writes=[self.xT])
        rsrc = self.ropeP_d if sqn == "p" else self.ropeS_d
        P.dma("sp", self.ropeT[32:64, :, 0:T], rsrc[:, :, t0:t0 + T], writes=[self.ropeT])
        for l in range(DEPTH):
            self.rmsnorm_x(T, "n1g", l)
            self.mixers(sqn, l, t0, T)
            self.outproj(l, T)
            self.rmsnorm_x(T, "n2g", l)
            self.ffn(l, T)
        self.final_norm(sqn, t0, T)

    def mixers(self, sqn, l, t0, T):
        P = self.P
        en = self.cfg.en
        wb_ = self.wsmring[0]
        self._wsmi += 1
        P.dma("sp", wb_[:, :], self.wsmb_d[l].ap(), reads=[self.wsmb_d[l]], writes=[wb_])
        self.wsm[l] = wb_
        def destA(ci):
            return self.pj, self.pj[:, ci, 0:T], 0, 128
        self.proj_chunks(l, T, 0, 5, destA)
        if "A" in en:
            self.mla(sqn, l, t0, T)
        else:
            P.op("pool", lambda e: e.memset(self.mixT[:, 0:2, 0:T], 0.0), writes=[self.mixT])
        def destB(ci):
            return self.pj, self.pj[:, ci - 5, 1:1 + T], 0, 128
        self.proj_chunks(l, T, 5, 13, destB)
        if "B" in en:
            self.rwkv(sqn, l, t0, T)
        else:
            P.op("pool", lambda e: e.memset(self.mixT[:, 2:4, 0:T], 0.0), writes=[self.mixT])
        def destC(ci):
            if ci < 15:
                return self.pj2, self.pj2[:, ci - 13, 0:T], 0, 128
            if ci < 21:
                return self.pj, self.pj[:, ci - 15, 3:3 + T], 0, 128
            return self.pj2, self.pj2[0:8, 2, 0:T], 0, 8
        self.proj_chunks(l, T, 13, 22, destC)
        if "C" in en:
            self.ssd(sqn, l, t0, T)
        else:
            P.op("pool", lambda e: e.memset(self.mixT[:, 4:6, 0:T], 0.0), writes=[self.mixT])
        def destD(ci):
            if ci < 28:
                return self.pj, self.pj[:, ci - 22, 3:3 + T], 0, 128
            if ci < 30:
                return self.pj2, self.pj2[:, ci - 28, 0:T], 0, 128
            return self.pj2, self.pj2[0:8, ci - 28, 0:T], 0, 8
        self.proj_chunks(l, T, 22, 32, destD)
        if "D" in en:
            self.gdn(sqn, l, t0, T)
        else:
            P.op("pool", lambda e: e.memset(self.mixT[:, 6:8, 0:T], 0.0), writes=[self.mixT])
        self.pump(10 ** 9)

    def tiles(self):
        cfg = self.cfg
        ts = [("p", 0, N_META)]
        for i in range(cfg.seq // cfg.tt):
            ts.append(("p", N_META + i * cfg.tt, cfg.tt))
        return ts

    def emit_states(self, sqn):
        P = self.P
        o = self.out[sqn]
        bl = []
        for l in range(DEPTH):
            for nm, b in (("convF", self.histF[l]), ("SB", self.S32["B"][l]), ("SC", self.S32["C"][l]),
                          ("SD", self.S32["D"][l]), ("convC", self.histC[l]), ("convD", self.histD[l]),
                          ("shift", self.shiftB[l])):
                own = self.__dict__.setdefault("_stout_" + sqn, Buf("stout_" + sqn, None, "dr"))
                P.dma("pool", o[nm][l], b.ap(), reads=[b], owner=own)
                bl.append(b)
        fin_h = ("d", own, own.cnt["sw"], "sw")
        for b in bl:
            b.readers = [fin_h if (r[0] == "d" and r[1] is own) else r for r in b.readers]
        self.out_handles.append(fin_h)

    def load_states(self):
        P = self.P
        bl = []
        for l in range(DEPTH):
            for nm, b in (("convF", self.histF[l]), ("SB", self.S32["B"][l]), ("SC", self.S32["C"][l]),
                          ("SD", self.S32["D"][l]), ("convC", self.histC[l]), ("convD", self.histD[l]),
                          ("shift", self.shiftB[l])):
                own = self.__dict__.setdefault("_stin", Buf("stin", None, "dr"))
                P.dma("sp", b.ap(), self.s_in[nm][l], writes=[b], owner=own)
                bl.append(b)
        for b in bl:
            b.last_w = ("d", own, own.cnt["hw"], "hw")

    def build(self):
        cfg = self.cfg
        P = self.P
        tl = self.tiles()
        n_tl = len(tl) + (1 if cfg.sample else 0)
        self.plan_weights([(l, g) for _ in range(n_tl) for l in range(DEPTH) for g in range(NGRP)])
        self.prologue()
        for (sqn, t0, T) in tl:
            self.process_tile(sqn, t0, T)
        self.emit_states("p")
        if cfg.sample:
            self.load_states()
            self.process_tile("s", 0, DEC_SEQ)
            self.emit_states("s")
        P.wait_all("pool", self.out_handles)
        P.build()
        return self.nc


def _prep_core_inputs(inp, b, cfg, shared):
    NT = cfg.ntok
    x = np.concatenate([np.asarray(inp["meta_tokens"], np.float32), np.asarray(inp["x_prompt"][b], np.float32)], 0)
    m = dict(shared)
    m["xT"] = np.ascontiguousarray(x.T)
    m["xsT"] = np.ascontiguousarray(np.asarray(inp["x_sample"][b], np.float32).T)
    ck = np.asarray(inp["cache_mla_ckv"][:, b], np.float32)
    m["s_ckvT"] = np.ascontiguousarray(ck.transpose(0, 2, 1))
    kr = np.asarray(inp["cache_mla_krope"][:, b], np.float32)
    m["s_kropeT"] = np.ascontiguousarray(kr.transpose(0, 2, 1))
    sb = np.asarray(inp["state_rwkv"][:, b], np.float32)
    m["s_SB"] = np.ascontiguousarray(sb.reshape(DEPTH, 2, 2, 64, 64).transpose(0, 2, 4, 1, 3).reshape(DEPTH, 128, 2, 64))
    m["s_shift"] = np.ascontiguousarray(np.asarray(inp["state_rwkv_shift"][:, b], np.float32).reshape(DEPTH, 8, 128).transpose(0, 2, 1))
    sc = np.asarray(inp["state_ssd"][:, b], np.float32)
    m["s_SC"] = np.ascontiguousarray(sc.reshape(DEPTH, 2, 2, 64, 64).transpose(0, 2, 4, 1, 3).reshape(DEPTH, 128, 2, 64))
    cc = np.asarray(inp["state_ssd_conv"][:, b], np.float32)
    a = np.zeros((DEPTH, 128, 6, 3), np.float32)
    for c, srcc in enumerate(C_XBC_SRC):
        a[:, :, c, :] = cc[:, :, srcc].transpose(0, 2, 1)
    m["s_convC"] = a
    sd = np.asarray(inp["state_gdn"][:, b], np.float32)
    m["s_SD"] = np.ascontiguousarray(sd.reshape(DEPTH, 2, 2, 64, 64).transpose(0, 2, 3, 1, 4).reshape(DEPTH, 128, 2, 64))
    cd = np.asarray(inp["state_gdn_conv"][:, b], np.float32)
    m["s_convD"] = np.ascontiguousarray(cd.reshape(DEPTH, 3, 6, 128).transpose(0, 3, 2, 1))
    cf = np.asarray(inp["state_ffn_conv"][:, b], np.float32)
    a = np.zeros((DEPTH, 128, 44, 2), np.float32)
    for c, srcc in enumerate(FF_ORDER):
        a[:, :, c, :] = cf[:, :, srcc].transpose(0, 2, 1)
    m["s_convF"] = a
    return m


def _post_core(res, cfg):
    o = {}
    for sq in ("p", "s"):
        g = lambda n: np.asarray(res[f"o_{sq}_{n}"])
        y = g("yT").T
        o[f"{sq}_y"] = y[N_META:] if sq == "p" else y
        o[f"{sq}_ckv"] = g("ckvT").transpose(0, 2, 1)
        o[f"{sq}_krope"] = g("kropeT").transpose(0, 2, 1)
        sb = g("SB").reshape(DEPTH, 2, 64, 2, 64)
        o[f"{sq}_SB"] = sb.transpose(0, 3, 1, 4, 2).reshape(DEPTH, 4, 64, 64)
        o[f"{sq}_shift"] = g("shift").transpose(0, 2, 1).reshape(DEPTH, 1024)
        sc = g("SC").reshape(DEPTH, 2, 64, 2, 64)
        o[f"{sq}_SC"] = sc.transpose(0, 3, 1, 4, 2).reshape(DEPTH, 4, 64, 64)
        cc = g("convC")
        full = np.zeros((DEPTH, 3, 512), np.float32)
        for c, srcc in enumerate(C_XBC_SRC):
            n = 128 if c < 2 else 64
            full[:, :, srcc[:n]] = cc[:, :n, c, :].transpose(0, 2, 1)
        o[f"{sq}_convC"] = full
        sd = g("SD").reshape(DEPTH, 2, 64, 2, 64)
        o[f"{sq}_SD"] = sd.transpose(0, 3, 1, 2, 4).reshape(DEPTH, 4, 64, 64)
        o[f"{sq}_convD"] = g("convD").transpose(0, 3, 2, 1).reshape(DEPTH, 3, 768)
        cf = g("convF")
        full = np.zeros((DEPTH, 2, 2 * D_FF), np.float32)
        for c, srcc in enumerate(FF_ORDER):
            full[:, :, srcc] = cf[:, :, c, :].transpose(0, 2, 1)
        o[f"{sq}_convF"] = full
    return o


OUT_ORDER = ["y", "ckv", "krope", "SB", "shift", "SC", "convC", "SD", "convD", "convF"]


def run_cfg(cfg, inp, n_cores, trace=False):
    kb = KB(cfg)
    nc = kb.build()
    wbig, wsm, vec, fin = prep_weights(inp)
    pos_p = np.arange(cfg.ntok, dtype=np.int64) - N_META
    pos_s = PAST + np.arange(DEC_SEQ, dtype=np.int64)
    shared = dict(ropeP=_rope_tables(pos_p), ropeS=_rope_tables(pos_s), cst=_consts(),
                  wbig=wbig, wsm=wsm, vec=vec, fin=fin)
    in_maps = [_prep_core_inputs(inp, b, cfg, shared) for b in range(n_cores)]
    res = run_bass_kernel_spmd(nc, in_maps, core_ids=list(range(n_cores)), trace=trace)
    per = [_post_core(r, cfg) for r in res.results]
    outs = []
    for sq in ("p", "s"):
        for n in OUT_ORDER:
            a = np.stack([p[f"{sq}_{n}"] for p in per], 0)
            if n != "y":
                a = np.moveaxis(a, 0, 1)
            outs.append(np.ascontiguousarray(a.astype(np.float32)))
    ordered = [outs[0], outs[10]] + outs[1:10] + outs[11:20]
    dbg = [{kb.dbg_out[k]: np.asarray(r[k]) for k in kb.dbg_out} for r in res.results]
    return tuple(ordered), dbg, res


def kernel(**inputs):
    cfg = Cfg(seq=int(np.asarray(inputs["x_prompt"]).shape[1]))
    outs, _, _ = run_cfg(cfg, inputs, 8)
    return outs


def _B(self, name, shape, dtype):
    d = self.__dict__.setdefault("_bufs", {})
    if name not in d:
        d[name] = self.P.sbuf(name, shape, dtype)
    return d[name]


KB.B = _B


def _mla_init(self):
    P = self.P
    TT = self.cfg.tt
    KT = TT // 128
    self.kaug = self.B("kaug", [128, 4, TT], BF16)
    self.qaug = self.B("qaug", [128, 4, TT], BF16)
    self.vaug = self.B("vaug", [128, KT, 4, 65], BF16)
    self.kseg = [self.B(f"kseg{i}", [128, 4, TT], BF16) for i in range(2)]
    self.vseg = [self.B(f"vseg{i}", [128, KT, 4, 65], BF16) for i in range(2)]
    xb = self.B("xbcs", [128, 8, TT], F32)
    kv = xb.ap().bitcast(BF16).rearrange("p c (a t) -> p (c a) t", a=2)
    self.kseg_alias = [View(xb, kv[:, 0:4, :]), View(xb, kv[:, 4:8, :])]
    va, vb2 = self.B("khat32", [128, 2, TT], F32), self.B("kb32", [128, 2, TT], F32)
    vv = lambda b: b.ap().bitcast(BF16).rearrange("p c t -> p (c t)")[:, 0:KT * 260].rearrange(
        "p (k h d) -> p k h d", k=KT, h=4)
    self.vseg_alias = [View(va, vv(va)), View(vb2, vv(vb2))]
    self.pbuf = [self.B(f"pbuf{i}", [128, TT], BF16) for i in range(3)]
    self.zt = self.B("zt", [128, 260], BF16)
    self.kmax2 = [self.B(f"kmax2_{l}", [32, 4], F32) for l in range(DEPTH)]
    self.ropeT = self.B("ropeT", [64, 2, TT], F32)
    P.op("pool", lambda e: e.memset(self.zt[:, :], 0.0), writes=[self.zt])
    for kb_ in [self.kaug] + self.kseg:
        P.op("pool", lambda e, kb_=kb_: e.memset(kb_[0:32, :, :], 0.0), writes=[kb_])
        P.op("pool", lambda e, kb_=kb_: e.memset(kb_[0:1, :, :], 1.0), writes=[kb_])
    P.op("pool", lambda e: e.memset(self.vaug[:, :, :, 64:65], 1.0), writes=[self.vaug])
    for l in range(DEPTH):
        P.op("pool", lambda e, l=l: e.memset(self.kmax2[l][:, :], 0.0), writes=[self.kmax2[l]])


KB.mla_init = _mla_init


def _key_norm_update(self, l, kbuf, n):
    P = self.P
    TT = self.cfg.tt
    sqk = self.B("sqk", [128, TT], BF16)
    mx = self.B("mxk", [32, 1], F32)
    km = self.kmax2[l]
    for h in range(4):
        P.op("act", lambda e, h=h: e.activation(sqk[:, 0:n], kbuf[:, h, 0:n], AF.Square), reads=[kbuf], writes=[sqk])
        pb = self.bank()
        P.op("pe", lambda e, pb=pb: e.matmul(pb[0:32, 0:n], self.cb("colk", n=32), sqk[:, 0:n], start=True, stop=True),
             reads=[self.cstb, sqk], writes=[pb])
        P.op("dve", lambda e, pb=pb: e.reduce_max(mx[:, 0:1], pb[0:32, 0:n], AX.X), reads=[pb], writes=[mx])
        P.op("dve", lambda e, h=h: e.tensor_tensor(km[:, h:h + 1], km[:, h:h + 1], mx[:, 0:1], ALU.max),
             reads=[km, mx], writes=[km])


KB.key_norm_update = _key_norm_update


def _attend(self, kbuf, vbuf, n, T, own, masked, st):
    for _ in self.attend_units(kbuf, vbuf, n, T, own, masked, st):
        pass


def _attend_units(self, kbuf, vbuf, n, T, own, masked, st):
    P = self.P
    QC = [(j * 128, min(128, T - j * 128)) for j in range((T + 127) // 128)]
    for kt in range((n + 127) // 128):
        nk = min(128, n - kt * 128)
        q0 = kt * 128 if own else 0
        for h in range(4):
            S = self.bank()
            P.op("pe", lambda e, S=S, kt=kt, nk=nk, h=h, q0=q0: e.matmul(
                S[0:nk, 0:T - q0], kbuf[:, h, kt * 128:kt * 128 + nk], self.qaug[:, h, q0:T], start=True, stop=True),
                reads=[kbuf, self.qaug], writes=[S])
            pt = self.pbuf[st["pi"] % 3]
            st["pi"] += 1
            P.op("act", lambda e, S=S, pt=pt, nk=nk, q0=q0: e.activation(
                pt[0:nk, 0:T - q0], S[0:nk, 0:T - q0], AF.Exp, scale=float(A_SCALE)), reads=[S], writes=[pt])
            if own and masked and nk == 128:
                P.op("dve", lambda e, pt=pt: e.memset(pt[64:128, 0:64], 0.0), writes=[pt])
            for j, (qs, qn) in enumerate(QC):
                if qs < q0:
                    continue
                ob = self.ps[6 + j]
                last = own and (kt == j)
                P.op("pe", lambda e, ob=ob, pt=pt, nk=nk, qs=qs, qn=qn, q0=q0, h=h, kt=kt, last=last: e.matmul(
                    ob[0:qn, h * 65:(h + 1) * 65], pt[0:nk, qs - q0:qs - q0 + qn], vbuf[0:nk, kt, h, :],
                    start=False, stop=last, skip_group_check=True),
                    reads=[pt, vbuf], writes=[ob])
        yield


KB.attend = _attend
KB.attend_units = _attend_units


def _pump(self, k=None):
    g = self.__dict__.get("_att")
    if g is None:
        return
    k = self._att_k if k is None else k
    for _ in range(k):
        try:
            next(g)
        except StopIteration:
            self._att = None
            return


KB.pump = _pump


def _mla(self, sqn, l, t0, T):
    P = self.P
    cfg = self.cfg
    TT = cfg.tt
    pj = self.pj
    vecl = self.vec[l]
    wsm = self.wsm[l]
    out = self.out[sqn]
    QC = [(j * 128, min(128, T - j * 128)) for j in range((T + 127) // 128)]
    sqA = self.B("sqA", [128, 2, TT], BF16)
    rtq = self.B("rtq", [128, TT], F32)
    rsq = self.B("rsq", [128, TT], F32)
    qn = self.B("qn", [128, 2, TT], BF16)
    sqc = self.B("sqc", [128, TT], BF16)
    cT32 = self.B("cT32", [128, TT], F32)
    cTb = self.B("cTb", [128, TT], BF16)
    krr = self.B("krr", [64, TT], F32)
    tmr = self.B("tmr", [64, TT], F32)
    tq1 = self.B("tq1", [64, TT], F32)
    tq2 = self.B("tq2", [64, TT], F32)
    sqq = self.B("sqq", [128, TT], BF16)
    nq = self.B("nq", [32, TT], F32)
    negk = self.B("negk", [32, 4], F32)
    kaug, qaug, vaug, ropeT = self.kaug, self.qaug, self.vaug, self.ropeT
    P.op("act", lambda e: e.activation(sqA[:, :, 0:T], pj[:, 0:2, 0:T], AF.Square), reads=[pj], writes=[sqA])
    pb = self.bank()
    for c in range(2):
        P.op("pe", lambda e, c=c, pb=pb: e.matmul(pb[:, 0:T], self.cb("ones", n=128), sqA[:, c, 0:T],
                                                  start=(c == 0), stop=(c == 1)), reads=[self.cstb, sqA], writes=[pb])
    P.op("act", lambda e, pb=pb: e.activation(rtq[:, 0:T], pb[:, 0:T], AF.Sqrt, bias=self.epsc[:, 0:1], scale=1.0 / 192.0),
         reads=[pb, self.epsc], writes=[rtq])
    P.op("dve", lambda e: e.reciprocal(rsq[:, 0:T], rtq[:, 0:T]), reads=[rtq], writes=[rsq])
    for c in range(2):
        P.op("dve", lambda e, c=c: e.scalar_tensor_tensor(qn[:, c, 0:T], pj[:, c, 0:T], self.V(l, "gq", c), rsq[:, 0:T],
                                                          op0=ALU.mult, op1=ALU.mult), reads=[pj, vecl, rsq], writes=[qn])
    P.op("act", lambda e: e.activation(sqc[:, 0:T], pj[:, 2, 0:T], AF.Square), reads=[pj], writes=[sqc])
    pb2 = self.bank()
    P.op("pe", lambda e: e.matmul(pb2[:, 0:T], self.cb("ones", n=128), sqc[:, 0:T], start=True, stop=True),
         reads=[self.cstb, sqc], writes=[pb2])
    P.op("act", lambda e: e.activation(rtq[:, 0:T], pb2[:, 0:T], AF.Sqrt, bias=self.epsc[:, 0:1], scale=1.0 / 128.0),
         reads=[pb2, self.epsc], writes=[rtq])
    P.op("dve", lambda e: e.reciprocal(rsq[:, 0:T], rtq[:, 0:T]), reads=[rtq], writes=[rsq])
    P.op("dve", lambda e: e.scalar_tensor_tensor(cT32[:, 0:T], pj[:, 2, 0:T], self.V(l, "gkv"), rsq[:, 0:T],
                                                 op0=ALU.mult, op1=ALU.mult), reads=[pj, vecl, rsq], writes=[cT32])
    P.op("act", lambda e: e.activation(cTb[:, 0:T], cT32[:, 0:T], AF.Identity), reads=[cT32], writes=[cTb])
    self.out_handles.append(P.dma("pool", out["ckvT"][l, :, t0:t0 + T], cT32[:, 0:T], reads=[cT32], owner=cT32))
    R = slice(32, 64)
    P.op("dve", lambda e: e.tensor_tensor(krr[R, 0:T], pj[R, 3, 0:T], ropeT[R, 0, 0:T], ALU.mult),
         reads=[pj, ropeT], writes=[krr])
    P.op("dve", lambda e: e.tensor_tensor(tmr[R, 0:T], pj[R, 4, 0:T], ropeT[R, 1, 0:T], ALU.mult),
         reads=[pj, ropeT], writes=[tmr])
    P.op("dve", lambda e: e.tensor_tensor(krr[R, 0:T], krr[R, 0:T], tmr[R, 0:T], ALU.add), reads=[krr, tmr], writes=[krr])
    self.out_handles.append(P.dma("pool", out["kropeT"][l, :, t0:t0 + T], krr[R, 0:T], reads=[krr], owner=krr))
    P.op("dve", lambda e: e.tensor_copy(kaug[R, :, 0:T], krr[R, 0:T].unsqueeze(1).to_broadcast([32, 4, T])),
         reads=[krr], writes=[kaug])
    if getattr(cfg, "cut", 99) == 1:
        P.op("pool", lambda e: e.memset(self.mixT[:, 0:2, 0:T], 0.0), writes=[self.mixT])
        return
    for h in range(4):
        pbk = self.bank()
        o = SOFF["wuk"] + h * 128
        P.op("pe", lambda e, pbk=pbk, o=o: e.matmul(pbk[:, 0:T], wsm[:, o:o + 128], cTb[:, 0:T], start=True, stop=True),
             reads=[wsm, cTb], writes=[pbk])
        self.evac(kaug[64:128, h, 0:T], pbk[64:128, 0:T], [pbk], [kaug])
    for kt in range((T + 127) // 128):
        nk = min(128, T - kt * 128)
        pbv = self.bank()
        o = SOFF["wuv"]
        P.op("pe", lambda e, pbv=pbv, kt=kt, nk=nk, o=o: e.matmul(
            pbv[0:nk, 0:256], cTb[:, kt * 128:kt * 128 + nk], wsm[:, o:o + 256], start=True, stop=True),
            reads=[cTb, wsm], writes=[pbv])
        self.evac(vaug[0:nk, kt, :, 0:64], pbv[0:nk, 0:256].rearrange("p (h d) -> p h d", h=4), [pbv], [vaug])
    if getattr(cfg, "cut", 99) == 2:
        P.op("pool", lambda e: e.memset(self.mixT[:, 0:2, 0:T], 0.0), writes=[self.mixT])
        return
    segs = []
    if sqn == "s":
        for h in range(4):
            pass
        P.op("pool", lambda e: e.memset(self.kmax2[l][:, :], 0.0), writes=[self.kmax2[l]])
        cpb = self.B("cpb", [128, TT], BF16)
        kps = self.B("kps", [64, TT], BF16)
        for si in range(PAST // TT):
            k0 = si * TT
            if si < 2:
                ks, vs = self.kseg[si], self.vseg[si]
            else:
                ks, vs = self.kseg_alias[si - 2], self.vseg_alias[si - 2]
                P.op("pool", lambda e, ks=ks: e.memset(ks[0:32, :, :], 0.0), writes=[ks])
                P.op("pool", lambda e, ks=ks: e.memset(ks[0:1, :, :], 1.0), writes=[ks])
            P.op("pool", lambda e, vs=vs: e.memset(vs[:, :, :, 64:65], 1.0), writes=[vs])
            P.dma("pool", cpb[:, 0:TT], self.s_in["ckvT"][l, :, k0:k0 + TT], writes=[cpb])
            P.dma("pool", kps[R, 0:TT], self.s_in["kropeT"][l, :, k0:k0 + TT], writes=[kps])
            P.op("dve", lambda e, ks=ks: e.tensor_copy(ks[R, :, 0:TT], kps[R, 0:TT].unsqueeze(1).to_broadcast([32, 4, TT])),
                 reads=[kps], writes=[ks])
            for h in range(4):
                pbk = self.bank()
                o = SOFF["wuk"] + h * 128
                P.op("pe", lambda e, pbk=pbk, o=o: e.matmul(pbk[:, 0:TT], wsm[:, o:o + 128], cpb[:, 0:TT], start=True, stop=True),
                     reads=[wsm, cpb], writes=[pbk])
                self.evac(ks[64:128, h, 0:TT], pbk[64:128, 0:TT], [pbk], [ks])
            for kt in range(TT // 128):
                pbv = self.bank()
                o = SOFF["wuv"]
                P.op("pe", lambda e, pbv=pbv, kt=kt, o=o: e.matmul(
                    pbv[:, 0:256], cpb[:, kt * 128:(kt + 1) * 128], wsm[:, o:o + 256], start=True, stop=True),
                    reads=[cpb, wsm], writes=[pbv])
                self.evac(vs[:, kt, :, 0:64], pbv[:, 0:256].rearrange("p (h d) -> p h d", h=4), [pbv], [vs])
            self.key_norm_update(l, ks, TT)
            segs.append((ks, vs, TT))
    self.key_norm_update(l, kaug, T)
    P.op("act", lambda e: e.activation(negk[:, :], self.kmax2[l][:, :], AF.Sqrt), reads=[self.kmax2[l]], writes=[negk])
    P.op("dve", lambda e: e.tensor_scalar(negk[:, :], negk[:, :], -1.0, None, op0=ALU.mult), reads=[negk], writes=[negk])
    if getattr(cfg, "cut", 99) == 3:
        P.op("pool", lambda e: e.memset(self.mixT[:, 0:2, 0:T], 0.0), writes=[self.mixT])
        return
    for h in range(4):
        pm = self.bank()
        psw = self.bank()
        base = SOFF["wuq"]
        for kc in range(2):
            kp = 128 if kc == 0 else 64
            o = base + (kc * 4 + h) * 192
            P.op("pe", lambda e, pm=pm, kc=kc, kp=kp, o=o: e.matmul(
                pm[:, 0:T], wsm[0:kp, o:o + 128], qn[0:kp, kc, 0:T], start=(kc == 0), stop=(kc == 1)),
                reads=[wsm, qn], writes=[pm])
        for kc in range(2):
            kp = 128 if kc == 0 else 64
            o = base + (kc * 4 + h) * 192 + 128
            P.op("pe", lambda e, psw=psw, kc=kc, kp=kp, o=o: e.matmul(
                psw[0:64, 0:T], wsm[0:kp, o:o + 64], qn[0:kp, kc, 0:T], start=(kc == 0), stop=(kc == 1)),
                reads=[wsm, qn], writes=[psw])
        sub = getattr(cfg, "sub", 99)
        if sub < 1:
            continue
        P.op("act", lambda e, pm=pm, h=h: e.activation(qaug[64:128, h, 0:T], pm[64:128, 0:T], AF.Identity),
             reads=[pm], writes=[qaug])
        if sub < 2:
            continue
        P.op("dve", lambda e, pm=pm: e.tensor_tensor(tq1[R, 0:T], pm[R, 0:T], ropeT[R, 0, 0:T], ALU.mult),
             reads=[pm, ropeT], writes=[tq1])
        P.op("dve", lambda e, psw=psw: e.tensor_tensor(tq2[R, 0:T], psw[R, 0:T], ropeT[R, 1, 0:T], ALU.mult),
             reads=[psw, ropeT], writes=[tq2])
        P.op("dve", lambda e, h=h: e.tensor_tensor(qaug[R, h, 0:T], tq1[R, 0:T], tq2[R, 0:T], ALU.add),
             reads=[tq1, tq2], writes=[qaug])
        if sub < 3:
            continue
        P.op("act", lambda e, pm=pm: e.activation(sqq[:, 0:T], pm[:, 0:T], AF.Square), reads=[pm], writes=[sqq])
        pn = self.bank()
        P.op("pe", lambda e, pn=pn: e.matmul(pn[0:32, 0:T], self.cb("col0", n=32), sqq[:, 0:T], start=True, stop=True),
             reads=[self.cstb, sqq], writes=[pn])
        if sub < 4:
            continue
        P.op("act", lambda e, pn=pn: e.activation(nq[:, 0:T], pn[0:32, 0:T], AF.Sqrt), reads=[pn], writes=[nq])
        if sub < 5:
            continue
        P.op("dve", lambda e, h=h: e.tensor_scalar(qaug[0:32, h, 0:T], nq[:, 0:T], negk[:, h:h + 1], None, op0=ALU.mult),
             reads=[nq, negk], writes=[qaug])
    if getattr(cfg, "cut", 99) == 4:
        P.op("pool", lambda e: e.memset(self.mixT[:, 0:2, 0:T], 0.0), writes=[self.mixT])
        return
    if sqn == "p":
        P.dma("sp", self.kT_d[l][:, :, t0:t0 + T], kaug[:, :, 0:T], reads=[kaug], writes=[self.kT_d[l]])
        for kt in range((T + 127) // 128):
            nk = min(128, T - kt * 128)
            P.dma("sp", self.v_d[l][t0 + kt * 128:t0 + kt * 128 + nk, :],
                  vaug[0:nk, kt, :, :].rearrange("p h d -> p (h d)"), reads=[vaug], writes=[self.v_d[l]])
    if getattr(cfg, "cut", 99) == 5:
        P.op("pool", lambda e: e.memset(self.mixT[:, 0:2, 0:T], 0.0), writes=[self.mixT])
        return
    for j, (qs, qn_) in enumerate(QC):
        ob = self.ps[6 + j]
        P.op("pe", lambda e, ob=ob, qn_=qn_: e.matmul(ob[0:qn_, 0:260], self.zt[0:1, 0:qn_], self.zt[0:1, 0:260],
                                                      start=True, stop=False, skip_group_check=True),
             reads=[self.zt], writes=[ob])
    st = self.__dict__.setdefault("_attst", {"pi": 0, "si": 0})
    prev = []
    if sqn == "p" and t0 > 0:
        prev = [(0, N_META)] + [(N_META + i * TT, TT) for i in range((t0 - N_META) // TT)]

    def load_seg(k0, n):
        ks, vs = self.kseg[st["si"] % 2], self.vseg[st["si"] % 2]
        st["si"] += 1
        P.dma("sp", ks[:, :, 0:n], self.kT_d[l][:, :, k0:k0 + n], reads=[self.kT_d[l]], writes=[ks])
        nkt = (n + 127) // 128
        if n >= 128:
            P.dma("sp", vs[:, 0:nkt, :, :].rearrange("p k h d -> p k (h d)"),
                  self.v_d[l][k0:k0 + n, :].rearrange("(k p) c -> p k c", p=128), reads=[self.v_d[l]], writes=[vs])
        else:
            P.dma("sp", vs[0:n, 0, :, :].rearrange("p h d -> p (h d)"), self.v_d[l][k0:k0 + n, :],
                  reads=[self.v_d[l]], writes=[vs])
        return ks, vs

    def gen():
        nxt = load_seg(*prev[0]) if prev else None
        for i, (k0, n) in enumerate(prev):
            ks, vs = nxt
            nxt = load_seg(*prev[i + 1]) if i + 1 < len(prev) else None
            yield from self.attend_units(ks, vs, n, T, False, False, st)
        for (ks, vs, n) in segs:
            yield from self.attend_units(ks, vs, n, T, False, False, st)
        yield from self.attend_units(kaug, vaug, T, T, True, sqn == "p", st)
        self.mla_finish(l, T)
        yield

    n_units = sum((n + 127) // 128 for (_, n) in prev) + sum((n + 127) // 128 for (_, _, n) in segs) + (T + 127) // 128 + 1
    self._att = gen()
    self._att_k = max(1, -(-n_units // 48))
    if sqn != "p" or not getattr(cfg, "interleave", True):
        self.pump(10 ** 9)


def _mla_finish(self, l, T):
    P = self.P
    vecl = self.vec[l]
    QC = [(j * 128, min(128, T - j * 128)) for j in range((T + 127) // 128)]
    rden = self.B("rden", [128, 4], F32)
    on = self.B("on", [128, 4, 64], F32)
    junk = self.B("junkA", [128, 256], F32)
    ss = self.B("ssA", [128, 1], F32)
    rts = self.B("rtsA", [128, 1], F32)
    yat = self.B("yat", [128, 256], BF16)
    for j, (qs, qn_) in enumerate(QC):
        ob = self.ps[6 + j]
        obv = ob[0:qn_, 0:260].rearrange("p (h d) -> p h d", h=4)
        P.op("dve", lambda e, obv=obv, qn_=qn_: e.reciprocal(rden[0:qn_, :], obv[:, :, 64]), reads=[ob], writes=[rden])
        P.op("dve", lambda e, obv=obv, qn_=qn_: e.tensor_tensor(
            on[0:qn_, :, :], obv[:, :, 0:64], rden[0:qn_, :].unsqueeze(2).to_broadcast([qn_, 4, 64]), ALU.mult),
            reads=[ob, rden], writes=[on])
        P.op("act", lambda e, qn_=qn_: e.activation(junk[0:qn_, :], on[0:qn_, :, :].rearrange("p h d -> p (h d)"),
                                                    AF.Square, accum_out=ss[0:qn_, 0:1]), reads=[on], writes=[junk, ss])
        P.op("act", lambda e, qn_=qn_: e.activation(rts[0:qn_, :], ss[0:qn_, :], AF.Sqrt, bias=self.epsc[0:qn_, 0:1],
                                                    scale=1.0 / 256.0), reads=[ss, self.epsc], writes=[rts])
        P.op("dve", lambda e, qn_=qn_: e.reciprocal(rts[0:qn_, :], rts[0:qn_, :]), reads=[rts], writes=[rts])
        P.op("act", lambda e, qn_=qn_: e.activation(yat[0:qn_, :], on[0:qn_, :, :].rearrange("p h d -> p (h d)"),
                                                    AF.Identity, scale=rts[0:qn_, 0:1]), reads=[on, rts], writes=[yat])
        for c in range(2):
            pbT = self.bank()
            P.op("pe", lambda e, pbT=pbT, c=c, qn_=qn_: e.matmul(
                pbT[:, 0:qn_], yat[0:qn_, c * 128:(c + 1) * 128], self.cb("ident", rows=slice(0, qn_), n=qn_),
                start=True, stop=True), reads=[yat, self.cstb], writes=[pbT])
            P.op("dve", lambda e, pbT=pbT, c=c, qs=qs, qn_=qn_: e.tensor_scalar(
                self.mixT[:, c, qs:qs + qn_], pbT[:, 0:qn_], self.V(l, "gout", c), None, op0=ALU.mult),
                reads=[pbT, vecl], writes=[self.mixT])


KB.mla = _mla
KB.mla_finish = _mla_finish


def _scan_init(self):
    P = self.P
    TT = self.cfg.tt
    self.rstm = self.B("rstm", [128, TT], F32)
    P.op("pool", lambda e: e.memset(self.rstm[:, :], 1.0), writes=[self.rstm])
    P.op("pool", lambda e: e.memset(self.rstm.ap().rearrange("p (b l) -> p b l", l=64)[:, :, 0:1], 0.0),
         writes=[self.rstm])
    self.S32 = {m: [self.B(f"S32{m}{l}", [128, 2, 64], F32) for l in range(DEPTH)] for m in "BCD"}
    self.histC = [self.B(f"histC{l}", [128, 6, 3], F32) for l in range(DEPTH)]
    self.histD = [self.B(f"histD{l}", [128, 6, 3], F32) for l in range(DEPTH)]
    self.shiftB = [self.B(f"shiftB{l}", [128, 8], F32) for l in range(DEPTH)]
    for nm in ("SbB", "SbC", "SbD"):
        b = self.B(nm, [128, 2, 2, 64], BF16)
        P.op("pool", lambda e, b=b: e.memset(b.ap(), 0.0), writes=[b])
    for nm in ("opM1", "opM2"):
        b = self.B(nm, [128, 2, 2, TT], BF16)
        P.op("pool", lambda e, b=b: e.memset(b.ap(), 0.0), writes=[b])
    for l in range(DEPTH):
        for b in [self.S32[m][l] for m in "BCD"] + [self.histC[l], self.histD[l], self.shiftB[l]]:
            P.op("pool", lambda e, b=b: e.memset(b.ap(), 0.0), writes=[b])


KB.scan_init = _scan_init


def _rows_prep(self, g8, T, pfx):
    P = self.P
    TT = self.cfg.tt
    blks = blocks_of(T)
    L = blks[0][1]
    NB = len(blks)
    gc8 = self.B(pfx + "gc8", [8, TT], F32)
    egc8 = self.B(pfx + "egc8", [8, TT], F32)
    edl8 = self.B(pfx + "edl8", [8, TT], F32)
    A8 = self.B(pfx + "A8", [8, TT], F32)
    NG8 = self.B(pfx + "NG8", [8, TT], F32)
    B8 = self.B(pfx + "B8", [8, 4, TT], F32)
    P.op("dve", lambda e: e.tensor_tensor_scan(gc8[:, 0:T], self.rstm[0:8, 0:T], g8[0:8, 0:T], 0.0, ALU.mult, ALU.add),
         reads=[self.rstm, g8], writes=[gc8])
    P.op("act", lambda e: e.activation(egc8[:, 0:T], gc8[:, 0:T], AF.Exp), reads=[gc8], writes=[egc8])
    gv = gc8[:, 0:T].rearrange("p (b l) -> p b l", l=L)
    P.op("dve", lambda e: e.tensor_tensor(edl8[:, 0:T].rearrange("p (b l) -> p b l", l=L),
                                          gv[:, :, L - 1:L].to_broadcast([8, NB, L]), gv, ALU.subtract),
         reads=[gc8], writes=[edl8])
    P.op("act", lambda e: e.activation(edl8[:, 0:T], edl8[:, 0:T], AF.Exp), reads=[edl8], writes=[edl8])
    c8 = slice(0, 8)
    P.op("dve", lambda e: e.tensor_scalar(A8[:, 0:T], gc8[:, 0:T], self.c32("m1", c8), self.c32("m2", c8),
                                          op0=ALU.mult, op1=ALU.add), reads=[gc8, self.cst], writes=[A8])
    P.op("dve", lambda e: e.tensor_scalar(NG8[:, 0:T], gc8[:, 0:T], self.c32("nm2", c8), self.c32("m1", c8),
                                          op0=ALU.mult, op1=ALU.add), reads=[gc8, self.cst], writes=[NG8])
    for h in range(4):
        P.op("dve", lambda e, h=h: e.tensor_scalar(B8[:, h, 0:T], NG8[:, 0:T], self.c32("selh", c8, 1, h), None,
                                                   op0=ALU.mult), reads=[NG8, self.cst], writes=[B8])
    return dict(gc8=gc8, egc8=egc8, edl8=edl8, A8=A8, B8=B8)


KB.rows_prep = _rows_prep


def _bcast_rows(self, rows8, T, c):
    P = self.P
    pb = self.bank()
    o = c * 128
    P.op("pe", lambda e, pb=pb, o=o: e.matmul(pb[:, 0:T], self.cst[0:8, o:o + 128], rows8[0:8, 0:T], start=True, stop=True),
         reads=[self.cst, rows8], writes=[pb])
    return pb


KB.bcast_rows = _bcast_rows


def _decay_exp(self, pb, col0, rp, b0, L, mask, transposed, out_ap, out_buf):
    P = self.P
    A8, B8 = rp["A8"], rp["B8"]
    mo = COFF[mask]
    for h in range(4):
        oc = col0 + h * L
        if transposed:
            P.op("pe", lambda e, h=h, oc=oc: e.matmul(pb[0:L, oc:oc + L], B8[0:8, h, b0:b0 + L], A8[0:8, b0:b0 + L],
                                                      start=True, stop=False, skip_group_check=True),
                 reads=[A8, B8], writes=[pb])
        else:
            P.op("pe", lambda e, h=h, oc=oc: e.matmul(pb[0:L, oc:oc + L], A8[0:8, b0:b0 + L], B8[0:8, h, b0:b0 + L],
                                                      start=True, stop=False, skip_group_check=True),
                 reads=[A8, B8], writes=[pb])
        P.op("pe", lambda e, oc=oc: e.matmul(pb[0:L, oc:oc + L], self.cb("ident", slice(0, L), L),
                                             self.cstb[0:L, mo:mo + L], start=False, stop=True, skip_group_check=True),
             reads=[self.cstb], writes=[pb])
    P.op("act", lambda e: e.activation(out_ap, pb[0:L, col0:col0 + 4 * L].rearrange("p (h l) -> p h l", h=4), AF.Exp),
         reads=[pb], writes=[out_buf])


KB.decay_exp = _decay_exp


def _to_tm(self, src, T, b0, L, dst_ap, dst_buf):
    P = self.P
    pb = self.bank()
    for c in range(2):
        P.op("pe", lambda e, c=c, pb=pb: e.matmul(pb[0:L, c * 128:(c + 1) * 128], src[:, c, b0:b0 + L],
                                                  self.cb("ident", n=128), start=True, stop=True, skip_group_check=True),
             reads=[src, self.cstb], writes=[pb])
    self.evac(dst_ap, pb[0:L, 0:256].rearrange("p (h d) -> p h d", h=4), [pb], [dst_buf])


KB.to_tm = _to_tm


def _conv4(self, l, T, nch, wname, bname, out_buf):
    P = self.P
    TT = self.cfg.tt
    pj = self.pj
    vecl = self.vec[l]
    w = lambda i, c: self.V(l, wname, c * 4 + i)
    for c in range(nch):
        if bname is not None:
            self.ACT(out_buf[:, c, 0:T], pj[:, c, 0:T], AF.Identity, [pj, vecl], [out_buf], bias=self.V(l, bname, c), scale=w(0, c))
        else:
            self.ACT(out_buf[:, c, 0:T], pj[:, c, 0:T], AF.Identity, [pj, vecl], [out_buf], scale=w(0, c))
    for i in (1, 2, 3):
        for c in range(nch):
            self.STT(out_buf[:, c, 0:T], pj[:, c, i:i + T], w(i, c), out_buf[:, c, 0:T], ALU.mult, ALU.add,
                     [pj, out_buf, vecl], [out_buf])
    self.ACT(out_buf[:, 0:nch, 0:T], out_buf[:, 0:nch, 0:T], AF.Silu, [out_buf], [out_buf])


KB.conv4 = _conv4


def _softplus_rows(self, dst, src_ap, src_buf, bias_ap, bias_buf, T):
    P = self.P
    P.op("act", lambda e: e.activation(dst[0:8, 0:T], src_ap, AF.Exp, bias=bias_ap), reads=[src_buf, bias_buf], writes=[dst])
    P.op("act", lambda e: e.activation(dst[0:8, 0:T], dst[0:8, 0:T], AF.Ln, bias=self.epsc[0:8, 2:3]),
         reads=[dst, self.epsc], writes=[dst])


KB.softplus_rows = _softplus_rows


def _ssd(self, sqn, l, t0, T):
    P = self.P
    TT = self.cfg.tt
    pj, pj2, vecl = self.pj, self.pj2, self.vec[l]
    blks = blocks_of(T)
    L = blks[0][1]
    NB = len(blks)
    hist = self.histC[l]
    S32 = self.S32["C"][l]
    xbcs = self.B("xbcs", [128, 8, TT], F32)
    siluz = self.B("siluz", [128, 2, TT], F32)
    dt8 = self.B("dt8", [8, TT], F32)
    adt8 = self.B("adt8", [8, TT], F32)
    xdt = self.B("opA", [128, 2, TT], BF16)
    Cdec = self.B("opB", [128, 2, TT], BF16)
    Bdec = self.B("opC", [128, 2, TT], BF16)
    bm_b = self.B("opD", [128, 2, TT], BF16)
    cm_b = self.B("opM1", [128, 2, 2, TT], BF16)
    eal = self.B("ealC", [128, 2, 8], F32)
    Sb = self.B("SbC", [128, 2, 2, 64], BF16)
    xdt_tm = self.B("tmA", [64, 4, 64], BF16)
    Bdec_tm = self.B("tmB", [64, 4, 64], BF16)
    E1 = self.B("E1", [64, 4, 64], F32)
    GT = self.B("GT", [64, 4, 64], BF16)
    ytm = self.B("otm", [64, TT // 64, 4, 64], BF16)
    y2 = self.B("y2C", [128, 2, TT], F32)
    sqy = self.B("sqyC", [128, 2, TT], BF16)
    P.op("dve", lambda e: e.tensor_copy(pj[:, 0:6, 0:3], hist[:, :, :]), reads=[hist], writes=[pj])
    P.op("dve", lambda e: e.tensor_copy(hist[:, :, :], pj[:, 0:6, T:T + 3]), reads=[pj], writes=[hist])
    self.conv4(l, T, 6, "ccw", "ccb", xbcs)
    P.op("act", lambda e: e.activation(siluz[:, :, 0:T], pj2[:, 0:2, 0:T], AF.Silu), reads=[pj2], writes=[siluz])
    self.softplus_rows(dt8, pj2[0:8, 2, 0:T], pj2, self.V(l, "cdtb", rows=slice(0, 8)), vecl, T)
    P.op("dve", lambda e: e.tensor_scalar(adt8[:, 0:T], dt8[:, 0:T], self.dv[l][0:8, 0:1], None, op0=ALU.mult),
         reads=[dt8, self.dv[l]], writes=[adt8])
    rp = self.rows_prep(adt8, T, "R")
    for c in range(2):
        pbd = self.bcast_rows(dt8, T, c)
        P.op("dve", lambda e, c=c, pbd=pbd: e.tensor_tensor(xdt[:, c, 0:T], xbcs[:, c, 0:T], pbd[:, 0:T], ALU.mult),
             reads=[xbcs, pbd], writes=[xdt])
        pbe = self.bcast_rows(rp["egc8"], T, c)
        P.op("dve", lambda e, c=c, pbe=pbe: e.tensor_tensor(Cdec[:, c, 0:T], xbcs[:, 4 + c, 0:T], pbe[:, 0:T], ALU.mult),
             reads=[xbcs, pbe], writes=[Cdec])
        P.op("dve", lambda e, c=c, pbe=pbe: e.tensor_copy(
            eal[:, c, 0:NB], pbe[:, 0:T].rearrange("p (b l) -> p b l", l=L)[:, :, L - 1]), reads=[pbe], writes=[eal])
        pbl = self.bcast_rows(rp["edl8"], T, c)
        P.op("dve", lambda e, c=c, pbl=pbl: e.tensor_tensor(Bdec[:, c, 0:T], xbcs[:, 2 + c, 0:T], pbl[:, 0:T], ALU.mult),
             reads=[xbcs, pbl], writes=[Bdec])
    P.op("act", lambda e: e.activation(bm_b[:, :, 0:T], xbcs[:, 2:4, 0:T], AF.Identity), reads=[xbcs], writes=[bm_b])
    self.mask_copy("act", cm_b, xbcs[0:64, 4:6, 0:T], xbcs[64:128, 4:6, 0:T], T, [xbcs])
    self.sb_copy(Sb, S32)
    for bi, (b0, _) in enumerate(blks):
        self.pump()
        self.to_tm(xdt, T, b0, L, xdt_tm[0:L, :, :], xdt_tm)
        self.to_tm(Bdec, T, b0, L, Bdec_tm[0:L, :, :], Bdec_tm)
        pr = self.bankb()
        for h in range(4):
            c, e_ = divmod(h, 2)
            hp = slice(64 * e_, 64 * e_ + 64)
            self.MM(pr[0:L, h * L:(h + 1) * L], bm_b[:, c, b0:b0 + L], cm_b[:, c, e_, b0:b0 + L], True, True, [bm_b, cm_b], [pr])
        if l == 0 and t0 > 0 and "Rpre" in self.cfg.dbg:
            Rp = self.B(f"Rpre{bi}", [64, 256], F32)
            P.op("dve", lambda e, pr=pr, Rp=Rp: e.tensor_copy(Rp[:, :], pr[0:64, 0:256]), reads=[pr], writes=[Rp])
            self.dbg("Rpre", Rp, Rp[:, :], [64, 256])
        self.decay_exp(pr, 256, rp, b0, L, "mn_inclT", True, E1[0:L, :, 0:L], E1)
        P.op("dve", lambda e, pr=pr: e.tensor_tensor(GT[0:L, :, 0:L], pr[0:L, 0:4 * L].rearrange("p (h l) -> p h l", h=4),
                                                     E1[0:L, :, 0:L], ALU.mult), reads=[pr, E1], writes=[GT])
        if l == 0 and t0 > 0 and bi == 0 and "Rraw" in self.cfg.dbg:
            Rr = self.B("Rraw", [64, 256], F32)
            P.op("dve", lambda e, pr=pr: e.tensor_copy(Rr[:, :], pr[0:64, 0:256]), reads=[pr], writes=[Rr])
            self.dbg("Rraw", Rr, Rr[:, :], [64, 256])
            self.dbg("bm_b", bm_b, bm_b[:, :, 0:64], [128, 2, 64])
            self.dbg("cm_b", cm_b, cm_b[:, :, 0:64], [128, 2, 64])
        if l == 0 and t0 > 0 and bi == 0:
            self.dbg("E1", E1, E1[:, :, :], [64, 4, 64])
            self.dbg("GT", GT, GT[:, :, :], [64, 4, 64])
            self.dbg("xdt_tm", xdt_tm, xdt_tm[:, :, :], [64, 4, 64])
            self.dbg("gc8", rp["gc8"], rp["gc8"][:, :], [8, TT])
            self.dbg("A8", rp["A8"], rp["A8"][:, :], [8, TT])
            self.dbg("B8", rp["B8"], rp["B8"][:, :, :], [8, 4, TT])
        py = self.bankb()
        for h in range(4):
            c, e_ = divmod(h, 2)
            hp = slice(64 * e_, 64 * e_ + 64)
            P.op("pe", lambda e, h=h, py=py: e.matmul(py[0:L, h * 64:(h + 1) * 64], GT[0:L, h, 0:L], xdt_tm[0:L, h, :],
                                                      start=True, stop=False, skip_group_check=True),
                 reads=[GT, xdt_tm], writes=[py])
            self.MM(py[0:L, h * 64:(h + 1) * 64], Cdec[:, c, b0:b0 + L], Sb[:, c, e_, :], False, True, [Cdec, Sb], [py])
        self.evac(ytm[0:L, bi, :, :], py[0:L, 0:256].rearrange("p (h d) -> p h d", h=4), [py], [ytm])
        pS = self.bankb()
        for h in range(4):
            c, e_ = divmod(h, 2)
            hp = slice(64 * e_, 64 * e_ + 64)
            P.op("pe", lambda e, h=h, c=c, hp=hp, pS=pS: e.matmul(
                pS[hp, c * 64:(c + 1) * 64], Bdec_tm[0:L, h, :], xdt_tm[0:L, h, :],
                start=True, stop=True, skip_group_check=True), reads=[Bdec_tm, xdt_tm], writes=[pS])
        for c in range(2):
            P.op("dve", lambda e, c=c, bi=bi, pS=pS: e.scalar_tensor_tensor(
                S32[:, c, :], S32[:, c, :], eal[:, c, bi:bi + 1], pS[:, c * 64:(c + 1) * 64], op0=ALU.mult, op1=ALU.add),
                reads=[S32, eal, pS], writes=[S32])
        self.sb_copy(Sb, S32)
    pys = [self.bankb(), self.bankb()]
    for bi, (b0, _) in enumerate(blks):
        for c in range(2):
            P.op("pe", lambda e, c=c, bi=bi, b0=b0: e.matmul(
                pys[c][:, b0:b0 + L], ytm[0:L, bi, 2 * c:2 * c + 2, :].rearrange("p h d -> p (h d)"),
                self.cb("ident", slice(0, L), L), start=(bi == 0), stop=True, skip_group_check=True),
                reads=[ytm, self.cstb], writes=[pys[c]])
    for c in range(2):
        P.op("dve", lambda e, c=c: e.scalar_tensor_tensor(y2[:, c, 0:T], xbcs[:, c, 0:T], self.V(l, "cd", c), pys[c][:, 0:T],
                                                          op0=ALU.mult, op1=ALU.add), reads=[xbcs, vecl, pys[c]], writes=[y2])
    P.op("dve", lambda e: e.tensor_tensor(y2[:, :, 0:T], y2[:, :, 0:T], siluz[:, :, 0:T], ALU.mult),
         reads=[y2, siluz], writes=[y2])
    P.op("act", lambda e: e.activation(sqy[:, :, 0:T], y2[:, :, 0:T], AF.Square), reads=[y2], writes=[sqy])
    pn = self.bank()
    for c in range(2):
        P.op("pe", lambda e, c=c: e.matmul(pn[:, 0:T], self.cb("ones", n=128), sqy[:, c, 0:T], start=(c == 0), stop=(c == 1)),
             reads=[self.cstb, sqy], writes=[pn])
    rtc = self.B("rtC", [128, TT], F32)
    P.op("act", lambda e: e.activation(rtc[:, 0:T], pn[:, 0:T], AF.Sqrt, bias=self.epsc[:, 0:1], scale=1.0 / 256.0),
         reads=[pn, self.epsc], writes=[rtc])
    P.op("dve", lambda e: e.reciprocal(rtc[:, 0:T], rtc[:, 0:T]), reads=[rtc], writes=[rtc])
    for c in range(2):
        P.op("dve", lambda e, c=c: e.scalar_tensor_tensor(self.mixT[:, 4 + c, 0:T], y2[:, c, 0:T], self.V(l, "cgn", c),
                                                          rtc[:, 0:T], op0=ALU.mult, op1=ALU.mult),
             reads=[y2, vecl, rtc], writes=[self.mixT])


    if l == 0 and t0 > 0:
        self.dbg("mixC", self.mixT, self.mixT[:, 4:6, 0:T], [128, 2, T])
        self.dbg("ytm", ytm, ytm[:, 0:4, :, :], [64, 4, 4, 64])
    return


KB.ssd = _ssd


def _MM(self, out, lhsT, rhs, start, stop, reads, writes):
    self.P.op("pe", lambda e: e.matmul(out, lhsT, rhs, start=start, stop=stop, skip_group_check=True),
              reads=reads, writes=writes)


def _TT(self, eng, out, in0, in1, op, reads, writes):
    self.P.op(eng, lambda e: e.tensor_tensor(out, in0, in1, op), reads=reads, writes=writes)


def _STT(self, out, in0, scalar, in1, op0, op1, reads, writes):
    self.P.op("dve", lambda e: e.scalar_tensor_tensor(out, in0, scalar, in1, op0=op0, op1=op1), reads=reads, writes=writes)


def _TS(self, eng, out, in0, s1, s2, op0, op1, reads, writes):
    if s2 is None:
        self.P.op(eng, lambda e: e.tensor_scalar(out, in0, s1, None, op0=op0), reads=reads, writes=writes)
    else:
        self.P.op(eng, lambda e: e.tensor_scalar(out, in0, s1, s2, op0=op0, op1=op1), reads=reads, writes=writes)


def _ACT(self, out, in_, func, reads, writes, bias=None, scale=None):
    kw = {}
    if bias is not None:
        kw["bias"] = bias
    if scale is not None:
        kw["scale"] = scale
    self.P.op("act", lambda e: e.activation(out, in_, func, **kw), reads=reads, writes=writes)


KB.MM, KB.TT, KB.STT, KB.TS, KB.ACT = _MM, _TT, _STT, _TS, _ACT


def _tri_inverse(self, NM0, L):
    P = self.P
    nlev = {16: 3, 32: 4, 64: 5}[L]
    NMb = [self.B("NMb0", [64, 2, 4, 64], F32), self.B("NMb1", [64, 2, 4, 64], F32)]
    Pb = [self.B("Pb0", [64, 4, 64], F32), self.B("Pb1", [64, 4, 64], F32)]
    idb = self.cb("ident", slice(0, L), L).unsqueeze(1).to_broadcast([L, 4, L])
    self.TT("dve", Pb[0][0:L, :, 0:L], NM0[0:L, 1, :, 0:L], idb, ALU.add, [NM0, self.cstb], [Pb[0]])
    cur = NM0
    pcur = Pb[0]
    for k in range(1, nlev + 1):
        self.pump()
        nxt = NMb[k % 2]
        pb = self.bankb()
        for h in range(4):
            self.MM(pb[0:L, h * L:(h + 1) * L], cur[0:L, 1, h, 0:L], cur[0:L, 0, h, 0:L], True, True, [cur], [pb])
            self.MM(pb[0:L, 256 + h * L:256 + (h + 1) * L], cur[0:L, 0, h, 0:L], cur[0:L, 1, h, 0:L], True, True, [cur], [pb])
        src = pb[0:L, :].rearrange("p (t x) -> p t x", t=2)[:, :, 0:4 * L].rearrange("p t (h l) -> p t h l", h=4)
        self.evac(nxt[0:L, :, :, 0:L], src, [pb], [nxt])
        pp = self.bankb()
        for h in range(4):
            self.MM(pp[0:L, h * L:(h + 1) * L], nxt[0:L, 0, h, 0:L], pcur[0:L, h, 0:L], True, True, [nxt, pcur], [pp])
        pnx = Pb[k % 2]
        self.TT("dve", pnx[0:L, :, 0:L], pp[0:L, 0:4 * L].rearrange("p (h l) -> p h l", h=4), pcur[0:L, :, 0:L],
                ALU.add, [pp, pcur], [pnx])
        cur, pcur = nxt, pnx
    PTb = self.B("PTb", [64, 4, 64], BF16)
    self.ACT(PTb[0:L, :, 0:L], pcur[0:L, :, 0:L], AF.Identity, [pcur], [PTb])
    return PTb


KB.tri_inverse = _tri_inverse


def _sb_copy(self, Sb, S32):
    self.ACT(Sb[0:64, :, 0, :], S32[0:64, :, :], AF.Identity, [S32], [Sb])
    self.ACT(Sb[64:128, :, 1, :], S32[64:128, :, :], AF.Identity, [S32], [Sb])


def _mask_copy(self, eng, dst, src_ap_lo, src_ap_hi, T, rbufs):
    if eng == "act":
        self.ACT(dst[0:64, :, 0, 0:T], src_ap_lo, AF.Identity, rbufs, [dst])
        self.ACT(dst[64:128, :, 1, 0:T], src_ap_hi, AF.Identity, rbufs, [dst])


KB.sb_copy, KB.mask_copy = _sb_copy, _mask_copy


def _gdn(self, sqn, l, t0, T):
    P = self.P
    TT_ = self.cfg.tt
    pj, pj2, vecl = self.pj, self.pj2, self.vec[l]
    blks = blocks_of(T)
    L = blks[0][1]
    NB = len(blks)
    hist = self.histD[l]
    S32 = self.S32["D"][l]
    qkvs = self.B("xbcs", [128, 8, TT_], F32)
    siluz = self.B("siluz", [128, 2, TT_], F32)
    sq4 = self.B("sq4D", [128, 4, TT_], BF16)
    rn = self.B("rnD", [128, TT_], F32)
    khat32 = self.B("khat32", [128, 2, TT_], F32)
    kb32 = self.B("kb32", [128, 2, TT_], F32)
    qhat_b = self.B("opA", [128, 2, TT_], BF16)
    khat_b = self.B("opM1", [128, 2, 2, TT_], BF16)
    kb_b = self.B("opC", [128, 2, TT_], BF16)
    vb = self.B("opD", [128, 2, TT_], BF16)
    nkbg = self.B("opE", [128, 2, TT_], BF16)
    qdec = self.B("opF", [128, 2, TT_], BF16)
    kdec = self.B("opG", [128, 2, TT_], BF16)
    beta8 = self.B("beta8", [8, TT_], F32)
    sp8 = self.B("dt8", [8, TT_], F32)
    g8 = self.B("adt8", [8, TT_], F32)
    eal = self.B("ealD", [128, 2, 8], F32)
    Sb = self.B("SbD", [128, 2, 2, 64], BF16)
    kdec_tm = self.B("tmA", [64, 4, 64], BF16)
    Ebuf = self.B("E1", [64, 4, 64], F32)
    NM0 = self.B("NM0", [64, 2, 4, 64], F32)
    AqkT = self.B("GT", [64, 4, 64], BF16)
    rhs_s = self.B("rhs_s", [64, 4, 64], BF16)
    vnew_s = self.B("vnew_s", [64, 4, 64], BF16)
    otm = self.B("otm32", [64, TT_ // 64, 4, 64], F32)
    on_b = self.B("otm", [64, TT_ // 64, 4, 64], BF16)
    sqo_buf = self.acc[0]
    sqo = sqo_buf.ap()[0:64, :, :].rearrange("p a t -> p (a t)").rearrange("p (b h d) -> p b h d", h=4, d=64)
    ss = self.B("ssD", [64, 32], F32)
    c8 = slice(0, 8)
    P.op("dve", lambda e: e.tensor_copy(pj[:, 0:6, 0:3], hist[:, :, :]), reads=[hist], writes=[pj])
    P.op("dve", lambda e: e.tensor_copy(hist[:, :, :], pj[:, 0:6, T:T + 3]), reads=[pj], writes=[hist])
    self.conv4(l, T, 6, "dcw", None, qkvs)
    self.ACT(siluz[:, :, 0:T], pj2[:, 0:2, 0:T], AF.Silu, [pj2], [siluz])
    self.ACT(sq4[:, :, 0:T], qkvs[:, 0:4, 0:T], AF.Square, [qkvs], [sq4])
    for c in range(4):
        pb = self.bank()
        self.MM(pb[:, 0:T], self.cb("blk1", n=128), sq4[:, c, 0:T], True, True, [self.cstb, sq4], [pb])
        self.ACT(rn[:, 0:T], pb[:, 0:T], AF.Sqrt, [pb, self.epsc], [rn], bias=self.epsc[:, 0:1])
        P.op("dve", lambda e: e.reciprocal(rn[:, 0:T], rn[:, 0:T]), reads=[rn], writes=[rn])
        if c < 2:
            self.STT(qhat_b[:, c, 0:T], qkvs[:, c, 0:T], 0.125, rn[:, 0:T], ALU.mult, ALU.mult, [qkvs, rn], [qhat_b])
        else:
            self.TT("dve", khat32[:, c - 2, 0:T], qkvs[:, c, 0:T], rn[:, 0:T], ALU.mult, [qkvs, rn], [khat32])
    self.mask_copy("act", khat_b, khat32[0:64, :, 0:T], khat32[64:128, :, 0:T], T, [khat32])
    self.ACT(beta8[0:8, 0:T], pj2[0:8, 2, 0:T], AF.Sigmoid, [pj2], [beta8])
    self.softplus_rows(sp8, pj2[0:8, 3, 0:T], pj2, self.V(l, "ddtb", rows=c8), vecl, T)
    self.TS("dve", g8[0:8, 0:T], sp8[0:8, 0:T], self.dv[l][0:8, 1:2], None, ALU.mult, None, [sp8, self.dv[l]], [g8])
    rp = self.rows_prep(g8, T, "R")
    for c in range(2):
        pbb = self.bcast_rows(beta8, T, c)
        self.TT("dve", kb32[:, c, 0:T], khat32[:, c, 0:T], pbb[:, 0:T], ALU.mult, [khat32, pbb], [kb32])
        self.TT("dve", vb[:, c, 0:T], qkvs[:, 4 + c, 0:T], pbb[:, 0:T], ALU.mult, [qkvs, pbb], [vb])
        pbe = self.bcast_rows(rp["egc8"], T, c)
        self.STT(nkbg[:, c, 0:T], kb32[:, c, 0:T], -1.0, pbe[:, 0:T], ALU.mult, ALU.mult, [kb32, pbe], [nkbg])
        self.TT("dve", qdec[:, c, 0:T], qhat_b[:, c, 0:T], pbe[:, 0:T], ALU.mult, [qhat_b, pbe], [qdec])
        P.op("dve", lambda e, c=c, pbe=pbe: e.tensor_copy(
            eal[:, c, 0:NB], pbe[:, 0:T].rearrange("p (b l) -> p b l", l=L)[:, :, L - 1]), reads=[pbe], writes=[eal])
        pbl = self.bcast_rows(rp["edl8"], T, c)
        self.TT("dve", kdec[:, c, 0:T], khat32[:, c, 0:T], pbl[:, 0:T], ALU.mult, [khat32, pbl], [kdec])
    self.ACT(kb_b[:, :, 0:T], kb32[:, :, 0:T], AF.Identity, [kb32], [kb_b])
    self.sb_copy(Sb, S32)
    for bi, (b0, _) in enumerate(blks):
        bs = slice(b0, b0 + L)
        self.pump()
        self.to_tm(kdec, T, b0, L, kdec_tm[0:L, :, :], kdec_tm)
        HP = [(h // 2, h % 2) for h in range(4)]
        pr = self.bankb()
        for h, (c, hp) in enumerate(HP):
            self.MM(pr[0:L, h * L:(h + 1) * L], kb_b[:, c, bs], khat_b[:, c, hp, bs], True, True, [kb_b, khat_b], [pr])
        self.decay_exp(pr, 256, rp, b0, L, "mn_str", False, Ebuf[0:L, :, 0:L], Ebuf)
        self.STT(NM0[0:L, 0, :, 0:L], pr[0:L, 0:4 * L].rearrange("p (h l) -> p h l", h=4), -1.0, Ebuf[0:L, :, 0:L],
                 ALU.mult, ALU.mult, [pr, Ebuf], [NM0])
        pr = self.bankb()
        for h, (c, hp) in enumerate(HP):
            self.MM(pr[0:L, h * L:(h + 1) * L], khat_b[:, c, hp, bs], kb_b[:, c, bs], True, True, [kb_b, khat_b], [pr])
        self.decay_exp(pr, 256, rp, b0, L, "mn_strT", True, Ebuf[0:L, :, 0:L], Ebuf)
        self.STT(NM0[0:L, 1, :, 0:L], pr[0:L, 0:4 * L].rearrange("p (h l) -> p h l", h=4), -1.0, Ebuf[0:L, :, 0:L],
                 ALU.mult, ALU.mult, [pr, Ebuf], [NM0])
        pr = self.bankb()
        for h, (c, hp) in enumerate(HP):
            self.MM(pr[0:L, h * L:(h + 1) * L], khat_b[:, c, hp, bs], qhat_b[:, c, bs], True, True, [qhat_b, khat_b], [pr])
        self.decay_exp(pr, 256, rp, b0, L, "mn_inclT", True, Ebuf[0:L, :, 0:L], Ebuf)
        self.TT("dve", AqkT[0:L, :, 0:L], pr[0:L, 0:4 * L].rearrange("p (h l) -> p h l", h=4), Ebuf[0:L, :, 0:L],
                ALU.mult, [pr, Ebuf], [AqkT])
        PT = self.tri_inverse(NM0, L)
        self.pump()
        pq = self.bankb()
        for h, (c, hp) in enumerate(HP):
            e_ = hp
            self.MM(pq[0:L, h * 64:(h + 1) * 64], vb[:, c, bs], self.cstb[:, COFF["ident"] + 64 * e_:COFF["ident"] + 64 * e_ + 64],
                    True, False, [vb, self.cstb], [pq])
            self.MM(pq[0:L, h * 64:(h + 1) * 64], nkbg[:, c, bs], Sb[:, c, e_, :], False, True, [nkbg, Sb], [pq])
        self.evac(rhs_s[0:L, :, :], pq[0:L, 0:256].rearrange("p (h d) -> p h d", h=4), [pq], [rhs_s])
        pv = self.bankb()
        for h in range(4):
            self.MM(pv[0:L, h * 64:(h + 1) * 64], PT[0:L, h, 0:L], rhs_s[0:L, h, :], True, True, [PT, rhs_s], [pv])
        self.evac(vnew_s[0:L, :, :], pv[0:L, 0:256].rearrange("p (h d) -> p h d", h=4), [pv], [vnew_s])
        po = self.bankb()
        for h, (c, hp) in enumerate(HP):
            self.MM(po[0:L, h * 64:(h + 1) * 64], qdec[:, c, bs], Sb[:, c, hp, :], True, False, [qdec, Sb], [po])
            self.MM(po[0:L, h * 64:(h + 1) * 64], AqkT[0:L, h, 0:L], vnew_s[0:L, h, :], False, True, [AqkT, vnew_s], [po])
        self.evac(otm[0:L, bi, :, :], po[0:L, 0:256].rearrange("p (h d) -> p h d", h=4), [po], [otm])
        pS = self.bankb()
        for h, (c, hp) in enumerate(HP):
            self.MM(pS[64 * hp:64 * hp + 64, c * 64:(c + 1) * 64], kdec_tm[0:L, h, :], vnew_s[0:L, h, :], True, True, [kdec_tm, vnew_s], [pS])
        for c in range(2):
            self.STT(S32[:, c, :], S32[:, c, :], eal[:, c, bi:bi + 1], pS[:, c * 64:(c + 1) * 64], ALU.mult, ALU.add,
                     [S32, eal, pS], [S32])
        self.sb_copy(Sb, S32)
    ov = otm[0:L, 0:NB, :, :]
    self.TT("dve", sqo[0:L, 0:NB, :, :], ov, ov, ALU.mult, [otm], [sqo_buf])
    P.op("dve", lambda e: e.reduce_sum(ss[0:L, 0:NB * 4], sqo[0:L, 0:NB, :, :].rearrange("p b h d -> p (b h) d"), AX.X),
         reads=[sqo_buf], writes=[ss])
    self.ACT(ss[0:L, 0:NB * 4], ss[0:L, 0:NB * 4], AF.Sqrt, [ss, self.epsc], [ss], bias=self.epsc[0:L, 0:1], scale=1.0 / 64.0)
    P.op("dve", lambda e: e.reciprocal(ss[0:L, 0:NB * 4], ss[0:L, 0:NB * 4]), reads=[ss], writes=[ss])
    self.TT("dve", on_b[0:L, 0:NB, :, :].rearrange("p b h d -> p (b h) d"),
            otm[0:L, 0:NB, :, :].rearrange("p b h d -> p (b h) d"),
            ss[0:L, 0:NB * 4].unsqueeze(2).to_broadcast([L, NB * 4, 64]), ALU.mult, [otm, ss], [on_b])
    pys = [self.bankb(), self.bankb()]
    for bi, (b0, _) in enumerate(blks):
        for c in range(2):
            self.MM(pys[c][:, b0:b0 + L], on_b[0:L, bi, 2 * c:2 * c + 2, :].rearrange("p h d -> p (h d)"),
                    self.cb("ident", slice(0, L), L), True, True, [on_b, self.cstb], [pys[c]])
    for c in range(2):
        self.STT(self.mixT[:, 6 + c, 0:T], pys[c][:, 0:T], self.V(l, "dgn", c), siluz[:, c, 0:T], ALU.mult, ALU.mult,
                 [pys[c], vecl, siluz], [self.mixT])


KB.gdn = _gdn


def _rwkv(self, sqn, l, t0, T):
    P = self.P
    TT_ = self.cfg.tt
    pj, vecl, wsm = self.pj, self.vec[l], self.wsm[l]
    blks = blocks_of(T)
    L = blks[0][1]
    NB = len(blks)
    S32 = self.S32["B"][l]
    shift = self.shiftB[l]
    F = lambda n: self.B(n, [128, 2, TT_], F32)
    H = lambda n: self.B(n, [128, 2, TT_], BF16)
    xm = self.B("xbcs", [128, 8, TT_], F32)
    lw, G, ex, ag, gT, kkr, kmod, bvec, bon = F("y2C"), F("khat32"), F("kb32"), F("agB"), F("gTB"), F("kkrB"), \
        F("kmodB"), F("bvecB"), F("bonB")
    misc = self.B("sq4D", [128, 4, TT_], BF16)
    sqk = self.B("sqyC", [128, 2, TT_], BF16)
    rn = self.B("rnD", [128, TT_], F32)
    At, Rt, Pe, Ke, v_b = H("opA"), H("opD"), H("opE"), H("opF"), H("opG")
    Pt = self.B("opM1", [128, 2, 2, TT_], BF16)
    Kt = self.B("opM2", [128, 2, 2, TT_], BF16)
    V_tm = self.B("tmA", [64, 4, 64], BF16)
    Pe_tm = self.B("tmB", [64, 4, 64], BF16)
    Ke_tm = self.B("tmC", [64, 4, 64], BF16)
    NM0 = self.B("NM0", [64, 2, 4, 64], F32)
    AakT = self.B("GT", [64, 4, 64], BF16)
    ArpT = self.B("ArpT", [64, 4, 64], BF16)
    ArkT = self.B("ArkT", [64, 4, 64], BF16)
    rhs_s = self.B("rhs_s", [64, 4, 64], BF16)
    U_s = self.B("vnew_s", [64, 4, 64], BF16)
    otm = self.B("otm32", [64, TT_ // 64, 4, 64], F32)
    on_b = self.B("otm", [64, TT_ // 64, 4, 64], BF16)
    sqo_buf = self.acc[0]
    sqo = sqo_buf.ap()[0:64, :, :].rearrange("p a t -> p (a t)").rearrange("p (b h d) -> p b h d", h=4, d=64)
    ss = self.B("ssD", [64, 32], F32)
    s1 = self.B("ssB", [64, 32], F32)
    WL = self.B("ealD", [128, 2, 8], F32)
    Sb = self.B("SbB", [128, 2, 2, 64], BF16)
    HP = [(h // 2, h % 2) for h in range(4)]
    P.op("dve", lambda e: e.tensor_copy(pj[:, 0:8, 0:1], shift[:, :].unsqueeze(2)), reads=[shift], writes=[pj])
    P.op("dve", lambda e: e.tensor_copy(shift[:, :].unsqueeze(2), pj[:, 0:8, T:T + 1]), reads=[pj], writes=[shift])
    self.TT("dve", xm[:, :, 0:T], pj[:, 0:8, 0:T], pj[:, 0:8, 1:T + 1], ALU.subtract, [pj], [xm])
    for c in range(8):
        self.STT(xm[:, c, 0:T], xm[:, c, 0:T], self.V(l, "mu", c), pj[:, c, 1:T + 1], ALU.mult, ALU.add, [xm, vecl, pj], [xm])
    self.ACT(misc[0:64, 0, 0:T], xm[0:64, 6, 0:T], AF.Tanh, [xm], [misc])
    self.ACT(misc[64:128, 1, 0:T], xm[64:128, 6, 0:T], AF.Identity, [xm], [misc])
    self.ACT(misc[:, 2, 0:T], xm[:, 7, 0:T], AF.Sigmoid, [xm], [misc])
    ow, og = SOFF["w2a2"], SOFF["g2"]
    for c in range(2):
        pb = self.bank()
        self.MM(pb[:, 0:T], wsm[0:64, ow + 128 * c:ow + 128 * c + 128], misc[0:64, 0, 0:T], True, True, [wsm, misc], [pb])
        self.ACT(lw[:, c, 0:T], pb[:, 0:T], AF.Sigmoid, [pb, vecl], [lw], bias=self.V(l, "w0", c))
        pb = self.bank()
        self.MM(pb[:, 0:T], wsm[64:128, ow + 128 * c:ow + 128 * c + 128], misc[64:128, 1, 0:T], True, True, [wsm, misc], [pb])
        self.ACT(ag[:, c, 0:T], pb[:, 0:T], AF.Sigmoid, [pb, vecl], [ag], bias=self.V(l, "a0", c))
        pb = self.bank()
        self.MM(pb[:, 0:T], wsm[:, og + 128 * c:og + 128 * c + 128], misc[:, 2, 0:T], True, True, [wsm, misc], [pb])
        self.evac(gT[:, c, 0:T], pb[:, 0:T], [pb], [gT])
    self.ACT(lw[:, :, 0:T], lw[:, :, 0:T], AF.Identity, [lw], [lw], scale=-math.exp(-0.5))
    for c in range(2):
        P.op("dve", lambda e, c=c: e.tensor_tensor_scan(G[:, c, 0:T], self.rstm[:, 0:T], lw[:, c, 0:T], 0.0, ALU.mult, ALU.add),
             reads=[self.rstm, lw], writes=[G])
    self.TT("dve", lw[:, :, 0:T], G[:, :, 0:T], lw[:, :, 0:T], ALU.subtract, [G, lw], [lw])
    for c in range(2):
        self.TS("dve", kkr[:, c, 0:T], xm[:, 2 + c, 0:T], self.V(l, "kk", c), None, ALU.mult, None, [xm, vecl], [kkr])
    self.ACT(sqk[:, :, 0:T], kkr[:, :, 0:T], AF.Square, [kkr], [sqk])
    for c in range(2):
        pb = self.bank()
        self.MM(pb[:, 0:T], self.cb("blk1", n=128), sqk[:, c, 0:T], True, True, [self.cstb, sqk], [pb])
        self.ACT(rn[:, 0:T], pb[:, 0:T], AF.Sqrt, [pb, self.epsc], [rn], bias=self.epsc[:, 0:1])
        P.op("dve", lambda e: e.reciprocal(rn[:, 0:T], rn[:, 0:T]), reads=[rn], writes=[rn])
        self.TT("dve", kkr[:, c, 0:T], kkr[:, c, 0:T], rn[:, 0:T], ALU.mult, [kkr, rn], [kkr])
        self.TS("dve", kmod[:, c, 0:T], ag[:, c, 0:T], self.V(l, "ka", c), self.dv[l][:, 2 + c:3 + c], ALU.mult, ALU.add,
                [ag, vecl, self.dv[l]], [kmod])
        self.TT("dve", kmod[:, c, 0:T], kmod[:, c, 0:T], xm[:, 2 + c, 0:T], ALU.mult, [kmod, xm], [kmod])
    self.TT("dve", bvec[:, :, 0:T], kkr[:, :, 0:T], ag[:, :, 0:T], ALU.mult, [kkr, ag], [bvec])
    for c in range(2):
        self.STT(sqk[:, c, 0:T], xm[:, c, 0:T], self.V(l, "rk", c), kmod[:, c, 0:T], ALU.mult, ALU.mult, [xm, vecl, kmod], [sqk])
        pb = self.bank()
        self.MM(pb[:, 0:T], self.cb("blk1", n=128), sqk[:, c, 0:T], True, True, [self.cstb, sqk], [pb])
        self.TT("dve", bon[:, c, 0:T], pb[:, 0:T], xm[:, 4 + c, 0:T], ALU.mult, [pb, xm], [bon])
    self.ACT(v_b[:, :, 0:T], xm[:, 4:6, 0:T], AF.Identity, [xm], [v_b])
    self.ACT(ex[:, :, 0:T], lw[:, :, 0:T], AF.Exp, [lw], [ex])
    self.STT(At[:, :, 0:T], kkr[:, :, 0:T], -1.0, ex[:, :, 0:T], ALU.mult, ALU.mult, [kkr, ex], [At])
    self.ACT(ex[:, :, 0:T], G[:, :, 0:T], AF.Exp, [G], [ex], scale=-1.0)
    for e_ in range(2):
        hs = slice(64 * e_, 64 * e_ + 64)
        self.TT("dve", Pt[hs, :, e_, 0:T], bvec[hs, :, 0:T], ex[hs, :, 0:T], ALU.mult, [bvec, ex], [Pt])
        self.TT("dve", Kt[hs, :, e_, 0:T], kmod[hs, :, 0:T], ex[hs, :, 0:T], ALU.mult, [kmod, ex], [Kt])
    self.ACT(ex[:, :, 0:T], G[:, :, 0:T], AF.Exp, [G], [ex])
    self.TT("dve", Rt[:, :, 0:T], xm[:, 0:2, 0:T], ex[:, :, 0:T], ALU.mult, [xm, ex], [Rt])
    exv = ex.ap()[:, :, 0:T].rearrange("p c (b l) -> p c b l", l=L)
    P.op("dve", lambda e: e.tensor_copy(WL[:, :, 0:NB], exv[:, :, :, L - 1]), reads=[ex], writes=[WL])
    Gv = G.ap()[:, :, 0:T].rearrange("p c (b l) -> p c b l", l=L)
    for c in range(2):
        self.TT("dve", ex[:, c, 0:T].rearrange("p (b l) -> p b l", l=L),
                Gv[:, c, :, L - 1:L].to_broadcast([128, NB, L]), Gv[:, c, :, :], ALU.subtract, [G], [ex])
    self.ACT(ex[:, :, 0:T], ex[:, :, 0:T], AF.Exp, [ex], [ex])
    self.TT("dve", Pe[:, :, 0:T], bvec[:, :, 0:T], ex[:, :, 0:T], ALU.mult, [bvec, ex], [Pe])
    self.TT("dve", Ke[:, :, 0:T], kmod[:, :, 0:T], ex[:, :, 0:T], ALU.mult, [kmod, ex], [Ke])
    self.sb_copy(Sb, S32)
    mk = lambda name: self.cb(name, slice(0, L), L).unsqueeze(1).to_broadcast([L, 4, L])
    for bi, (b0, _) in enumerate(blks):
        bs = slice(b0, b0 + L)
        self.pump()
        self.to_tm(v_b, T, b0, L, V_tm[0:L, :, :], V_tm)
        self.to_tm(Pe, T, b0, L, Pe_tm[0:L, :, :], Pe_tm)
        self.to_tm(Ke, T, b0, L, Ke_tm[0:L, :, :], Ke_tm)
        prods = [(At, Pt, "m1_str", NM0[0:L, 0, :, 0:L], NM0), (Pt, At, "m1_strT", NM0[0:L, 1, :, 0:L], NM0),
                 (Kt, At, "m1_strT", AakT[0:L, :, 0:L], AakT), (Pt, Rt, "m1_inclT", ArpT[0:L, :, 0:L], ArpT),
                 (Kt, Rt, "m1_inclT", ArkT[0:L, :, 0:L], ArkT)]
        for (lh, rh, mname, oap, obuf) in prods:
            pr = self.bankb()
            for h, (c, hp) in enumerate(HP):
                la = lh[:, c, hp, bs] if (lh is Pt or lh is Kt) else lh[:, c, bs]
                ra = rh[:, c, hp, bs] if (rh is Pt or rh is Kt) else rh[:, c, bs]
                self.MM(pr[0:L, h * L:(h + 1) * L], la, ra, True, True, [lh, rh], [pr])
            self.TT("dve", oap, pr[0:L, 0:4 * L].rearrange("p (h l) -> p h l", h=4), mk(mname), ALU.mult,
                    [pr, self.cstb], [obuf])
        PT = self.tri_inverse(NM0, L)
        pq = self.bankb()
        for h, (c, hp) in enumerate(HP):
            self.MM(pq[0:L, h * 64:(h + 1) * 64], At[:, c, bs], Sb[:, c, hp, :], True, False, [At, Sb], [pq])
            self.MM(pq[0:L, h * 64:(h + 1) * 64], AakT[0:L, h, 0:L], V_tm[0:L, h, :], False, True, [AakT, V_tm], [pq])
        self.evac(rhs_s[0:L, :, :], pq[0:L, 0:256].rearrange("p (h d) -> p h d", h=4), [pq], [rhs_s])
        pv = self.bankb()
        for h in range(4):
            self.MM(pv[0:L, h * 64:(h + 1) * 64], PT[0:L, h, 0:L], rhs_s[0:L, h, :], True, True, [PT, rhs_s], [pv])
        self.evac(U_s[0:L, :, :], pv[0:L, 0:256].rearrange("p (h d) -> p h d", h=4), [pv], [U_s])
        po = self.bankb()
        for h, (c, hp) in enumerate(HP):
            self.MM(po[0:L, h * 64:(h + 1) * 64], Rt[:, c, bs], Sb[:, c, hp, :], True, False, [Rt, Sb], [po])
            self.MM(po[0:L, h * 64:(h + 1) * 64], ArpT[0:L, h, 0:L], U_s[0:L, h, :], False, False, [ArpT, U_s], [po])
            self.MM(po[0:L, h * 64:(h + 1) * 64], ArkT[0:L, h, 0:L], V_tm[0:L, h, :], False, True, [ArkT, V_tm], [po])
        self.evac(otm[0:L, bi, :, :], po[0:L, 0:256].rearrange("p (h d) -> p h d", h=4), [po], [otm])
        pS = self.bankb()
        for h, (c, hp) in enumerate(HP):
            hs = slice(64 * hp, 64 * hp + 64)
            self.MM(pS[hs, c * 64:(c + 1) * 64], Pe_tm[0:L, h, :], U_s[0:L, h, :], True, False, [Pe_tm, U_s], [pS])
            self.MM(pS[hs, c * 64:(c + 1) * 64], Ke_tm[0:L, h, :], V_tm[0:L, h, :], False, True, [Ke_tm, V_tm], [pS])
        for c in range(2):
            self.STT(S32[:, c, :], S32[:, c, :], WL[:, c, bi:bi + 1], pS[:, c * 64:(c + 1) * 64], ALU.mult, ALU.add,
                     [S32, WL, pS], [S32])
        self.sb_copy(Sb, S32)
    NH = NB * 4
    o3 = otm[0:L, 0:NB, :, :].rearrange("p b h d -> p (b h) d")
    P.op("dve", lambda e: e.reduce_sum(s1[0:L, 0:NH], o3, AX.X), reads=[otm], writes=[s1])
    self.TS("dve", s1[0:L, 0:NH], s1[0:L, 0:NH], 1.0 / 64.0, None, ALU.mult, None, [s1], [s1])
    self.TT("dve", o3, o3, s1[0:L, 0:NH].unsqueeze(2).to_broadcast([L, NH, 64]), ALU.subtract, [otm, s1], [otm])
    q3 = sqo[0:L, 0:NB, :, :].rearrange("p b h d -> p (b h) d")
    self.TT("dve", q3, o3, o3, ALU.mult, [otm], [sqo_buf])
    P.op("dve", lambda e: e.reduce_sum(ss[0:L, 0:NH], q3, AX.X), reads=[sqo_buf], writes=[ss])
    self.ACT(ss[0:L, 0:NH], ss[0:L, 0:NH], AF.Sqrt, [ss, self.epsc], [ss], bias=self.epsc[0:L, 1:2], scale=1.0 / 64.0)
    P.op("dve", lambda e: e.reciprocal(ss[0:L, 0:NH], ss[0:L, 0:NH]), reads=[ss], writes=[ss])
    self.TT("dve", on_b[0:L, 0:NB, :, :].rearrange("p b h d -> p (b h) d"), o3,
            ss[0:L, 0:NH].unsqueeze(2).to_broadcast([L, NH, 64]), ALU.mult, [otm, ss], [on_b])
    pys = [self.bankb(), self.bankb()]
    for bi, (b0, _) in enumerate(blks):
        for c in range(2):
            self.MM(pys[c][:, b0:b0 + L], on_b[0:L, bi, 2 * c:2 * c + 2, :].rearrange("p h d -> p (h d)"),
                    self.cb("ident", slice(0, L), L), True, True, [on_b, self.cstb], [pys[c]])
    for c in range(2):
        self.TS("dve", lw[:, c, 0:T], pys[c][:, 0:T], self.V(l, "gnw", c), self.V(l, "gnb", c), ALU.mult, ALU.add,
                [pys[c], vecl], [lw])
        self.TT("dve", lw[:, c, 0:T], lw[:, c, 0:T], bon[:, c, 0:T], ALU.add, [lw, bon], [lw])
        self.TT("dve", self.mixT[:, 2 + c, 0:T], lw[:, c, 0:T], gT[:, c, 0:T], ALU.mult, [lw, gT], [self.mixT])


KB.rwkv = _rwkv
```
